# Optimizing a Trainium2 kernel written in Bass

```python
import math
import jax, jax.numpy as jnp
from jax import lax
import numpy as np

D_MODEL = 1024
BATCH = 8
SEQ = 4096
DEPTH = 4

CHUNK = 64
D_MIX = D_MODEL
POOL_W = D_MIX // 2
POOL_WINDOWS = (2, 4, 8, 16)
N_POOL = len(POOL_WINDOWS)
POOL_GROUP = POOL_W // N_POOL
ATT_W = D_MIX - POOL_W
N_HEADS = 8
HEAD_DIM = ATT_W // N_HEADS
LEFT_CHUNKS = 8
BAND = (LEFT_CHUNKS + 1) * CHUNK
MAX_REL = 128
N_REL = 2 * MAX_REL + 1
D_FF = 2816
CONV_K = 3
PLE_DIM = 256
IN_COLS = POOL_W + 3 * ATT_W
EPS = 1e-6

kernel_name = "hybrid_pool_chunkattn_convffn_ple"


def rmsnorm(x, g):
    xf = x.astype(jnp.float32)
    y = xf * lax.rsqrt(jnp.mean(xf * xf, axis=-1, keepdims=True) + EPS)
    return (y * g.astype(jnp.float32)).astype(x.dtype)


def pool_mixer(u, w, b, scale):
    B, S, _ = u.shape
    ug = u.reshape(B, S, N_POOL, POOL_GROUP).astype(jnp.float32)
    cs = jnp.cumsum(ug, axis=1)
    cs = jnp.concatenate([jnp.zeros_like(cs[:, :1]), cs], axis=1)
    t = jnp.arange(S)
    pooled = []
    for gi, win in enumerate(POOL_WINDOWS):
        start = jnp.maximum(t + 1 - win, 0)
        cnt = (t + 1 - start).astype(jnp.float32)
        s = cs[:, t + 1, gi] - cs[:, start, gi]
        pooled.append(s / cnt[None, :, None])
    y = (jnp.stack(pooled, axis=2) - ug).astype(u.dtype)
    y = jnp.einsum('bsgc,gcd->bsgd', y, w) + b.reshape(N_POOL, POOL_GROUP)
    return y.reshape(B, S, POOL_W) * scale


def _rel_index():
    i = np.arange(CHUNK)[:, None]
    j = np.arange(BAND)[None, :]
    dist = i + LEFT_CHUNKS * CHUNK - j
    return (np.clip(dist, -MAX_REL, MAX_REL) + MAX_REL).astype(np.int32)


def chunk_attention(q, k, v, rel_bias):
    B, S, H, Dh = q.shape
    nc = S // CHUNK
    pad = LEFT_CHUNKS * CHUNK
    kp = jnp.pad(k, ((0, 0), (pad, 0), (0, 0), (0, 0)))
    vp = jnp.pad(v, ((0, 0), (pad, 0), (0, 0), (0, 0)))
    bias = rel_bias.astype(jnp.float32)[:, _rel_index()]
    qc = q.reshape(B, nc, CHUNK, H, Dh).transpose(1, 0, 2, 3, 4)
    scale = 1.0 / math.sqrt(Dh)
    band_pos = jnp.arange(BAND)

    def one_chunk(args):
        c, qb = args
        kb = lax.dynamic_slice_in_dim(kp, c * CHUNK, BAND, axis=1)
        vb = lax.dynamic_slice_in_dim(vp, c * CHUNK, BAND, axis=1)
        s = jnp.einsum('bqhd,bkhd->bhqk', qb, kb).astype(jnp.float32) * scale + bias
        valid = band_pos + c * CHUNK >= pad
        s = jnp.where(valid[None, None, None, :], s, -jnp.inf)
        pr = jax.nn.softmax(s, axis=-1).astype(vb.dtype)
        return jnp.einsum('bhqk,bkhd->bqhd', pr, vb)

    out = lax.map(one_chunk, (jnp.arange(nc), qc))
    return out.transpose(1, 0, 2, 3, 4).reshape(B, S, H * Dh)


def causal_dwconv(h, w, b):
    S = h.shape[1]
    hp = jnp.pad(h, ((0, 0), (CONV_K - 1, 0), (0, 0)))
    y = hp[:, 0:S] * w[0]
    for kk in range(1, CONV_K):
        y = y + hp[:, kk:kk + S] * w[kk]
    return y + b


def setup_inputs(seed: int = 0) -> dict:
    key = jax.random.key(seed)
    ks = jax.random.split(key, 20)
    f32 = jnp.float32
    nrm = lambda k, shape, s: (jax.random.normal(k, shape, f32) * s).astype(f32)
    return {
        "x": nrm(ks[0], (BATCH, SEQ, D_MODEL), 1.0),
        "p": nrm(ks[1], (DEPTH, BATCH, SEQ, PLE_DIM), 1.0),
        "norm_mix_g": 1.0 + nrm(ks[2], (DEPTH, D_MODEL), 0.02),
        "w_in": nrm(ks[3], (DEPTH, D_MODEL, IN_COLS), D_MODEL ** -0.5),
        "pool_w": nrm(ks[4], (DEPTH, N_POOL, POOL_GROUP, POOL_GROUP), POOL_GROUP ** -0.5),
        "pool_b": nrm(ks[5], (DEPTH, POOL_W), 0.01),
        "pool_scale": 1.0 + nrm(ks[6], (DEPTH, POOL_W), 0.1),
        "rel_bias": nrm(ks[7], (DEPTH, N_HEADS, N_REL), 0.5),
        "w_out": nrm(ks[8], (DEPTH, D_MIX, D_MODEL), 0.5 * D_MIX ** -0.5),
        "norm_ffn_g": 1.0 + nrm(ks[9], (DEPTH, D_MODEL), 0.02),
        "w_up": nrm(ks[10], (DEPTH, D_MODEL, 2 * D_FF), D_MODEL ** -0.5),
        "conv_w": nrm(ks[11], (DEPTH, CONV_K, D_FF), CONV_K ** -0.5),
        "conv_b": nrm(ks[12], (DEPTH, D_FF), 0.01),
        "w_down": nrm(ks[13], (DEPTH, D_FF, D_MODEL), 0.5 * D_FF ** -0.5),
        "norm_ple_g": 1.0 + nrm(ks[14], (DEPTH, D_MODEL), 0.02),
        "w_ple_gate": nrm(ks[15], (DEPTH, D_MODEL, D_MODEL), D_MODEL ** -0.5),
        "b_ple_gate": nrm(ks[16], (DEPTH, D_MODEL), 0.01),
        "w_ple": nrm(ks[17], (DEPTH, PLE_DIM, D_MODEL), 0.5 * PLE_DIM ** -0.5),
        "final_g": 1.0 + nrm(ks[18], (D_MODEL,), 0.02),
    }


def reference(x, p, norm_mix_g, w_in, pool_w, pool_b, pool_scale, rel_bias, w_out,
              norm_ffn_g, w_up, conv_w, conv_b, w_down, norm_ple_g, w_ple_gate,
              b_ple_gate, w_ple, final_g):
    B, S, _ = x.shape
    h = x
    for i in range(DEPTH):
        hn = rmsnorm(h, norm_mix_g[i])
        z = hn @ w_in[i]
        u = z[..., :POOL_W]
        q, k, v = jnp.split(z[..., POOL_W:], 3, axis=-1)
        q = q.reshape(B, S, N_HEADS, HEAD_DIM)
        k = k.reshape(B, S, N_HEADS, HEAD_DIM)
        v = v.reshape(B, S, N_HEADS, HEAD_DIM)
        a = pool_mixer(u, pool_w[i], pool_b[i], pool_scale[i])
        o = chunk_attention(q, k, v, rel_bias[i])
        h = h + jnp.concatenate([a, o], axis=-1) @ w_out[i]
        hn = rmsnorm(h, norm_ffn_g[i])
        up = hn @ w_up[i]
        gt, val = up[..., :D_FF], up[..., D_FF:]
        gt = causal_dwconv(gt, conv_w[i], conv_b[i])
        h = h + (jax.nn.gelu(gt, approximate=False) * val) @ w_down[i]
        hn = rmsnorm(h, norm_ple_g[i])
        gate = jax.nn.sigmoid(hn @ w_ple_gate[i] + b_ple_gate[i])
        h = h + gate * (p[i] @ w_ple[i])
    return rmsnorm(h, final_g)
```

```python
import numpy as np
import concourse.bass as bass
import concourse.mybir as mybir
from concourse.bass_utils import run_bass_kernel_spmd

F32 = mybir.dt.float32
BF16 = mybir.dt.bfloat16
U8 = mybir.dt.uint8
ALU = mybir.AluOpType
AF = mybir.ActivationFunctionType

D = 1024
NCH = 8
T = 512
DFF = 2816
NF = 22
PLE = 256
NEG = -30000.0
EPS = 1e-6
SEQ = 4096
DEPTH = 4
NDMA_RING = 8


def _esz(dt):
    if dt == F32:
        return 4
    if dt == BF16:
        return 2
    if dt == U8:
        return 1
    raise ValueError(dt)


class Op:
    __slots__ = ("eng", "emit", "deps", "signal", "epoch", "value", "is_dma", "dsem", "dval", "ring_wait")

    def __init__(self, eng, emit, epoch):
        self.eng = eng
        self.emit = emit
        self.deps = set()
        self.signal = False
        self.epoch = epoch
        self.value = None
        self.is_dma = False
        self.dsem = None
        self.dval = None
        self.ring_wait = None


class Prog:
    ENGS = ("pe", "act", "dve", "pool", "sp")
    PAGE = 4096

    def __init__(self, nc):
        self.nc = nc
        self.ops = {e: [] for e in self.ENGS}
        self.recs = {}
        self.epoch = 0
        self.dma_n = {e: 0 for e in self.ENGS}
        self.dma_last = {}

    @staticmethod
    def rng(ap):
        sp = str(ap.space)
        pat = ap.ap
        esz = _esz(ap.dtype)
        off = int(ap.offset)
        if "DRAM" in sp:
            lo = off
            hi = off
            for (s, c) in pat:
                ext = (c - 1) * s
                if ext < 0:
                    lo += ext
                else:
                    hi += ext
            return ("D:" + ap.tensor.name, lo * esz, (hi + 1) * esz)
        pstride = pat[0][0]
        fo = off % pstride if pstride > 0 else off
        lo = fo
        hi = fo
        for (s, c) in pat[1:]:
            ext = (c - 1) * s
            if ext < 0:
                lo += ext
            else:
                hi += ext
        name = "SB" if "SB" in sp else ("P:" + ap.tensor.name)
        return (name, lo * esz, (hi + 1) * esz)

    def _pages(self, lo, hi):
        return range(lo // self.PAGE, (hi - 1) // self.PAGE + 1)

    def _access(self, op, ap, kind):
        space, lo, hi = self.rng(ap)
        pages = self.recs.setdefault(space, {})
        for pg in self._pages(lo, hi):
            lst = pages.get(pg)
            if not lst:
                continue
            for rec in lst:
                if rec[0] < hi and lo < rec[1]:
                    if kind == "W" or rec[2] == "W":
                        if rec[3] is not op:
                            op.deps.add(rec[3])
        return space, lo, hi

    def _record(self, op, space, lo, hi, kind):
        pages = self.recs[space]
        for pg in self._pages(lo, hi):
            lst = pages.setdefault(pg, [])
            plo = max(lo, pg * self.PAGE)
            phi = min(hi, (pg + 1) * self.PAGE)
            if kind == "W":
                lst[:] = [r for r in lst if not (plo <= max(r[0], pg * self.PAGE) and min(r[1], (pg + 1) * self.PAGE) <= phi)]
                lst.append([lo, hi, "W", op])
            else:
                done = False
                if not op.is_dma:
                    for r in lst:
                        if r[2] == "R" and r[0] == lo and r[1] == hi and r[3].eng == op.eng and not r[3].is_dma:
                            r[3] = op
                            done = True
                            break
                if not done:
                    lst.append([lo, hi, "R", op])

    def op(self, eng, emit, reads=(), writes=()):
        o = Op(eng, emit, self.epoch)
        acc = []
        for ap in reads:
            if ap is None or isinstance(ap, (int, float)):
                continue
            acc.append(self._access(o, ap, "R") + ("R",))
        for ap in writes:
            acc.append(self._access(o, ap, "W") + ("W",))
        for (space, lo, hi, kind) in acc:
            self._record(o, space, lo, hi, kind)
        self.ops[eng].append(o)
        return o

    def dma(self, eng, out, in_, **kw):
        def emit(e):
            return e.dma_start(out=out, in_=in_, **kw)
        o = Op(eng, emit, self.epoch)
        o.is_dma = True
        n = self.dma_n[eng]
        self.dma_n[eng] = n + 1
        ring = n % NDMA_RING
        o.dsem = (eng, ring)
        o.dval = 16 * (n // NDMA_RING + 1)
        prev = self.dma_last.get((eng, ring))
        o.ring_wait = prev
        self.dma_last[(eng, ring)] = o
        a1 = self._access(o, in_, "R") + ("R",)
        a2 = self._access(o, out, "W") + ("W",)
        self._record(o, *a1)
        self._record(o, *a2)
        self.ops[eng].append(o)
        return o

    def new_epoch(self):
        self.epoch += 1

    def finalize_and_emit(self, final_wait_ops):
        nc = self.nc
        for e in self.ENGS:
            for o in self.ops[e]:
                drop = []
                for d in o.deps:
                    if d.is_dma:
                        continue
                    if d.eng == o.eng and not o.is_dma:
                        if o.eng in ("pe", "sp"):
                            drop.append(d)
                            continue
                    d.signal = True
                for d in drop:
                    o.deps.discard(d)
        for o in final_wait_ops:
            if not o.is_dma:
                o.signal = True
        nep = self.epoch + 1
        for e in self.ENGS:
            cnt = [0] * nep
            for o in self.ops[e]:
                if o.is_dma:
                    continue
                if o.signal:
                    cnt[o.epoch] += 1
                    o.value = cnt[o.epoch]
        import contextlib
        with contextlib.ExitStack() as st:
            esem = {}
            for e in ("pe", "act", "dve", "pool"):
                for ep in range(nep):
                    esem[(e, ep)] = st.enter_context(nc.semaphore(f"s_{e}_{ep}"))
            dsem = {}
            for e in ("sp", "pool", "act"):
                if self.dma_n[e] == 0:
                    continue
                for r in range(NDMA_RING):
                    dsem[(e, r)] = st.enter_context(nc.semaphore(f"d_{e}_{r}"))
            block = st.enter_context(nc.Block())

            def run(eng_name, eng):
                waited = {}

                def wait(sem, val):
                    if waited.get(id(sem), 0) >= val:
                        return
                    waited[id(sem)] = val
                    eng.wait_ge(sem, val)

                for o in self.ops[eng_name]:
                    if o.is_dma and o.ring_wait is not None:
                        wait(dsem[o.ring_wait.dsem], o.ring_wait.dval)
                    for d in o.deps:
                        if d.is_dma:
                            wait(dsem[d.dsem], d.dval)
                        else:
                            wait(esem[(d.eng, d.epoch)], d.value)
                    ins = o.emit(eng)
                    if o.is_dma:
                        ins.then_inc(dsem[o.dsem], 16)
                    elif o.signal:
                        ins.then_inc(esem[(o.eng, o.epoch)], 1)
                if eng_name == "sp":
                    for o in final_wait_ops:
                        if o.is_dma:
                            wait(dsem[o.dsem], o.dval)
                        else:
                            wait(esem[(o.eng, o.epoch)], o.value)

            @block.tensor
            def _(eng):
                run("pe", eng)

            @block.scalar
            def _(eng):
                run("act", eng)

            @block.vector
            def _(eng):
                run("dve", eng)

            @block.gpsimd
            def _(eng):
                run("pool", eng)

            @block.sync
            def _(eng):
                run("sp", eng)

    def mm(self, out, lhsT, rhs, start, stop, **kw):
        return self.op("pe", lambda e: e.matmul(out, lhsT=lhsT, rhs=rhs, start=start, stop=stop, **kw),
                       reads=(lhsT, rhs), writes=(out,))

    def tr(self, out, in_, ident):
        return self.op("pe", lambda e: e.transpose(out, in_, ident), reads=(in_, ident), writes=(out,))

    def act(self, out, in_, func, bias=None, scale=None):
        kw = {}
        if bias is not None:
            kw["bias"] = bias
        if scale is not None:
            kw["scale"] = scale
        return self.op("act", lambda e: e.activation(out=out, in_=in_, func=func, **kw),
                       reads=(in_, bias, scale), writes=(out,))

    def tt(self, eng, out, in0, in1, op):
        return self.op(eng, lambda e: e.tensor_tensor(out=out, in0=in0, in1=in1, op=op),
                       reads=(in0, in1), writes=(out,))

    def ts(self, eng, out, in0, s1, s2, op0, op1=None):
        if op1 is None:
            return self.op(eng, lambda e: e.tensor_single_scalar(out=out, in_=in0, scalar=s1, op=op0),
                           reads=(in0, s1), writes=(out,))
        return self.op(eng, lambda e: e.tensor_scalar(out=out, in0=in0, scalar1=s1, scalar2=s2, op0=op0, op1=op1),
                       reads=(in0, s1, s2), writes=(out,))

    def stt(self, out, in0, scalar, in1, op0, op1):
        return self.op("dve", lambda e: e.scalar_tensor_tensor(out=out, in0=in0, scalar=scalar, in1=in1, op0=op0, op1=op1),
                       reads=(in0, scalar, in1), writes=(out,))

    def copy(self, eng, out, in_):
        if eng == "act":
            return self.act(out, in_, AF.Copy)
        return self.op(eng, lambda e: e.tensor_copy(out=out, in_=in_), reads=(in_,), writes=(out,))

    def memset(self, eng, out, val):
        return self.op(eng, lambda e: e.memset(out, val), writes=(out,))

    def recip(self, out, in_):
        return self.op("dve", lambda e: e.reciprocal(out=out, in_=in_), reads=(in_,), writes=(out,))


class Arena:
    def __init__(self, nc, nbytes):
        self.t = nc.alloc_sbuf_tensor("arena", [128, nbytes], U8)
        self.nbytes = nbytes

    def view(self, off, shape, dt):
        n = 1
        for s in shape[1:]:
            n *= s
        nb = n * _esz(dt)
        assert off % 4 == 0 and off + nb <= self.nbytes, (off, nb, self.nbytes)
        v = self.t[:, off:off + nb].bitcast(dt)
        if len(shape) == 2:
            return v
        if len(shape) == 3:
            return v.rearrange("p (a b) -> p a b", a=shape[1])
        if len(shape) == 4:
            return v.rearrange("p (a b c) -> p a b c", a=shape[1], b=shape[2])
        if len(shape) == 5:
            return v.rearrange("p (a b c d) -> p a b c d", a=shape[1], b=shape[2], c=shape[3])
        raise ValueError(shape)


def build(S=SEQ, NL=DEPTH):
    NB = S // T
    nc = bass.Bass("TRN2", target_bir_lowering=False)
    P = Prog(nc)

    def din(name, shape):
        return nc.dram_tensor(name, list(shape), F32, kind="ExternalInput").ap()

    x_d = din("x", [S, D])
    p_d = din("p", [NL, S, PLE])
    gmix_d = din("norm_mix_g", [NL, D])
    w_in_d = din("w_in", [NL, D, 2048])
    pool_w_d = din("pool_w", [NL, 4, 128, 128])
    pool_b_d = din("pool_b", [NL, 512])
    pool_s_d = din("pool_scale", [NL, 512])
    relb_d = din("rel_bias", [NL * 8, 257])
    w_out_d = din("w_out", [NL, D, D])
    gffn_d = din("norm_ffn_g", [NL, D])
    w_up_d = din("w_up", [NL, D, 2 * DFF])
    conv_w_d = din("conv_w", [NL, 3, DFF])
    conv_b_d = din("conv_b", [NL, DFF])
    w_down_d = din("w_down", [NL, DFF, D])
    gple_d = din("norm_ple_g", [NL, D])
    w_gate_d = din("w_ple_gate", [NL, D, D])
    b_gate_d = din("b_ple_gate", [NL, D])
    w_ple_d = din("w_ple", [NL, PLE, D])
    gfin_d = din("final_g", [1, D])
    consts_d = din("consts", [128, 200])
    out_d = nc.dram_tensor("out", [S, D], F32, kind="ExternalOutput").ap()
    hT_d = nc.dram_tensor("hT_scr", [NCH, 128, S], F32, kind="Internal").ap()
    E_t = nc.dram_tensor("E_scr", [NL * 8, 128, 768], F32, kind="Internal")
    E_d = E_t.ap()

    off = 0

    def take(n):
        nonlocal off
        o = off
        off += (n + 31) // 32 * 32
        return o

    o_hblk = take(NCH * T * 4)
    o_hb = take(NCH * T * 2)
    o_rstd = take(T * 4)
    o_consts = take(200 * 4)
    o_ones = take(128 * 2)
    o_gains = take((3 * NL + 1) * 8 * 4)
    o_poolb = take(NL * 4 * 4)
    o_pools = take(NL * 4 * 4)
    o_convw = take(NL * 3 * NF * 4)
    o_convb = take(NL * NF * 4)
    o_bgate = take(NL * 8 * 4)
    o_ghalo = take(NF * 2 * 4)
    o_zero = take(T * 4)
    o_W = take(135168)
    o_X = take(36 * 1024)
    total = off
    assert total <= 206 * 1024, total
    A = Arena(nc, total)

    hblk = A.view(o_hblk, [128, NCH, T], F32)
    hb = A.view(o_hb, [128, NCH, T], BF16)
    rstd = A.view(o_rstd, [128, T], F32)
    consts = A.view(o_consts, [128, 200], F32)
    eps_col = consts[:, 192:193]
    ident = consts[:, 0:128]
    invcnt = consts[:, 128:192].rearrange("p (g t) -> p g t", g=4)
    ones_bf = A.view(o_ones, [128, 128], BF16)
    gains = A.view(o_gains, [128, 3 * NL + 1, 8], F32)
    poolb = A.view(o_poolb, [128, NL * 4], F32)
    pools = A.view(o_pools, [128, NL * 4], F32)
    convw = A.view(o_convw, [128, NL * 3, NF], F32)
    convb = A.view(o_convb, [128, NL, NF], F32)
    bgate = A.view(o_bgate, [128, NL, 8], F32)
    ghalo = A.view(o_ghalo, [128, NF, 2], F32)
    zero = A.view(o_zero, [128, T], F32)

    w_up = A.view(o_W, [128, NCH, 2 * DFF], BF16)
    w_down = A.view(o_W + 90112, [128, NF, D], BF16)
    w_in = A.view(o_W, [128, NCH, 2048], BF16)
    w_out = A.view(o_W + 32768, [128, NCH, D], BF16)
    w_gate = A.view(o_W + 49152, [128, NCH, D], BF16)
    w_ple = A.view(o_W + 65536, [128, 2, D], BF16)
    pool_w = A.view(o_W + 69632, [128, 4, 128], BF16)
    oM = o_W + 70656
    Mb = A.view(oM, [128, 8, 640], F32)
    vaug = A.view(oM + 20480, [128, 8, 4, 2, 128], BF16)
    kT = A.view(oM + 36864, [128, 4, 2 * T], BF16)
    qT = A.view(oM + 45056, [128, 4, T], BF16)
    cat = A.view(oM + 49152, [128, NCH, T], BF16)
    pstage = A.view(oM + 57344, [128, 4, PLE], F32)
    pT = A.view(oM + 61440, [128, 2, T], BF16)
    iostage = A.view(oM, [128, 4, D], F32)
    Esb = A.view(oM + 20480, [32, 768], F32)
    actb = A.view(o_X, [128, NF, T], BF16)
    gbuf = [A.view(o_X + 22528 + i * 2080, [128, 520], F32) for i in range(2)]
    accb = [A.view(o_X + 26688 + i * 2048, [128, T], F32) for i in range(2)]
    glb = [A.view(o_X + 30784 + i * 2048, [128, T], F32) for i in range(2)]
    ubuf = A.view(o_X, [128, 4, 528], F32)
    ptmp = [A.view(o_X + 8448 + i * 2112, [128, 528], F32) for i in range(2)]
    ybf = [A.view(o_X + 12672 + i * 1024, [128, T], BF16) for i in range(2)]
    pTs = [A.view(o_X + 14720 + i * 1024, [128, T], BF16) for i in range(3)]
    scb = [A.view(o_X + 17792 + i * 2048, [128, T], F32) for i in range(2)]
    gateb = [A.view(o_X + 21888 + i * 2048, [128, T], F32) for i in range(2)]
    recb = [A.view(o_X + 25984 + i * 2048, [128, T], F32) for i in range(2)]
    tmp16 = A.view(o_X + 30080, [128, 16], F32)
    tmpb = [A.view(o_X + 30208 + i * 2048, [128, T], F32) for i in range(2)]

    psb = [nc.alloc_psum_tensor(f"ps{i}", [128, T], F32)[:, :] for i in range(8)]
    pools_ps = {"mm": [0, 1, 2], "sc": [3, 4], "o": [5, 6], "aux": [7]}
    ps_ctr = {k: 0 for k in pools_ps}

    def bank(pool):
        lst = pools_ps[pool]
        i = ps_ctr[pool]
        ps_ctr[pool] = i + 1
        return psb[lst[i % len(lst)]]

    ctr = {"i": 0}

    def rr(lst):
        ctr["i"] += 1
        return lst[ctr["i"] % len(lst)]

    P.dma("sp", consts, consts_d)
    P.memset("pool", ones_bf, 1.0 / 1024.0)
    P.memset("pool", zero, 0.0)
    for L in range(NL):
        P.dma("sp", gains[:, 3 * L + 0, :], gmix_d[L].rearrange("(c p) -> p c", p=128))
        P.dma("sp", gains[:, 3 * L + 1, :], gffn_d[L].rearrange("(c p) -> p c", p=128))
        P.dma("sp", gains[:, 3 * L + 2, :], gple_d[L].rearrange("(c p) -> p c", p=128))
        P.dma("sp", poolb[:, 4 * L:4 * L + 4], pool_b_d[L].rearrange("(c p) -> p c", p=128))
        P.dma("sp", pools[:, 4 * L:4 * L + 4], pool_s_d[L].rearrange("(c p) -> p c", p=128))
        for k in range(3):
            P.dma("sp", convw[:, 3 * L + k, :], conv_w_d[L, k].rearrange("(c p) -> p c", p=128))
        P.dma("sp", convb[:, L, :], conv_b_d[L].rearrange("(c p) -> p c", p=128))
        P.dma("sp", bgate[:, L, :], b_gate_d[L].rearrange("(c p) -> p c", p=128))
    P.dma("sp", gains[:, 3 * NL, :], gfin_d[0].rearrange("(c p) -> p c", p=128))
    P.dma("sp", Esb[0:NL * 8, 0:256], relb_d[:, 1:257])
    P.ts("dve", Esb[0:NL * 8, 256:768], zero[0:NL * 8, 0:512], Esb[0:NL * 8, 255:256], None, ALU.add)
    for L in range(NL):
        P.dma("sp", E_d[L * 8:(L + 1) * 8], Esb[L * 8:(L + 1) * 8, :].unsqueeze(1).broadcast_to([8, 128, 768]))

    def load_h(b):
        for c in range(NCH):
            P.dma("sp", hblk[:, c, :], hT_d[c, :, b * T:(b + 1) * T])

    def store_h(b):
        for c in range(NCH):
            P.dma("sp", hT_d[c, :, b * T:(b + 1) * T], hblk[:, c, :])

    def wview(wd, L):
        return wd[L].rearrange("(c p) n -> p c n", p=128)

    def load_w_AC(Lc, La):
        if Lc is not None:
            for n0 in (0, 512):
                P.dma("pool", w_gate[:, :, n0:n0 + 512], wview(w_gate_d, Lc)[:, :, n0:n0 + 512])
            P.dma("pool", w_ple[:, :, :], wview(w_ple_d, Lc))
        if La is not None:
            for n0 in range(0, 2048, 512):
                P.dma("pool", w_in[:, :, n0:n0 + 512], wview(w_in_d, La)[:, :, n0:n0 + 512])
            P.dma("pool", pool_w[:, :, :], pool_w_d[La].rearrange("g c d -> c g d"))
            for n0 in (0, 512):
                P.dma("pool", w_out[:, :, n0:n0 + 512], wview(w_out_d, La)[:, :, n0:n0 + 512])
            src = bass.AP(E_t, (La * 8) * 128 * 768 + 127, [[767, 128], [128 * 768, 8], [1, 640]])
            P.dma("sp", Mb[:, :, :], src)
            P.memset("pool", Mb[64:128, :, 0:64], NEG)
            P.memset("pool", Mb[0:64, :, 576:640], NEG)
            P.memset("pool", vaug[:, :, :, :, :], 1.0)

    def load_w_B(L):
        wv = wview(w_up_d, L)
        for fp in range(NF // 2):
            P.dma("pool", w_up[:, :, fp * 256:(fp + 1) * 256], wv[:, :, fp * 256:(fp + 1) * 256])
            P.dma("pool", w_up[:, :, DFF + fp * 256:DFF + (fp + 1) * 256], wv[:, :, DFF + fp * 256:DFF + (fp + 1) * 256])
        wd = w_down_d[L].rearrange("(f p) n -> p f n", p=128)
        for n0 in range(0, D, 256):
            P.dma("pool", w_down[:, :, n0:n0 + 256], wd[:, :, n0:n0 + 256])

    def norm(gidx):
        hbf = hb.rearrange("p c t -> p (c t)")
        P.act(hbf, hblk.rearrange("p c t -> p (c t)"), AF.Square)
        bk = bank("aux")
        for c in range(NCH):
            P.mm(bk, ones_bf, hb[:, c, :], c == 0, c == NCH - 1)
        P.act(rstd, bk, AF.Sqrt, bias=eps_col)
        P.recip(rstd, rstd)
        return gidx

    def norm_apply_bf(gidx):
        for c in range(NCH):
            P.stt(hb[:, c, :], hblk[:, c, :], gains[:, gidx, c:c + 1], rstd, ALU.mult, ALU.mult)

    def proj(wt, col0, rhs_chunks, nk):
        bk = bank("mm")
        for c in range(nk):
            P.mm(bk, wt[:, c, col0:col0 + 128], rhs_chunks[:, c, :], c == 0, c == nk - 1)
        return bk

    def stage_init(b):
        P.dma("sp", iostage, x_d[b * T:(b + 1) * T, :].rearrange("(tt p) f -> p tt f", p=128))
        for c in range(NCH):
            bk = bank("mm")
            for tt in range(4):
                P.tr(bk[:, tt * 128:(tt + 1) * 128], iostage[:, tt, c * 128:(c + 1) * 128], ident)
            P.copy("act" if c % 2 else "dve", hblk[:, c, :], bk)

    def stage_A(L, b):
        norm(3 * L)
        norm_apply_bf(3 * L)
        cur = b % 2
        prv = 1 - cur
        for g in range(4):
            bk = proj(w_in, g * 128, hb, NCH)
            if b == 0:
                P.memset("pool", ubuf[:, g, 0:16], 0.0)
            else:
                P.copy("pool", ubuf[:, g, 0:16], ubuf[:, g, 512:528])
            P.copy("act", ubuf[:, g, 16:528], bk)
        for j in range(4):
            bk = proj(w_in, 512 + j * 128, hb, NCH)
            P.act(qT[:, j, :], bk, AF.Copy, scale=0.125)
        for j in range(4):
            bk = proj(w_in, 1024 + j * 128, hb, NCH)
            P.copy("dve", kT[:, j, cur * T:(cur + 1) * T], bk)
        for tt in range(4):
            bk = bank("mm")
            for c in range(NCH):
                P.mm(bk, hb[:, c, tt * 128:(tt + 1) * 128], w_in[:, c, 1536:2048], c == 0, c == NCH - 1)
            bv = bk.rearrange("p (j e d) -> p j e d", j=4, e=2)
            tile = cur * 4 + tt
            P.copy("act", vaug[:, tile, :, 0, 0:64], bv[:, :, 0, :])
            P.copy("dve", vaug[:, tile, :, 1, 64:128], bv[:, :, 1, :])
        for g in range(4):
            src = ubuf[:, g, :]
            prev = src
            for k in range(1, g + 2):
                sh = 1 << (k - 1)
                lo = (1 << k) - 1
                dst = ptmp[k % 2]
                P.tt("pool", dst[:, lo:528], prev[:, lo:528], prev[:, lo - sh:528 - sh], ALU.add)
                prev = dst
            w = 1 << (g + 1)
            yb = ybf[g % 2]
            P.stt(yb, prev[:, 16:528], 1.0 / w, src[:, 16:528], ALU.mult, ALU.subtract)
            if b == 0:
                P.tt("dve", tmp16, prev[:, 16:32], invcnt[:, g, :], ALU.mult)
                P.tt("dve", yb[:, 0:16], tmp16, src[:, 16:32], ALU.subtract)
            bk = bank("mm")
            P.mm(bk, pool_w[:, g, :], yb, True, True)
            P.ts("dve", cat[:, g, :], bk, poolb[:, 4 * L + g:4 * L + g + 1], pools[:, 4 * L + g:4 * L + g + 1], ALU.add, ALU.mult)
        for h in range(8):
            j = h // 2
            e = h % 2
            po = 64 * e
            so = 64 - po
            ob = bank("o")
            rs = [r for r in range(8) if 4 * b - 4 + r >= 0]
            for r in rs:
                qlo = max(0, r - 4)
                qhi = min(3, r)
                nq = qhi - qlo + 1
                half = prv if r < 4 else cur
                kt = r % 4
                kc0 = half * T + kt * 128
                sbk = bank("sc")
                P.mm(sbk[:, 0:nq * 128], kT[po:po + 64, j, kc0:kc0 + 128], qT[po:po + 64, j, qlo * 128:(qhi + 1) * 128], True, True)
                d0 = qlo - r + 4
                sc = rr(scb)
                P.tt("dve", sc[:, 0:nq * 128], sbk[:, 0:nq * 128], Mb[:, h, d0 * 128:(d0 + nq) * 128], ALU.add)
                pt = rr(pTs)
                P.act(pt[:, 0:nq * 128], sc[:, 0:nq * 128], AF.Exp)
                vt = half * 4 + kt
                P.mm(ob[:, qlo * 128:(qhi + 1) * 128], vaug[:, vt, j, e, :], pt[:, 0:nq * 128], r == rs[0], r == rs[-1],
                     skip_group_check=True)
            rec = rr(recb)
            P.recip(rec[so:so + 64, :], ob[so:so + 64, :])
            P.tt("dve", cat[po:po + 64, 4 + j, :], ob[po:po + 64, :], rec[so:so + 64, :], ALU.mult)
        for m in range(NCH):
            bk = proj(w_out, m * 128, cat, NCH)
            P.tt("dve", hblk[:, m, :], bk, hblk[:, m, :], ALU.add)

    def stage_B(L, b):
        norm(3 * L + 1)
        norm_apply_bf(3 * L + 1)
        for f in range(NF):
            bg = bank("mm")
            for c in range(NCH):
                P.mm(bg, w_up[:, c, f * 128:(f + 1) * 128], hb[:, c, :], c == 0, c == NCH - 1)
            bv = bank("sc" if f % 2 else "o")
            for c in range(NCH):
                P.mm(bv, w_up[:, c, DFF + f * 128:DFF + (f + 1) * 128], hb[:, c, :], c == 0, c == NCH - 1)
            gb = gbuf[f % 2]
            if b == 0:
                P.memset("pool", gb[:, 0:2], 0.0)
            else:
                P.copy("pool", gb[:, 0:2], ghalo[:, f, :])
            P.copy("act", gb[:, 2:514], bg)
            P.copy("pool", ghalo[:, f, :], gb[:, 512:514])
            acc = accb[f % 2]
            P.ts("dve", acc, gb[:, 0:512], convw[:, 3 * L + 0, f:f + 1], convb[:, L, f:f + 1], ALU.mult, ALU.add)
            P.stt(acc, gb[:, 1:513], convw[:, 3 * L + 1, f:f + 1], acc, ALU.mult, ALU.add)
            P.stt(acc, gb[:, 2:514], convw[:, 3 * L + 2, f:f + 1], acc, ALU.mult, ALU.add)
            gl = glb[f % 2]
            P.act(gl, acc, AF.Gelu)
            P.tt("dve", actb[:, f, :], gl, bv, ALU.mult)
        for m in range(NCH):
            bk = bank("mm")
            for f in range(NF):
                P.mm(bk, w_down[:, f, m * 128:(m + 1) * 128], actb[:, f, :], f == 0, f == NF - 1)
            P.tt("dve", hblk[:, m, :], bk, hblk[:, m, :], ALU.add)

    def stage_C(L, b):
        P.dma("sp", pstage, p_d[L, b * T:(b + 1) * T, :].rearrange("(tt p) k -> p tt k", p=128))
        norm(3 * L + 2)
        norm_apply_bf(3 * L + 2)
        for kc in range(2):
            bk = bank("aux")
            for tt in range(4):
                P.tr(bk[:, tt * 128:(tt + 1) * 128], pstage[:, tt, kc * 128:(kc + 1) * 128], ident)
            P.copy("act", pT[:, kc, :], bk)
        for m in range(NCH):
            bg = proj(w_gate, m * 128, hb, NCH)
            gt = rr(gateb)
            P.act(gt, bg, AF.Sigmoid, bias=bgate[:, L, m:m + 1])
            bp = bank("sc")
            for kc in range(2):
                P.mm(bp, w_ple[:, kc, m * 128:(m + 1) * 128], pT[:, kc, :], kc == 0, kc == 1)
            tb = rr(tmpb)
            P.tt("dve", tb, gt, bp, ALU.mult)
            P.tt("pool", hblk[:, m, :], tb, hblk[:, m, :], ALU.add)

    def stage_final(b):
        norm(3 * NL)
        for c in range(NCH):
            P.stt(hblk[:, c, :], hblk[:, c, :], gains[:, 3 * NL, c:c + 1], rstd, ALU.mult, ALU.mult)
        for tt in range(4):
            for hf in range(2):
                bk = bank("mm")
                for i in range(4):
                    c = hf * 4 + i
                    P.tr(bk[:, i * 128:(i + 1) * 128], hblk[:, c, tt * 128:(tt + 1) * 128], ident)
                P.copy("act" if hf else "dve", iostage[:, tt, hf * 512:(hf + 1) * 512], bk)
        return P.dma("sp", out_d[b * T:(b + 1) * T, :].rearrange("(tt p) f -> p tt f", p=128), iostage)

    finals = []
    for b in range(NB):
        stage_init(b)
        store_h(b)
    for L in range(NL):
        P.new_epoch()
        load_w_AC(L - 1 if L > 0 else None, L)
        for b in range(NB):
            load_h(b)
            if L > 0:
                stage_C(L - 1, b)
            stage_A(L, b)
            store_h(b)
        P.new_epoch()
        load_w_B(L)
        for b in range(NB):
            load_h(b)
            stage_B(L, b)
            store_h(b)
    P.new_epoch()
    load_w_AC(NL - 1, None)
    for b in range(NB):
        load_h(b)
        stage_C(NL - 1, b)
        finals.append(stage_final(b))
    with nc.allow_non_contiguous_dma(reason="small one-time parameter vectors"):
        P.finalize_and_emit(finals)
    return nc


def make_consts():
    c = np.zeros((128, 200), np.float32)
    c[:, 192] = EPS
    c[:, 0:128] = np.eye(128, dtype=np.float32)
    for g in range(4):
        w = 1 << (g + 1)
        for t in range(16):
            c[:, 128 + g * 16 + t] = 1.0 / min(t + 1, w)
    return c


_NC_CACHE = {}


def run_cores(inputs, S, NL, n_cores):
    key = (S, NL)
    if key not in _NC_CACHE:
        _NC_CACHE[key] = build(S, NL)
    nc = _NC_CACHE[key]
    consts = make_consts()
    in_maps = []
    f = lambda a: np.ascontiguousarray(np.asarray(a, dtype=np.float32))
    for i in range(n_cores):
        m = {
            "x": f(inputs["x"][i]),
            "p": f(inputs["p"][:, i]),
            "norm_mix_g": f(inputs["norm_mix_g"]),
            "w_in": f(inputs["w_in"]),
            "pool_w": f(inputs["pool_w"]),
            "pool_b": f(inputs["pool_b"]),
            "pool_scale": f(inputs["pool_scale"]),
            "rel_bias": f(inputs["rel_bias"]).reshape(NL * 8, 257),
            "w_out": f(inputs["w_out"]),
            "norm_ffn_g": f(inputs["norm_ffn_g"]),
            "w_up": f(inputs["w_up"]),
            "conv_w": f(inputs["conv_w"]),
            "conv_b": f(inputs["conv_b"]),
            "w_down": f(inputs["w_down"]),
            "norm_ple_g": f(inputs["norm_ple_g"]),
            "w_ple_gate": f(inputs["w_ple_gate"]),
            "b_ple_gate": f(inputs["b_ple_gate"]),
            "w_ple": f(inputs["w_ple"]),
            "final_g": f(inputs["final_g"]).reshape(1, D),
            "consts": consts,
        }
        in_maps.append(m)
    res = run_bass_kernel_spmd(nc, in_maps, core_ids=list(range(n_cores)))
    return np.stack([np.asarray(r["out"]) for r in res.results], axis=0)


def kernel(**inputs):
    out = run_cores(inputs, SEQ, DEPTH, 8)
    return out.astype(np.float32)
```

```python
import numpy as np
import concourse.bass as bass
import concourse.mybir as mybir
from concourse.bass_utils import run_bass_kernel_spmd

F32 = mybir.dt.float32
BF16 = mybir.dt.bfloat16
U8 = mybir.dt.uint8
ALU = mybir.AluOpType
AF = mybir.ActivationFunctionType

D = 1024
NCH = 8
T = 512
DFF = 2816
NF = 22
PLE = 256
NEG = -30000.0
EPS = 1e-6
SEQ = 4096
DEPTH = 4
NDMA_RING = 8


def _esz(dt):
    if dt == F32:
        return 4
    if dt == BF16:
        return 2
    if dt == U8:
        return 1
    raise ValueError(dt)


class Op:
    __slots__ = ("eng", "emit", "deps", "signal", "epoch", "value", "is_dma", "dsem", "dval", "ring_wait")

    def __init__(self, eng, emit, epoch):
        self.eng = eng
        self.emit = emit
        self.deps = set()
        self.signal = False
        self.epoch = epoch
        self.value = None
        self.is_dma = False
        self.dsem = None
        self.dval = None
        self.ring_wait = None


class Prog:
    ENGS = ("pe", "act", "dve", "pool", "sp")
    PAGE = 4096

    def __init__(self, nc):
        self.nc = nc
        self.ops = {e: [] for e in self.ENGS}
        self.recs = {}
        self.epoch = 0
        self.dma_n = {e: 0 for e in self.ENGS}
        self.dma_last = {}

    @staticmethod
    def rng(ap):
        sp = str(ap.space)
        pat = ap.ap
        esz = _esz(ap.dtype)
        off = int(ap.offset)
        if "DRAM" in sp:
            lo = off
            hi = off
            for (s, c) in pat:
                ext = (c - 1) * s
                if ext < 0:
                    lo += ext
                else:
                    hi += ext
            return ("D:" + ap.tensor.name, lo * esz, (hi + 1) * esz)
        pstride = pat[0][0]
        fo = off % pstride if pstride > 0 else off
        lo = fo
        hi = fo
        for (s, c) in pat[1:]:
            ext = (c - 1) * s
            if ext < 0:
                lo += ext
            else:
                hi += ext
        name = "SB" if "SB" in sp else ("P:" + ap.tensor.name)
        return (name, lo * esz, (hi + 1) * esz)

    def _pages(self, lo, hi):
        return range(lo // self.PAGE, (hi - 1) // self.PAGE + 1)

    def _access(self, op, ap, kind):
        space, lo, hi = self.rng(ap)
        pages = self.recs.setdefault(space, {})
        for pg in self._pages(lo, hi):
            lst = pages.get(pg)
            if not lst:
                continue
            for rec in lst:
                if rec[0] < hi and lo < rec[1]:
                    if kind == "W" or rec[2] == "W":
                        if rec[3] is not op:
                            op.deps.add(rec[3])
        return space, lo, hi

    def _record(self, op, space, lo, hi, kind):
        pages = self.recs[space]
        for pg in self._pages(lo, hi):
            lst = pages.setdefault(pg, [])
            plo = max(lo, pg * self.PAGE)
            phi = min(hi, (pg + 1) * self.PAGE)
            if kind == "W":
                lst[:] = [r for r in lst if not (plo <= max(r[0], pg * self.PAGE) and min(r[1], (pg + 1) * self.PAGE) <= phi)]
                lst.append([lo, hi, "W", op])
            else:
                done = False
                if not op.is_dma:
                    for r in lst:
                        if r[2] == "R" and r[0] == lo and r[1] == hi and r[3].eng == op.eng and not r[3].is_dma:
                            r[3] = op
                            done = True
                            break
                if not done:
                    lst.append([lo, hi, "R", op])

    def op(self, eng, emit, reads=(), writes=()):
        o = Op(eng, emit, self.epoch)
        acc = []
        for ap in reads:
            if ap is None or isinstance(ap, (int, float)):
                continue
            acc.append(self._access(o, ap, "R") + ("R",))
        for ap in writes:
            acc.append(self._access(o, ap, "W") + ("W",))
        for (space, lo, hi, kind) in acc:
            self._record(o, space, lo, hi, kind)
        self.ops[eng].append(o)
        return o

    def dma(self, eng, out, in_, **kw):
        def emit(e):
            return e.dma_start(out=out, in_=in_, **kw)
        o = Op(eng, emit, self.epoch)
        o.is_dma = True
        n = self.dma_n[eng]
        self.dma_n[eng] = n + 1
        ring = n % NDMA_RING
        o.dsem = (eng, ring)
        o.dval = 16 * (n // NDMA_RING + 1)
        prev = self.dma_last.get((eng, ring))
        o.ring_wait = prev
        self.dma_last[(eng, ring)] = o
        a1 = self._access(o, in_, "R") + ("R",)
        a2 = self._access(o, out, "W") + ("W",)
        self._record(o, *a1)
        self._record(o, *a2)
        self.ops[eng].append(o)
        return o

    def new_epoch(self):
        self.epoch += 1

    def finalize_and_emit(self, final_wait_ops):
        nc = self.nc
        for e in self.ENGS:
            for o in self.ops[e]:
                drop = []
                for d in o.deps:
                    if d.is_dma:
                        continue
                    if d.eng == o.eng and not o.is_dma:
                        if o.eng in ("pe", "sp"):
                            drop.append(d)
                            continue
                    d.signal = True
                for d in drop:
                    o.deps.discard(d)
        for o in final_wait_ops:
            if not o.is_dma:
                o.signal = True
        nep = self.epoch + 1
        for e in self.ENGS:
            cnt = [0] * nep
            for o in self.ops[e]:
                if o.is_dma:
                    continue
                if o.signal:
                    cnt[o.epoch] += 1
                    o.value = cnt[o.epoch]
        import contextlib
        with contextlib.ExitStack() as st:
            esem = {}
            for e in ("pe", "act", "dve", "pool"):
                for ep in range(nep):
                    esem[(e, ep)] = st.enter_context(nc.semaphore(f"s_{e}_{ep}"))
            dsem = {}
            for e in ("sp", "pool", "act"):
                if self.dma_n[e] == 0:
                    continue
                for r in range(NDMA_RING):
                    dsem[(e, r)] = st.enter_context(nc.semaphore(f"d_{e}_{r}"))
            block = st.enter_context(nc.Block())

            def run(eng_name, eng):
                waited = {}

                def wait(sem, val):
                    if waited.get(id(sem), 0) >= val:
                        return
                    waited[id(sem)] = val
                    eng.wait_ge(sem, val)

                for o in self.ops[eng_name]:
                    if o.is_dma and o.ring_wait is not None:
                        wait(dsem[o.ring_wait.dsem], o.ring_wait.dval)
                    for d in o.deps:
                        if d.is_dma:
                            wait(dsem[d.dsem], d.dval)
                        else:
                            wait(esem[(d.eng, d.epoch)], d.value)
                    ins = o.emit(eng)
                    if o.is_dma:
                        ins.then_inc(dsem[o.dsem], 16)
                    elif o.signal:
                        ins.then_inc(esem[(o.eng, o.epoch)], 1)
                if eng_name == "sp":
                    for o in final_wait_ops:
                        if o.is_dma:
                            wait(dsem[o.dsem], o.dval)
                        else:
                            wait(esem[(o.eng, o.epoch)], o.value)

            @block.tensor
            def _(eng):
                run("pe", eng)

            @block.scalar
            def _(eng):
                run("act", eng)

            @block.vector
            def _(eng):
                run("dve", eng)

            @block.gpsimd
            def _(eng):
                run("pool", eng)

            @block.sync
            def _(eng):
                run("sp", eng)

    def mm(self, out, lhsT, rhs, start, stop, **kw):
        return self.op("pe", lambda e: e.matmul(out, lhsT=lhsT, rhs=rhs, start=start, stop=stop, **kw),
                       reads=(lhsT, rhs), writes=(out,))

    def tr(self, out, in_, ident):
        return self.op("pe", lambda e: e.transpose(out, in_, ident), reads=(in_, ident), writes=(out,))

    def act(self, out, in_, func, bias=None, scale=None):
        kw = {}
        if bias is not None:
            kw["bias"] = bias
        if scale is not None:
            kw["scale"] = scale
        return self.op("act", lambda e: e.activation(out=out, in_=in_, func=func, **kw),
                       reads=(in_, bias, scale), writes=(out,))

    def tt(self, eng, out, in0, in1, op):
        return self.op(eng, lambda e: e.tensor_tensor(out=out, in0=in0, in1=in1, op=op),
                       reads=(in0, in1), writes=(out,))

    def ts(self, eng, out, in0, s1, s2, op0, op1=None):
        if op1 is None:
            return self.op(eng, lambda e: e.tensor_single_scalar(out=out, in_=in0, scalar=s1, op=op0),
                           reads=(in0, s1), writes=(out,))
        return self.op(eng, lambda e: e.tensor_scalar(out=out, in0=in0, scalar1=s1, scalar2=s2, op0=op0, op1=op1),
                       reads=(in0, s1, s2), writes=(out,))

    def stt(self, out, in0, scalar, in1, op0, op1):
        return self.op("dve", lambda e: e.scalar_tensor_tensor(out=out, in0=in0, scalar=scalar, in1=in1, op0=op0, op1=op1),
                       reads=(in0, scalar, in1), writes=(out,))

    def copy(self, eng, out, in_):
        if eng == "act":
            return self.act(out, in_, AF.Copy)
        return self.op(eng, lambda e: e.tensor_copy(out=out, in_=in_), reads=(in_,), writes=(out,))

    def memset(self, eng, out, val):
        return self.op(eng, lambda e: e.memset(out, val), writes=(out,))

    def recip(self, out, in_):
        return self.op("dve", lambda e: e.reciprocal(out=out, in_=in_), reads=(in_,), writes=(out,))


class Arena:
    def __init__(self, nc, nbytes):
        self.t = nc.alloc_sbuf_tensor("arena", [128, nbytes], U8)
        self.nbytes = nbytes

    def view(self, off, shape, dt):
        n = 1
        for s in shape[1:]:
            n *= s
        nb = n * _esz(dt)
        assert off % 4 == 0 and off + nb <= self.nbytes, (off, nb, self.nbytes)
        v = self.t[:, off:off + nb].bitcast(dt)
        if len(shape) == 2:
            return v
        if len(shape) == 3:
            return v.rearrange("p (a b) -> p a b", a=shape[1])
        if len(shape) == 4:
            return v.rearrange("p (a b c) -> p a b c", a=shape[1], b=shape[2])
        if len(shape) == 5:
            return v.rearrange("p (a b c d) -> p a b c d", a=shape[1], b=shape[2], c=shape[3])
        raise ValueError(shape)


def build(S=SEQ, NL=DEPTH):
    NB = S // T
    nc = bass.Bass("TRN2", target_bir_lowering=False)
    P = Prog(nc)

    def din(name, shape):
        return nc.dram_tensor(name, list(shape), F32, kind="ExternalInput").ap()

    x_d = din("x", [S, D])
    p_d = din("p", [NL, S, PLE])
    gmix_d = din("norm_mix_g", [NL, D])
    w_in_d = din("w_in", [NL, D, 2048])
    pool_w_d = din("pool_w", [NL, 4, 128, 128])
    pool_b_d = din("pool_b", [NL, 512])
    pool_s_d = din("pool_scale", [NL, 512])
    relb_d = din("rel_bias", [NL * 8, 257])
    w_out_d = din("w_out", [NL, D, D])
    gffn_d = din("norm_ffn_g", [NL, D])
    w_up_d = din("w_up", [NL, D, 2 * DFF])
    conv_w_d = din("conv_w", [NL, 3, DFF])
    conv_b_d = din("conv_b", [NL, DFF])
    w_down_d = din("w_down", [NL, DFF, D])
    gple_d = din("norm_ple_g", [NL, D])
    w_gate_d = din("w_ple_gate", [NL, D, D])
    b_gate_d = din("b_ple_gate", [NL, D])
    w_ple_d = din("w_ple", [NL, PLE, D])
    gfin_d = din("final_g", [1, D])
    consts_d = din("consts", [128, 200])
    out_d = nc.dram_tensor("out", [S, D], F32, kind="ExternalOutput").ap()
    hT_d = nc.dram_tensor("hT_scr", [NCH, 128, S], F32, kind="Internal").ap()
    E_t = nc.dram_tensor("E_scr", [NL * 8, 128, 768], F32, kind="Internal")
    E_d = E_t.ap()

    off = 0

    def take(n):
        nonlocal off
        o = off
        off += (n + 31) // 32 * 32
        return o

    o_hblk = take(NCH * T * 4)
    o_hblk1 = take(NCH * T * 4)
    o_hb = take(NCH * T * 2)
    o_rstd = take(T * 4)
    o_consts = take(200 * 4)
    o_ones = take(128 * 2)
    o_gains = take((3 * NL + 1) * 8 * 4)
    o_poolb = take(NL * 4 * 4)
    o_pools = take(NL * 4 * 4)
    o_convw = take(NL * 3 * NF * 4)
    o_convb = take(NL * NF * 4)
    o_bgate = take(NL * 8 * 4)
    o_ghalo = take(NF * 2 * 4)
    o_W = take(135168)
    o_X = take(28160)
    o_zero = o_X
    total = off
    assert total <= 208 * 1024, total
    A = Arena(nc, total)

    hblks = [A.view(o_hblk, [128, NCH, T], F32), A.view(o_hblk1, [128, NCH, T], F32)]
    H = [hblks[0]]
    hb = A.view(o_hb, [128, NCH, T], BF16)
    rstd = A.view(o_rstd, [128, T], F32)
    consts = A.view(o_consts, [128, 200], F32)
    eps_col = consts[:, 192:193]
    ident = consts[:, 0:128]
    invcnt = consts[:, 128:192].rearrange("p (g t) -> p g t", g=4)
    ones_bf = A.view(o_ones, [128, 128], BF16)
    gains = A.view(o_gains, [128, 3 * NL + 1, 8], F32)
    poolb = A.view(o_poolb, [128, NL * 4], F32)
    pools = A.view(o_pools, [128, NL * 4], F32)
    convw = A.view(o_convw, [128, NL * 3, NF], F32)
    convb = A.view(o_convb, [128, NL, NF], F32)
    bgate = A.view(o_bgate, [128, NL, 8], F32)
    ghalo = A.view(o_ghalo, [128, NF, 2], F32)
    zero = A.view(o_zero, [128, T], F32)

    w_up = A.view(o_W, [128, NF, 2, NCH, 128], BF16)
    w_down = A.view(o_W + 90112, [128, NF, D], BF16)
    w_in = A.view(o_W, [128, NCH, 2048], BF16)
    w_out = A.view(o_W + 32768, [128, NCH, D], BF16)
    w_gate = A.view(o_W + 49152, [128, NCH, D], BF16)
    w_ple = A.view(o_W + 65536, [128, 2, D], BF16)
    pool_w = A.view(o_W + 69632, [128, 4, 128], BF16)
    oM = o_W + 70656
    Mb = A.view(oM, [128, 8, 640], F32)
    vaug = A.view(oM + 20480, [128, 8, 4, 2, 128], BF16)
    kT = A.view(oM + 36864, [128, 4, 2 * T], BF16)
    qT = A.view(oM + 45056, [128, 4, T], BF16)
    cat = A.view(oM + 49152, [128, NCH, T], BF16)
    pstage = A.view(oM + 57344, [128, 4, PLE], F32)
    pT = A.view(oM + 61440, [128, 2, T], BF16)
    iostage = A.view(oM, [128, 4, D], F32)
    Esb = A.view(oM + 20480, [32, 768], F32)
    NFH = NF // 2
    actb = A.view(o_X, [128, NFH, T], BF16)
    gbuf = [A.view(o_X + 11264 + i * 2080, [128, 520], F32) for i in range(2)]
    accb = [A.view(o_X + 15424 + i * 2048, [128, T], F32) for i in range(3)]
    ubuf = A.view(o_X, [128, 4, 528], F32)
    ptmp = [A.view(o_X + 8448 + i * 2112, [128, 528], F32) for i in range(2)]
    ybf = [A.view(o_X + 12672 + i * 1024, [128, T], BF16) for i in range(2)]
    NPT = 5
    pTs = [A.view(o_X + 14720 + i * 1024, [128, T], BF16) for i in range(NPT)]
    gateb = [A.view(o_X + 19840 + i * 2048, [128, T], F32) for i in range(2)]
    recb = [A.view(o_X + 23936 + i * 2048, [128, T], F32) for i in range(2)]
    tmp16 = A.view(o_X + 28032, [128, 16], F32)

    psb = [nc.alloc_psum_tensor(f"ps{i}", [128, T], F32)[:, :] for i in range(8)]
    pools_ps = {"mm": [0, 1], "sc": [2, 3, 4, 5], "o": [6, 7], "aux": [0, 1], "g": [2, 3, 4], "v": [5, 6, 7]}
    ps_ctr = {k: 0 for k in pools_ps}

    def bank(pool):
        lst = pools_ps[pool]
        if pool == "aux":
            pool = "mm"
        i = ps_ctr[pool]
        ps_ctr[pool] = i + 1
        return psb[lst[i % len(lst)]]

    ctr = {"i": 0}

    def rr(lst):
        ctr["i"] += 1
        return lst[ctr["i"] % len(lst)]

    P.dma("sp", consts, consts_d)
    P.memset("pool", ones_bf, 1.0 / 1024.0)
    P.memset("pool", zero, 0.0)
    for L in range(NL):
        P.dma("sp", gains[:, 3 * L + 0, :], gmix_d[L].rearrange("(c p) -> p c", p=128))
        P.dma("sp", gains[:, 3 * L + 1, :], gffn_d[L].rearrange("(c p) -> p c", p=128))
        P.dma("sp", gains[:, 3 * L + 2, :], gple_d[L].rearrange("(c p) -> p c", p=128))
        P.dma("sp", poolb[:, 4 * L:4 * L + 4], pool_b_d[L].rearrange("(c p) -> p c", p=128))
        P.dma("sp", pools[:, 4 * L:4 * L + 4], pool_s_d[L].rearrange("(c p) -> p c", p=128))
        for k in range(3):
            P.dma("sp", convw[:, 3 * L + k, :], conv_w_d[L, k].rearrange("(c p) -> p c", p=128))
        P.dma("sp", convb[:, L, :], conv_b_d[L].rearrange("(c p) -> p c", p=128))
        P.dma("sp", bgate[:, L, :], b_gate_d[L].rearrange("(c p) -> p c", p=128))
    P.dma("sp", gains[:, 3 * NL, :], gfin_d[0].rearrange("(c p) -> p c", p=128))
    P.dma("sp", Esb[0:NL * 8, 0:256], relb_d[:, 1:257])
    P.ts("dve", Esb[0:NL * 8, 256:768], zero[0:NL * 8, 0:512], Esb[0:NL * 8, 255:256], None, ALU.add)
    for L in range(NL):
        P.dma("sp", E_d[L * 8:(L + 1) * 8], Esb[L * 8:(L + 1) * 8, :].unsqueeze(1).broadcast_to([8, 128, 768]))

    def load_h(b, slot):
        for c in range(NCH):
            P.dma("sp", hblks[slot][:, c, :], hT_d[c, :, b * T:(b + 1) * T])

    def store_h(b):
        for c in range(NCH):
            P.dma("sp", hT_d[c, :, b * T:(b + 1) * T], H[0][:, c, :])

    def wview(wd, L):
        return wd[L].rearrange("(c p) n -> p c n", p=128)

    def load_w_AC(Lc, La):
        if Lc is not None:
            for n0 in (0, 512):
                P.dma("pool", w_gate[:, :, n0:n0 + 512], wview(w_gate_d, Lc)[:, :, n0:n0 + 512])
            P.dma("pool", w_ple[:, :, :], wview(w_ple_d, Lc))
        if La is not None:
            for n0 in range(0, 2048, 512):
                P.dma("pool", w_in[:, :, n0:n0 + 512], wview(w_in_d, La)[:, :, n0:n0 + 512])
            P.dma("pool", pool_w[:, :, :], pool_w_d[La].rearrange("g c d -> c g d"))
            for n0 in (0, 512):
                P.dma("pool", w_out[:, :, n0:n0 + 512], wview(w_out_d, La)[:, :, n0:n0 + 512])
            src = bass.AP(E_t, (La * 8) * 128 * 768 + 127, [[767, 128], [128 * 768, 8], [1, 640]])
            P.dma("sp", Mb[:, :, :], src)
            P.memset("pool", Mb[64:128, :, 0:64], NEG)
            P.memset("pool", Mb[0:64, :, 576:640], NEG)
            P.memset("pool", vaug[:, :, :, :, :], 1.0)

    def load_w_B(L):
        wv = wview(w_up_d, L)
        for f in range(NF):
            for gv in range(2):
                c0 = gv * DFF + f * 128
                P.dma("pool", w_up[:, f, gv, :, :], wv[:, :, c0:c0 + 128])
        wd = w_down_d[L].rearrange("(f p) n -> p f n", p=128)
        for n0 in range(0, D, 256):
            P.dma("pool", w_down[:, :, n0:n0 + 256], wd[:, :, n0:n0 + 256])

    def norm(gidx):
        for c in range(NCH):
            P.act(hb[:, c, :], H[0][:, c, :], AF.Square)
        bk = bank("aux")
        for c in range(NCH):
            P.mm(bk, ones_bf, hb[:, c, :], c == 0, c == NCH - 1)
        P.act(rstd, bk, AF.Ln, bias=eps_col)
        P.act(rstd, rstd, AF.Exp, scale=-0.5)
        return gidx

    def norm_apply_bf(gidx):
        for c in range(NCH):
            P.stt(hb[:, c, :], H[0][:, c, :], gains[:, gidx, c:c + 1], rstd, ALU.mult, ALU.mult)

    def proj(wt, col0, rhs_chunks, nk):
        bk = bank("mm")
        for c in range(nk):
            P.mm(bk, wt[:, c, col0:col0 + 128], rhs_chunks[:, c, :], c == 0, c == nk - 1)
        return bk

    def stage_init(b):
        P.dma("sp", iostage, x_d[b * T:(b + 1) * T, :].rearrange("(tt p) f -> p tt f", p=128))
        for c in range(NCH):
            bk = bank("mm")
            for tt in range(4):
                P.tr(bk[:, tt * 128:(tt + 1) * 128], iostage[:, tt, c * 128:(c + 1) * 128], ident)
            P.copy("act" if c % 2 else "dve", H[0][:, c, :], bk)

    def stage_A(L, b):
        norm(3 * L)
        norm_apply_bf(3 * L)
        cur = b % 2
        prv = 1 - cur
        for g in range(4):
            bk = proj(w_in, g * 128, hb, NCH)
            if b == 0:
                P.memset("pool", ubuf[:, g, 0:16], 0.0)
            else:
                P.copy("pool", ubuf[:, g, 0:16], ubuf[:, g, 512:528])
            P.copy("act", ubuf[:, g, 16:528], bk)
        for j in range(4):
            bk = proj(w_in, 512 + j * 128, hb, NCH)
            P.act(qT[:, j, :], bk, AF.Copy, scale=0.125)
        for j in range(4):
            bk = proj(w_in, 1024 + j * 128, hb, NCH)
            P.copy("dve", kT[:, j, cur * T:(cur + 1) * T], bk)
        for tt in range(4):
            bk = bank("mm")
            for c in range(NCH):
                P.mm(bk, hb[:, c, tt * 128:(tt + 1) * 128], w_in[:, c, 1536:2048], c == 0, c == NCH - 1)
            bv = bk.rearrange("p (j e d) -> p j e d", j=4, e=2)
            tile = cur * 4 + tt
            P.copy("act", vaug[:, tile, :, 0, 0:64], bv[:, :, 0, :])
            P.copy("dve", vaug[:, tile, :, 1, 64:128], bv[:, :, 1, :])
        for g in range(4):
            src = ubuf[:, g, :]
            prev = src
            for k in range(1, g + 2):
                sh = 1 << (k - 1)
                lo = (1 << k) - 1
                dst = ptmp[k % 2]
                P.tt("pool", dst[:, lo:528], prev[:, lo:528], prev[:, lo - sh:528 - sh], ALU.add)
                prev = dst
            w = 1 << (g + 1)
            yb = ybf[g % 2]
            P.stt(yb, prev[:, 16:528], 1.0 / w, src[:, 16:528], ALU.mult, ALU.subtract)
            if b == 0:
                P.tt("dve", tmp16, prev[:, 16:32], invcnt[:, g, :], ALU.mult)
                P.tt("dve", yb[:, 0:16], tmp16, src[:, 16:32], ALU.subtract)
            bk = bank("mm")
            P.mm(bk, pool_w[:, g, :], yb, True, True)
            P.ts("dve", cat[:, g, :], bk, poolb[:, 4 * L + g:4 * L + g + 1], pools[:, 4 * L + g:4 * L + g + 1], ALU.add, ALU.mult)
        steps = []
        for h in range(8):
            rs = [r for r in range(8) if 4 * b - 4 + r >= 0]
            for r in rs:
                steps.append((h, r, r == rs[0], r == rs[-1]))
        LA = 3
        st_pt = {}
        obs = {}

        def front(i):
            h, r, first, last = steps[i]
            j = h // 2
            po = 64 * (h % 2)
            qlo = max(0, r - 4)
            qhi = min(3, r)
            nq = qhi - qlo + 1
            half = prv if r < 4 else cur
            kc0 = half * T + (r % 4) * 128
            sbk = bank("sc")
            P.mm(sbk[:, 0:nq * 128], kT[po:po + 64, j, kc0:kc0 + 128], qT[po:po + 64, j, qlo * 128:(qhi + 1) * 128], True, True)
            d0 = qlo - r + 4
            P.tt("dve", sbk[:, 0:nq * 128], sbk[:, 0:nq * 128], Mb[:, h, d0 * 128:(d0 + nq) * 128], ALU.add)
            pt = pTs[i % NPT]
            P.act(pt[:, 0:nq * 128], sbk[:, 0:nq * 128], AF.Exp)
            st_pt[i] = pt

        def back(i):
            h, r, first, last = steps[i]
            j = h // 2
            e = h % 2
            po = 64 * e
            so = 64 - po
            qlo = max(0, r - 4)
            qhi = min(3, r)
            nq = qhi - qlo + 1
            half = prv if r < 4 else cur
            vt = half * 4 + (r % 4)
            if first:
                obs[h] = bank("o")
            ob = obs[h]
            P.mm(ob[:, qlo * 128:(qhi + 1) * 128], vaug[:, vt, j, e, :], st_pt[i][:, 0:nq * 128], first, last,
                 skip_group_check=True)
            if last:
                rec = rr(recb)
                P.act(rec[so:so + 64, :], ob[so:so + 64, :], AF.Ln)
                P.act(rec[so:so + 64, :], rec[so:so + 64, :], AF.Exp, scale=-1.0)
                P.tt("dve", cat[po:po + 64, 4 + j, :], ob[po:po + 64, :], rec[so:so + 64, :], ALU.mult)

        for i in range(len(steps) + LA):
            if i < len(steps):
                front(i)
            if i >= LA:
                back(i - LA)
        for m in range(NCH):
            bk = proj(w_out, m * 128, cat, NCH)
            P.tt("dve", H[0][:, m, :], bk, H[0][:, m, :], ALU.add)

    def stage_B(L, b):
        norm(3 * L + 1)
        norm_apply_bf(3 * L + 1)
        for hf in range(2):
            for fi in range(NFH):
                f = hf * NFH + fi
                bg = bank("g")
                for c in range(NCH):
                    P.mm(bg, w_up[:, f, 0, c, :], hb[:, c, :], c == 0, c == NCH - 1)
                bv = bank("v")
                for c in range(NCH):
                    P.mm(bv, w_up[:, f, 1, c, :], hb[:, c, :], c == 0, c == NCH - 1)
                gb = gbuf[f % 2]
                if b == 0:
                    P.memset("pool", gb[:, 0:2], 0.0)
                else:
                    P.copy("pool", gb[:, 0:2], ghalo[:, f, :])
                P.copy("act", gb[:, 2:514], bg)
                P.copy("pool", ghalo[:, f, :], gb[:, 512:514])
                acc = accb[f % 3]
                P.act(acc, gb[:, 0:512], AF.Identity, bias=convb[:, L, f:f + 1], scale=convw[:, 3 * L + 0, f:f + 1])
                P.stt(acc, gb[:, 1:513], convw[:, 3 * L + 1, f:f + 1], acc, ALU.mult, ALU.add)
                P.stt(acc, gb[:, 2:514], convw[:, 3 * L + 2, f:f + 1], acc, ALU.mult, ALU.add)
                P.act(acc, acc, AF.Gelu)
                P.tt("dve", actb[:, fi, :], acc, bv, ALU.mult)
            for m in range(NCH):
                bk = bank("mm")
                for fi in range(NFH):
                    f = hf * NFH + fi
                    P.mm(bk, w_down[:, f, m * 128:(m + 1) * 128], actb[:, fi, :], fi == 0, fi == NFH - 1)
                P.tt("dve", H[0][:, m, :], bk, H[0][:, m, :], ALU.add)

    def stage_C(L, b):
        P.dma("sp", pstage, p_d[L, b * T:(b + 1) * T, :].rearrange("(tt p) k -> p tt k", p=128))
        norm(3 * L + 2)
        norm_apply_bf(3 * L + 2)
        for kc in range(2):
            bk = bank("aux")
            for tt in range(4):
                P.tr(bk[:, tt * 128:(tt + 1) * 128], pstage[:, tt, kc * 128:(kc + 1) * 128], ident)
            P.copy("act", pT[:, kc, :], bk)
        for m in range(NCH):
            bg = proj(w_gate, m * 128, hb, NCH)
            gt = rr(gateb)
            P.act(gt, bg, AF.Sigmoid, bias=bgate[:, L, m:m + 1])
            bp = bank("sc")
            for kc in range(2):
                P.mm(bp, w_ple[:, kc, m * 128:(m + 1) * 128], pT[:, kc, :], kc == 0, kc == 1)
            P.tt("dve", gt, gt, bp, ALU.mult)
            P.tt("pool", H[0][:, m, :], gt, H[0][:, m, :], ALU.add)

    def stage_final(b):
        norm(3 * NL)
        for c in range(NCH):
            P.stt(H[0][:, c, :], H[0][:, c, :], gains[:, 3 * NL, c:c + 1], rstd, ALU.mult, ALU.mult)
        for tt in range(4):
            for hf in range(2):
                bk = bank("mm")
                for i in range(4):
                    c = hf * 4 + i
                    P.tr(bk[:, i * 128:(i + 1) * 128], H[0][:, c, tt * 128:(tt + 1) * 128], ident)
                P.copy("act" if hf else "dve", iostage[:, tt, hf * 512:(hf + 1) * 512], bk)
        return P.dma("sp", out_d[b * T:(b + 1) * T, :].rearrange("(tt p) f -> p tt f", p=128), iostage)

    finals = []
    for b in range(NB):
        H[0] = hblks[b % 2]
        stage_init(b)
        store_h(b)
    sweeps = []
    for L in range(NL):
        sweeps.append(("A", L))
        sweeps.append(("B", L))
    sweeps.append(("F", NL))
    items = [(si, b) for si in range(len(sweeps)) for b in range(NB)]
    load_h(0, 0)
    for idx, (si, b) in enumerate(items):
        kind, L = sweeps[si]
        slot = idx % 2
        if b == 0:
            P.new_epoch()
            if kind == "A":
                load_w_AC(L - 1 if L > 0 else None, L)
            elif kind == "B":
                load_w_B(L)
            else:
                load_w_AC(NL - 1, None)
        if idx + 1 < len(items):
            load_h(items[idx + 1][1], 1 - slot)
        H[0] = hblks[slot]
        if kind == "A":
            if L > 0:
                stage_C(L - 1, b)
            stage_A(L, b)
            store_h(b)
        elif kind == "B":
            stage_B(L, b)
            store_h(b)
        else:
            stage_C(NL - 1, b)
            finals.append(stage_final(b))
    with nc.allow_non_contiguous_dma(reason="small one-time parameter vectors"):
        P.finalize_and_emit(finals)
    return nc


def make_consts():
    c = np.zeros((128, 200), np.float32)
    c[:, 192] = EPS
    c[:, 0:128] = np.eye(128, dtype=np.float32)
    for g in range(4):
        w = 1 << (g + 1)
        for t in range(16):
            c[:, 128 + g * 16 + t] = 1.0 / min(t + 1, w)
    return c


_NC_CACHE = {}


def run_cores(inputs, S, NL, n_cores):
    key = (S, NL)
    if key not in _NC_CACHE:
        _NC_CACHE[key] = build(S, NL)
    nc = _NC_CACHE[key]
    consts = make_consts()
    in_maps = []
    f = lambda a: np.ascontiguousarray(np.asarray(a, dtype=np.float32))
    for i in range(n_cores):
        m = {
            "x": f(inputs["x"][i]),
            "p": f(inputs["p"][:, i]),
            "norm_mix_g": f(inputs["norm_mix_g"]),
            "w_in": f(inputs["w_in"]),
            "pool_w": f(inputs["pool_w"]),
            "pool_b": f(inputs["pool_b"]),
            "pool_scale": f(inputs["pool_scale"]),
            "rel_bias": f(inputs["rel_bias"]).reshape(NL * 8, 257),
            "w_out": f(inputs["w_out"]),
            "norm_ffn_g": f(inputs["norm_ffn_g"]),
            "w_up": f(inputs["w_up"]),
            "conv_w": f(inputs["conv_w"]),
            "conv_b": f(inputs["conv_b"]),
            "w_down": f(inputs["w_down"]),
            "norm_ple_g": f(inputs["norm_ple_g"]),
            "w_ple_gate": f(inputs["w_ple_gate"]),
            "b_ple_gate": f(inputs["b_ple_gate"]),
            "w_ple": f(inputs["w_ple"]),
            "final_g": f(inputs["final_g"]).reshape(1, D),
            "consts": consts,
        }
        in_maps.append(m)
    res = run_bass_kernel_spmd(nc, in_maps, core_ids=list(range(n_cores)))
    return np.stack([np.asarray(r["out"]) for r in res.results], axis=0)


def kernel(**inputs):
    out = run_cores(inputs, SEQ, DEPTH, 8)
    return out.astype(np.float32)
```

```python
import numpy as np
import concourse.bass as bass
import concourse.mybir as mybir
from concourse.bass_utils import run_bass_kernel_spmd

F32 = mybir.dt.float32
BF16 = mybir.dt.bfloat16
U8 = mybir.dt.uint8
ALU = mybir.AluOpType
AF = mybir.ActivationFunctionType

D = 1024
NCH = 8
T = 512
DFF = 2816
NF = 22
PLE = 256
NEG = -30000.0
EPS = 1e-6
SEQ = 4096
DEPTH = 4
NDMA_RING = 8


def _esz(dt):
    if dt == F32:
        return 4
    if dt == BF16:
        return 2
    if dt == U8:
        return 1
    raise ValueError(dt)


class Op:
    __slots__ = ("eng", "emit", "deps", "signal", "epoch", "value", "is_dma", "dsem", "dval", "ring_wait")

    def __init__(self, eng, emit, epoch):
        self.eng = eng
        self.emit = emit
        self.deps = set()
        self.signal = False
        self.epoch = epoch
        self.value = None
        self.is_dma = False
        self.dsem = None
        self.dval = None
        self.ring_wait = None


class Prog:
    ENGS = ("pe", "act", "dve", "pool", "sp")
    PAGE = 4096

    def __init__(self, nc):
        self.nc = nc
        self.ops = {e: [] for e in self.ENGS}
        self.recs = {}
        self.epoch = 0
        self.dma_n = {e: 0 for e in self.ENGS}
        self.dma_last = {}

    @staticmethod
    def rng(ap):
        sp = str(ap.space)
        pat = ap.ap
        esz = _esz(ap.dtype)
        off = int(ap.offset)
        if "DRAM" in sp:
            lo = off
            hi = off
            for (s, c) in pat:
                ext = (c - 1) * s
                if ext < 0:
                    lo += ext
                else:
                    hi += ext
            return ("D:" + ap.tensor.name, lo * esz, (hi + 1) * esz)
        pstride = pat[0][0]
        fo = off % pstride if pstride > 0 else off
        lo = fo
        hi = fo
        for (s, c) in pat[1:]:
            ext = (c - 1) * s
            if ext < 0:
                lo += ext
            else:
                hi += ext
        name = "SB" if "SB" in sp else ("P:" + ap.tensor.name)
        return (name, lo * esz, (hi + 1) * esz)

    def _pages(self, lo, hi):
        return range(lo // self.PAGE, (hi - 1) // self.PAGE + 1)

    def _access(self, op, ap, kind):
        space, lo, hi = self.rng(ap)
        pages = self.recs.setdefault(space, {})
        for pg in self._pages(lo, hi):
            lst = pages.get(pg)
            if not lst:
                continue
            for rec in lst:
                if rec[0] < hi and lo < rec[1]:
                    if kind == "W" or rec[2] == "W":
                        if rec[3] is not op:
                            op.deps.add(rec[3])
        return space, lo, hi

    def _record(self, op, space, lo, hi, kind):
        pages = self.recs[space]
        for pg in self._pages(lo, hi):
            lst = pages.setdefault(pg, [])
            plo = max(lo, pg * self.PAGE)
            phi = min(hi, (pg + 1) * self.PAGE)
            if kind == "W":
                lst[:] = [r for r in lst if not (plo <= max(r[0], pg * self.PAGE) and min(r[1], (pg + 1) * self.PAGE) <= phi)]
                lst.append([lo, hi, "W", op])
            else:
                done = False
                if not op.is_dma:
                    for r in lst:
                        if r[2] == "R" and r[0] == lo and r[1] == hi and r[3].eng == op.eng and not r[3].is_dma:
                            r[3] = op
                            done = True
                            break
                if not done:
                    lst.append([lo, hi, "R", op])

    def op(self, eng, emit, reads=(), writes=()):
        o = Op(eng, emit, self.epoch)
        acc = []
        for ap in reads:
            if ap is None or isinstance(ap, (int, float)):
                continue
            acc.append(self._access(o, ap, "R") + ("R",))
        for ap in writes:
            acc.append(self._access(o, ap, "W") + ("W",))
        for (space, lo, hi, kind) in acc:
            self._record(o, space, lo, hi, kind)
        self.ops[eng].append(o)
        return o

    def dma(self, eng, out, in_, **kw):
        def emit(e):
            return e.dma_start(out=out, in_=in_, **kw)
        o = Op(eng, emit, self.epoch)
        o.is_dma = True
        n = self.dma_n[eng]
        self.dma_n[eng] = n + 1
        ring = n % NDMA_RING
        o.dsem = (eng, ring)
        o.dval = 16 * (n // NDMA_RING + 1)
        prev = self.dma_last.get((eng, ring))
        o.ring_wait = prev
        self.dma_last[(eng, ring)] = o
        a1 = self._access(o, in_, "R") + ("R",)
        a2 = self._access(o, out, "W") + ("W",)
        self._record(o, *a1)
        self._record(o, *a2)
        self.ops[eng].append(o)
        return o

    def new_epoch(self):
        self.epoch += 1

    def finalize_and_emit(self, final_wait_ops):
        nc = self.nc
        for e in self.ENGS:
            for o in self.ops[e]:
                drop = []
                for d in o.deps:
                    if d.is_dma:
                        continue
                    if d.eng == o.eng and not o.is_dma:
                        if o.eng in ("pe", "sp"):
                            drop.append(d)
                            continue
                    d.signal = True
                for d in drop:
                    o.deps.discard(d)
        for o in final_wait_ops:
            if not o.is_dma:
                o.signal = True
        nep = self.epoch + 1
        for e in self.ENGS:
            cnt = [0] * nep
            for o in self.ops[e]:
                if o.is_dma:
                    continue
                if o.signal:
                    cnt[o.epoch] += 1
                    o.value = cnt[o.epoch]
        import contextlib
        with contextlib.ExitStack() as st:
            esem = {}
            for e in ("pe", "act", "dve", "pool"):
                for ep in range(nep):
                    esem[(e, ep)] = st.enter_context(nc.semaphore(f"s_{e}_{ep}"))
            dsem = {}
            for e in ("sp", "pool", "act"):
                if self.dma_n[e] == 0:
                    continue
                for r in range(NDMA_RING):
                    dsem[(e, r)] = st.enter_context(nc.semaphore(f"d_{e}_{r}"))
            block = st.enter_context(nc.Block())

            def run(eng_name, eng):
                waited = {}

                def wait(sem, val):
                    if waited.get(id(sem), 0) >= val:
                        return
                    waited[id(sem)] = val
                    eng.wait_ge(sem, val)

                for o in self.ops[eng_name]:
                    if o.is_dma and o.ring_wait is not None:
                        wait(dsem[o.ring_wait.dsem], o.ring_wait.dval)
                    for d in o.deps:
                        if d.is_dma:
                            wait(dsem[d.dsem], d.dval)
                        else:
                            wait(esem[(d.eng, d.epoch)], d.value)
                    ins = o.emit(eng)
                    if o.is_dma:
                        ins.then_inc(dsem[o.dsem], 16)
                    elif o.signal:
                        ins.then_inc(esem[(o.eng, o.epoch)], 1)
                if eng_name == "sp":
                    for o in final_wait_ops:
                        if o.is_dma:
                            wait(dsem[o.dsem], o.dval)
                        else:
                            wait(esem[(o.eng, o.epoch)], o.value)

            @block.tensor
            def _(eng):
                run("pe", eng)

            @block.scalar
            def _(eng):
                run("act", eng)

            @block.vector
            def _(eng):
                run("dve", eng)

            @block.gpsimd
            def _(eng):
                run("pool", eng)

            @block.sync
            def _(eng):
                run("sp", eng)

    def mm(self, out, lhsT, rhs, start, stop, **kw):
        return self.op("pe", lambda e: e.matmul(out, lhsT=lhsT, rhs=rhs, start=start, stop=stop, **kw),
                       reads=(lhsT, rhs), writes=(out,))

    def tr(self, out, in_, ident):
        return self.op("pe", lambda e: e.transpose(out, in_, ident), reads=(in_, ident), writes=(out,))

    def act(self, out, in_, func, bias=None, scale=None):
        kw = {}
        if bias is not None:
            kw["bias"] = bias
        if scale is not None:
            kw["scale"] = scale
        return self.op("act", lambda e: e.activation(out=out, in_=in_, func=func, **kw),
                       reads=(in_, bias, scale), writes=(out,))

    def tt(self, eng, out, in0, in1, op):
        return self.op(eng, lambda e: e.tensor_tensor(out=out, in0=in0, in1=in1, op=op),
                       reads=(in0, in1), writes=(out,))

    def ts(self, eng, out, in0, s1, s2, op0, op1=None):
        if op1 is None:
            return self.op(eng, lambda e: e.tensor_single_scalar(out=out, in_=in0, scalar=s1, op=op0),
                           reads=(in0, s1), writes=(out,))
        return self.op(eng, lambda e: e.tensor_scalar(out=out, in0=in0, scalar1=s1, scalar2=s2, op0=op0, op1=op1),
                       reads=(in0, s1, s2), writes=(out,))

    def stt(self, out, in0, scalar, in1, op0, op1):
        return self.op("dve", lambda e: e.scalar_tensor_tensor(out=out, in0=in0, scalar=scalar, in1=in1, op0=op0, op1=op1),
                       reads=(in0, scalar, in1), writes=(out,))

    def copy(self, eng, out, in_):
        if eng == "act":
            return self.act(out, in_, AF.Copy)
        return self.op(eng, lambda e: e.tensor_copy(out=out, in_=in_), reads=(in_,), writes=(out,))

    def memset(self, eng, out, val):
        return self.op(eng, lambda e: e.memset(out, val), writes=(out,))

    def recip(self, out, in_):
        return self.op("dve", lambda e: e.reciprocal(out=out, in_=in_), reads=(in_,), writes=(out,))


class Arena:
    def __init__(self, nc, nbytes):
        self.t = nc.alloc_sbuf_tensor("arena", [128, nbytes], U8)
        self.nbytes = nbytes

    def view(self, off, shape, dt):
        n = 1
        for s in shape[1:]:
            n *= s
        nb = n * _esz(dt)
        assert off % 4 == 0 and off + nb <= self.nbytes, (off, nb, self.nbytes)
        v = self.t[:, off:off + nb].bitcast(dt)
        if len(shape) == 2:
            return v
        if len(shape) == 3:
            return v.rearrange("p (a b) -> p a b", a=shape[1])
        if len(shape) == 4:
            return v.rearrange("p (a b c) -> p a b c", a=shape[1], b=shape[2])
        if len(shape) == 5:
            return v.rearrange("p (a b c d) -> p a b c d", a=shape[1], b=shape[2], c=shape[3])
        raise ValueError(shape)


def build(S=SEQ, NL=DEPTH):
    NB = S // T
    nc = bass.Bass("TRN2", target_bir_lowering=False)
    P = Prog(nc)

    def din(name, shape):
        return nc.dram_tensor(name, list(shape), F32, kind="ExternalInput").ap()

    x_d = din("x", [S, D])
    p_d = din("p", [NL, S, PLE])
    gmix_d = din("norm_mix_g", [NL, D])
    w_in_d = din("w_in", [NL, D, 2048])
    pool_w_d = din("pool_w", [NL, 4, 128, 128])
    pool_b_d = din("pool_b", [NL, 512])
    pool_s_d = din("pool_scale", [NL, 512])
    relb_d = din("rel_bias", [NL * 8, 257])
    w_out_d = din("w_out", [NL, D, D])
    gffn_d = din("norm_ffn_g", [NL, D])
    w_up_d = din("w_up", [NL, D, 2 * DFF])
    conv_w_d = din("conv_w", [NL, 3, DFF])
    conv_b_d = din("conv_b", [NL, DFF])
    w_down_d = din("w_down", [NL, DFF, D])
    gple_d = din("norm_ple_g", [NL, D])
    w_gate_d = din("w_ple_gate", [NL, D, D])
    b_gate_d = din("b_ple_gate", [NL, D])
    w_ple_d = din("w_ple", [NL, PLE, D])
    gfin_d = din("final_g", [1, D])
    consts_d = din("consts", [128, 200])
    out_d = nc.dram_tensor("out", [S, D], F32, kind="ExternalOutput").ap()
    hT_d = nc.dram_tensor("hT_scr", [NCH, 128, S], F32, kind="Internal").ap()
    E_t = nc.dram_tensor("E_scr", [NL * 8, 128, 768], F32, kind="Internal")
    E_d = E_t.ap()

    off = 0

    def take(n):
        nonlocal off
        o = off
        off += (n + 31) // 32 * 32
        return o

    o_hblk = take(NCH * T * 4)
    o_hblk1 = take(NCH * T * 4)
    o_hb = take(NCH * T * 2)
    o_rstd = take(T * 4)
    o_consts = take(200 * 4)
    o_ones = take(128 * 2)
    o_gains = take((3 * NL + 1) * 8 * 4)
    o_poolb = take(NL * 4 * 4)
    o_pools = take(NL * 4 * 4)
    o_convw = take(NL * 3 * NF * 4)
    o_convb = take(NL * NF * 4)
    o_bgate = take(NL * 8 * 4)
    o_ghalo = take(NF * 2 * 4)
    o_W = take(135168)
    o_X = take(29184)
    o_zero = o_X
    total = off
    print("SBUF arena bytes/partition:", total)
    assert total <= 208 * 1024, total
    A = Arena(nc, total)

    hblks = [A.view(o_hblk, [128, NCH, T], F32), A.view(o_hblk1, [128, NCH, T], F32)]
    H = [hblks[0]]
    hb = A.view(o_hb, [128, NCH, T], BF16)
    rstd = A.view(o_rstd, [128, T], F32)
    consts = A.view(o_consts, [128, 200], F32)
    eps_col = consts[:, 192:193]
    ident = consts[:, 0:128]
    invcnt = consts[:, 128:192].rearrange("p (g t) -> p g t", g=4)
    ones_bf = A.view(o_ones, [128, 128], BF16)
    gains = A.view(o_gains, [128, 3 * NL + 1, 8], F32)
    poolb = A.view(o_poolb, [128, NL * 4], F32)
    pools = A.view(o_pools, [128, NL * 4], F32)
    convw = A.view(o_convw, [128, NL * 3, NF], F32)
    convb = A.view(o_convb, [128, NL, NF], F32)
    bgate = A.view(o_bgate, [128, NL, 8], F32)
    ghalo = A.view(o_ghalo, [128, NF, 2], F32)
    zero = A.view(o_zero, [128, T], F32)

    w_up = A.view(o_W, [128, NF, 2, NCH, 128], BF16)
    w_down = A.view(o_W + 90112, [128, NF, D], BF16)
    w_in = A.view(o_W, [128, NCH, 2048], BF16)
    w_out = A.view(o_W + 32768, [128, NCH, D], BF16)
    w_gate = A.view(o_W + 49152, [128, NCH, D], BF16)
    w_ple = A.view(o_W + 65536, [128, 2, D], BF16)
    pool_w = A.view(o_W + 69632, [128, 4, 128], BF16)
    oM = o_W + 70656
    Mb = A.view(oM, [128, 8, 640], F32)
    vaug = A.view(oM + 20480, [128, 8, 4, 2, 128], BF16)
    kT = A.view(oM + 36864, [128, 4, 2 * T], BF16)
    qT = A.view(oM + 45056, [128, 4, T], BF16)
    cat = A.view(oM + 49152, [128, NCH, T], BF16)
    pstage = A.view(oM + 57344, [128, 4, PLE], F32)
    pT = A.view(oM + 61440, [128, 2, T], BF16)
    iostage = A.view(oM, [128, 4, D], F32)
    Esb = A.view(oM + 20480, [32, 768], F32)
    NFH = NF // 2
    actb = A.view(o_X, [128, NFH, T], BF16)
    gbuf = [A.view(o_X + 11264 + i * 2080, [128, 520], F32) for i in range(2)]
    accb = [A.view(o_X + 15424 + i * 2048, [128, T], F32) for i in range(3)]
    ubuf = A.view(o_X, [128, 4, 528], F32)
    ptmp = [A.view(o_X + 8448 + i * 2112, [128, 528], F32) for i in range(2)]
    ybf = [A.view(o_X + 12672 + i * 1024, [128, T], BF16) for i in range(4)]
    NPT = 4
    pTs = [A.view(o_X + 16768 + i * 2048, [128, 2, T], BF16) for i in range(NPT)]
    gateb = [A.view(o_X + 24960 + i * 2048, [128, T], F32) for i in range(2)]
    recb = gateb
    tmp16 = A.view(o_X + 29056, [128, 16], F32)

    psp = [nc.alloc_psum_tensor(f"psp{i}", [128, 2, T], F32)[:, :, :] for i in range(4)]
    psb = [psp[i // 2][:, i % 2, :] for i in range(8)]
    pools_ps = {"mm": [0, 1, 2, 3], "o": [6, 7], "aux": [0, 1, 2, 3], "g": [2, 3, 4], "v": [5, 6, 7]}
    scp_ctr = [0]

    def bank_pair():
        i = scp_ctr[0]
        scp_ctr[0] = i + 1
        return psp[i % 3]
    ps_ctr = {k: 0 for k in pools_ps}

    def bank(pool):
        lst = pools_ps[pool]
        if pool == "aux":
            pool = "mm"
        i = ps_ctr[pool]
        ps_ctr[pool] = i + 1
        return psb[lst[i % len(lst)]]

    ctr = {"i": 0}

    def rr(lst):
        ctr["i"] += 1
        return lst[ctr["i"] % len(lst)]

    P.dma("sp", consts, consts_d)
    P.memset("pool", ones_bf, 1.0 / 1024.0)
    P.memset("pool", zero, 0.0)
    for L in range(NL):
        P.dma("sp", gains[:, 3 * L + 0, :], gmix_d[L].rearrange("(c p) -> p c", p=128))
        P.dma("sp", gains[:, 3 * L + 1, :], gffn_d[L].rearrange("(c p) -> p c", p=128))
        P.dma("sp", gains[:, 3 * L + 2, :], gple_d[L].rearrange("(c p) -> p c", p=128))
        P.dma("sp", poolb[:, 4 * L:4 * L + 4], pool_b_d[L].rearrange("(c p) -> p c", p=128))
        P.dma("sp", pools[:, 4 * L:4 * L + 4], pool_s_d[L].rearrange("(c p) -> p c", p=128))
        for k in range(3):
            P.dma("sp", convw[:, 3 * L + k, :], conv_w_d[L, k].rearrange("(c p) -> p c", p=128))
        P.dma("sp", convb[:, L, :], conv_b_d[L].rearrange("(c p) -> p c", p=128))
        P.dma("sp", bgate[:, L, :], b_gate_d[L].rearrange("(c p) -> p c", p=128))
    P.dma("sp", gains[:, 3 * NL, :], gfin_d[0].rearrange("(c p) -> p c", p=128))
    P.dma("sp", Esb[0:NL * 8, 0:256], relb_d[:, 1:257])
    P.ts("dve", Esb[0:NL * 8, 256:768], zero[0:NL * 8, 0:512], Esb[0:NL * 8, 255:256], None, ALU.add)
    for L in range(NL):
        P.dma("sp", E_d[L * 8:(L + 1) * 8], Esb[L * 8:(L + 1) * 8, :].unsqueeze(1).broadcast_to([8, 128, 768]))

    def load_h(b, slot):
        for c in range(NCH):
            P.dma("sp", hblks[slot][:, c, :], hT_d[c, :, b * T:(b + 1) * T])

    def store_h(b):
        for c in range(NCH):
            P.dma("sp", hT_d[c, :, b * T:(b + 1) * T], H[0][:, c, :])

    def wview(wd, L):
        return wd[L].rearrange("(c p) n -> p c n", p=128)

    def load_w_AC(Lc, La):
        if Lc is not None:
            for n0 in (0, 512):
                P.dma("pool", w_gate[:, :, n0:n0 + 512], wview(w_gate_d, Lc)[:, :, n0:n0 + 512])
            P.dma("pool", w_ple[:, :, :], wview(w_ple_d, Lc))
        if La is not None:
            for n0 in range(0, 2048, 512):
                P.dma("pool", w_in[:, :, n0:n0 + 512], wview(w_in_d, La)[:, :, n0:n0 + 512])
            P.dma("pool", pool_w[:, :, :], pool_w_d[La].rearrange("g c d -> c g d"))
            for n0 in (0, 512):
                P.dma("pool", w_out[:, :, n0:n0 + 512], wview(w_out_d, La)[:, :, n0:n0 + 512])
            src = bass.AP(E_t, (La * 8) * 128 * 768 + 127, [[767, 128], [128 * 768, 8], [1, 640]])
            P.dma("sp", Mb[:, :, :], src)
            P.memset("pool", Mb[64:128, :, 0:64], NEG)
            P.memset("pool", Mb[0:64, :, 576:640], NEG)
            P.memset("pool", vaug[:, :, :, :, :], 1.0)

    def load_w_B(L):
        wv = wview(w_up_d, L)
        for f in range(NF):
            for gv in range(2):
                c0 = gv * DFF + f * 128
                P.dma("pool", w_up[:, f, gv, :, :], wv[:, :, c0:c0 + 128])
        wd = w_down_d[L].rearrange("(f p) n -> p f n", p=128)
        for n0 in range(0, D, 256):
            P.dma("pool", w_down[:, :, n0:n0 + 256], wd[:, :, n0:n0 + 256])

    def norm(gidx):
        for c in range(NCH):
            P.act(hb[:, c, :], H[0][:, c, :], AF.Square)
        bk = bank("aux")
        for c in range(NCH):
            P.mm(bk, ones_bf, hb[:, c, :], c == 0, c == NCH - 1)
        P.act(rstd, bk, AF.Ln, bias=eps_col)
        P.act(rstd, rstd, AF.Exp, scale=-0.5)
        return gidx

    def norm_apply_bf(gidx):
        for c in range(NCH):
            P.stt(hb[:, c, :], H[0][:, c, :], gains[:, gidx, c:c + 1], rstd, ALU.mult, ALU.mult)

    def proj(wt, col0, rhs_chunks, nk):
        bk = bank("mm")
        for c in range(nk):
            P.mm(bk, wt[:, c, col0:col0 + 128], rhs_chunks[:, c, :], c == 0, c == nk - 1)
        return bk

    def stage_init(b):
        P.dma("sp", iostage, x_d[b * T:(b + 1) * T, :].rearrange("(tt p) f -> p tt f", p=128))
        for c in range(NCH):
            bk = bank("mm")
            for tt in range(4):
                P.tr(bk[:, tt * 128:(tt + 1) * 128], iostage[:, tt, c * 128:(c + 1) * 128], ident)
            P.copy("act" if c % 2 else "dve", H[0][:, c, :], bk)

    def stage_A(L, b):
        norm(3 * L)
        norm_apply_bf(3 * L)
        cur = b % 2
        prv = 1 - cur
        for g in range(4):
            bk = proj(w_in, g * 128, hb, NCH)
            if b == 0:
                P.memset("pool", ubuf[:, g, 0:16], 0.0)
            else:
                P.copy("pool", ubuf[:, g, 0:16], ubuf[:, g, 512:528])
            P.copy("act", ubuf[:, g, 16:528], bk)
        for j in range(4):
            bk = proj(w_in, 512 + j * 128, hb, NCH)
            P.act(qT[:, j, :], bk, AF.Copy, scale=0.125)
        for j in range(4):
            bk = proj(w_in, 1024 + j * 128, hb, NCH)
            P.copy("dve", kT[:, j, cur * T:(cur + 1) * T], bk)
        for tt in range(4):
            bk = bank("mm")
            for c in range(NCH):
                P.mm(bk, hb[:, c, tt * 128:(tt + 1) * 128], w_in[:, c, 1536:2048], c == 0, c == NCH - 1)
            bv = bk.rearrange("p (j e d) -> p j e d", j=4, e=2)
            tile = cur * 4 + tt
            P.copy("act", vaug[:, tile, :, 0, 0:64], bv[:, :, 0, :])
            P.copy("dve", vaug[:, tile, :, 1, 64:128], bv[:, :, 1, :])
        for g in range(4):
            src = ubuf[:, g, :]
            prev = src
            for k in range(1, g + 2):
                sh = 1 << (k - 1)
                lo = (1 << k) - 1
                dst = ptmp[k % 2]
                P.tt("pool", dst[:, lo:528], prev[:, lo:528], prev[:, lo - sh:528 - sh], ALU.add)
                prev = dst
            w = 1 << (g + 1)
            yb = ybf[g]
            P.stt(yb, prev[:, 16:528], 1.0 / w, src[:, 16:528], ALU.mult, ALU.subtract)
            if b == 0:
                P.tt("dve", tmp16, prev[:, 16:32], invcnt[:, g, :], ALU.mult)
                P.tt("dve", yb[:, 0:16], tmp16, src[:, 16:32], ALU.subtract)
        steps = []
        for j in range(4):
            rs = [r for r in range(8) if 4 * b - 4 + r >= 0]
            for r in rs:
                steps.append((j, r, r == rs[0], r == rs[-1]))
        LA = 2
        st_pt = {}

        def geom(r):
            qlo = max(0, r - 4)
            qhi = min(3, r)
            half = prv if r < 4 else cur
            return qlo, qhi, qhi - qlo + 1, half

        def front(i):
            j, r, first, last = steps[i]
            qlo, qhi, nq, half = geom(r)
            n = nq * 128
            kc0 = half * T + (r % 4) * 128
            sp2 = bank_pair()
            for e in range(2):
                po = 64 * e
                P.mm(sp2[:, e, 0:n], kT[po:po + 64, j, kc0:kc0 + 128], qT[po:po + 64, j, qlo * 128:(qhi + 1) * 128], True, True)
            d0 = qlo - r + 4
            P.tt("dve", sp2[:, :, 0:n], sp2[:, :, 0:n], Mb[:, 2 * j:2 * j + 2, d0 * 128:(d0 + nq) * 128], ALU.add)
            pt = pTs[i % NPT]
            P.act(pt[:, :, 0:n], sp2[:, :, 0:n], AF.Exp)
            st_pt[i] = pt

        def back(i):
            j, r, first, last = steps[i]
            qlo, qhi, nq, half = geom(r)
            n = nq * 128
            vt = half * 4 + (r % 4)
            pt = st_pt[i]
            for e in range(2):
                ob = psb[6 + e]
                P.mm(ob[:, qlo * 128:(qhi + 1) * 128], vaug[:, vt, j, e, :], pt[:, e, 0:n], first, last,
                     skip_group_check=True)
            if last:
                for e in range(2):
                    ob = psb[6 + e]
                    po = 64 * e
                    so = 64 - po
                    rec = recb[e]
                    P.act(rec[so:so + 64, :], ob[so:so + 64, :], AF.Ln)
                    P.act(rec[so:so + 64, :], rec[so:so + 64, :], AF.Exp, scale=-1.0)
                    P.tt("dve", cat[po:po + 64, 4 + j, :], ob[po:po + 64, :], rec[so:so + 64, :], ALU.mult)

        for i in range(len(steps) + LA):
            if i < len(steps):
                front(i)
            if i >= LA:
                back(i - LA)
        for g in range(4):
            bk = bank("mm")
            P.mm(bk, pool_w[:, g, :], ybf[g], True, True)
            P.ts("dve", cat[:, g, :], bk, poolb[:, 4 * L + g:4 * L + g + 1], pools[:, 4 * L + g:4 * L + g + 1], ALU.add, ALU.mult)
        for m in range(NCH):
            bk = proj(w_out, m * 128, cat, NCH)
            P.tt("dve", H[0][:, m, :], bk, H[0][:, m, :], ALU.add)

    def stage_B(L, b):
        norm(3 * L + 1)
        norm_apply_bf(3 * L + 1)
        for hf in range(2):
            for fi in range(NFH):
                f = hf * NFH + fi
                bg = bank("g")
                for c in range(NCH):
                    P.mm(bg, w_up[:, f, 0, c, :], hb[:, c, :], c == 0, c == NCH - 1)
                bv = bank("v")
                for c in range(NCH):
                    P.mm(bv, w_up[:, f, 1, c, :], hb[:, c, :], c == 0, c == NCH - 1)
                gb = gbuf[f % 2]
                if b == 0:
                    P.memset("pool", gb[:, 0:2], 0.0)
                else:
                    P.copy("pool", gb[:, 0:2], ghalo[:, f, :])
                P.copy("act", gb[:, 2:514], bg)
                P.copy("pool", ghalo[:, f, :], gb[:, 512:514])
                acc = accb[f % 3]
                P.act(acc, gb[:, 0:512], AF.Identity, bias=convb[:, L, f:f + 1], scale=convw[:, 3 * L + 0, f:f + 1])
                P.stt(acc, gb[:, 1:513], convw[:, 3 * L + 1, f:f + 1], acc, ALU.mult, ALU.add)
                P.stt(acc, gb[:, 2:514], convw[:, 3 * L + 2, f:f + 1], acc, ALU.mult, ALU.add)
                P.act(acc, acc, AF.Gelu)
                P.tt("dve", actb[:, fi, :], acc, bv, ALU.mult)
            for m in range(NCH):
                bk = bank("mm")
                for fi in range(NFH):
                    f = hf * NFH + fi
                    P.mm(bk, w_down[:, f, m * 128:(m + 1) * 128], actb[:, fi, :], fi == 0, fi == NFH - 1)
                P.tt("dve", H[0][:, m, :], bk, H[0][:, m, :], ALU.add)

    def stage_C(L, b):
        P.dma("sp", pstage, p_d[L, b * T:(b + 1) * T, :].rearrange("(tt p) k -> p tt k", p=128))
        norm(3 * L + 2)
        norm_apply_bf(3 * L + 2)
        for kc in range(2):
            bk = bank("aux")
            for tt in range(4):
                P.tr(bk[:, tt * 128:(tt + 1) * 128], pstage[:, tt, kc * 128:(kc + 1) * 128], ident)
            P.copy("act", pT[:, kc, :], bk)
        for m in range(NCH):
            bg = proj(w_gate, m * 128, hb, NCH)
            gt = rr(gateb)
            P.act(gt, bg, AF.Sigmoid, bias=bgate[:, L, m:m + 1])
            bp = bank("mm")
            for kc in range(2):
                P.mm(bp, w_ple[:, kc, m * 128:(m + 1) * 128], pT[:, kc, :], kc == 0, kc == 1)
            P.tt("dve", gt, gt, bp, ALU.mult)
            P.tt("pool", H[0][:, m, :], gt, H[0][:, m, :], ALU.add)

    def stage_final(b):
        norm(3 * NL)
        for c in range(NCH):
            P.stt(H[0][:, c, :], H[0][:, c, :], gains[:, 3 * NL, c:c + 1], rstd, ALU.mult, ALU.mult)
        for tt in range(4):
            for hf in range(2):
                bk = bank("mm")
                for i in range(4):
                    c = hf * 4 + i
                    P.tr(bk[:, i * 128:(i + 1) * 128], H[0][:, c, tt * 128:(tt + 1) * 128], ident)
                P.copy("act" if hf else "dve", iostage[:, tt, hf * 512:(hf + 1) * 512], bk)
        return P.dma("sp", out_d[b * T:(b + 1) * T, :].rearrange("(tt p) f -> p tt f", p=128), iostage)

    finals = []
    for b in range(NB):
        H[0] = hblks[b % 2]
        stage_init(b)
        store_h(b)
    sweeps = []
    for L in range(NL):
        sweeps.append(("A", L))
        sweeps.append(("B", L))
    sweeps.append(("F", NL))
    items = [(si, b) for si in range(len(sweeps)) for b in range(NB)]
    load_h(0, 0)
    for idx, (si, b) in enumerate(items):
        kind, L = sweeps[si]
        slot = idx % 2
        if b == 0:
            P.new_epoch()
            if kind == "A":
                load_w_AC(L - 1 if L > 0 else None, L)
            elif kind == "B":
                load_w_B(L)
            else:
                load_w_AC(NL - 1, None)
        if idx + 1 < len(items):
            load_h(items[idx + 1][1], 1 - slot)
        H[0] = hblks[slot]
        if kind == "A":
            if L > 0:
                stage_C(L - 1, b)
            stage_A(L, b)
            store_h(b)
        elif kind == "B":
            stage_B(L, b)
            store_h(b)
        else:
            stage_C(NL - 1, b)
            finals.append(stage_final(b))
    with nc.allow_non_contiguous_dma(reason="small one-time parameter vectors"):
        P.finalize_and_emit(finals)
    return nc


def make_consts():
    c = np.zeros((128, 200), np.float32)
    c[:, 192] = EPS
    c[:, 0:128] = np.eye(128, dtype=np.float32)
    for g in range(4):
        w = 1 << (g + 1)
        for t in range(16):
            c[:, 128 + g * 16 + t] = 1.0 / min(t + 1, w)
    return c


_NC_CACHE = {}


def run_cores(inputs, S, NL, n_cores):
    key = (S, NL)
    if key not in _NC_CACHE:
        _NC_CACHE[key] = build(S, NL)
    nc = _NC_CACHE[key]
    consts = make_consts()
    in_maps = []
    f = lambda a: np.ascontiguousarray(np.asarray(a, dtype=np.float32))
    for i in range(n_cores):
        m = {
            "x": f(inputs["x"][i]),
            "p": f(inputs["p"][:, i]),
            "norm_mix_g": f(inputs["norm_mix_g"]),
            "w_in": f(inputs["w_in"]),
            "pool_w": f(inputs["pool_w"]),
            "pool_b": f(inputs["pool_b"]),
            "pool_scale": f(inputs["pool_scale"]),
            "rel_bias": f(inputs["rel_bias"]).reshape(NL * 8, 257),
            "w_out": f(inputs["w_out"]),
            "norm_ffn_g": f(inputs["norm_ffn_g"]),
            "w_up": f(inputs["w_up"]),
            "conv_w": f(inputs["conv_w"]),
            "conv_b": f(inputs["conv_b"]),
            "w_down": f(inputs["w_down"]),
            "norm_ple_g": f(inputs["norm_ple_g"]),
            "w_ple_gate": f(inputs["w_ple_gate"]),
            "b_ple_gate": f(inputs["b_ple_gate"]),
            "w_ple": f(inputs["w_ple"]),
            "final_g": f(inputs["final_g"]).reshape(1, D),
            "consts": consts,
        }
        in_maps.append(m)
    res = run_bass_kernel_spmd(nc, in_maps, core_ids=list(range(n_cores)))
    return np.stack([np.asarray(r["out"]) for r in res.results], axis=0)


def kernel(**inputs):
    out = run_cores(inputs, SEQ, DEPTH, 8)
    return out.astype(np.float32)
```

```python
import numpy as np
import concourse.bass as bass
import concourse.mybir as mybir
from concourse.bass_utils import run_bass_kernel_spmd

F32 = mybir.dt.float32
BF16 = mybir.dt.bfloat16
U8 = mybir.dt.uint8
ALU = mybir.AluOpType
AF = mybir.ActivationFunctionType

D = 1024
NCH = 8
T = 512
DFF = 2816
NF = 22
PLE = 256
NEG = -30000.0
EPS = 1e-6
SEQ = 4096
DEPTH = 4
NDMA_RING = 8


def _esz(dt):
    if dt == F32:
        return 4
    if dt == BF16:
        return 2
    if dt == U8:
        return 1
    raise ValueError(dt)


class Op:
    __slots__ = ("eng", "emit", "deps", "signal", "epoch", "value", "is_dma", "dsem", "dval", "ring_wait")

    def __init__(self, eng, emit, epoch):
        self.eng = eng
        self.emit = emit
        self.deps = set()
        self.signal = False
        self.epoch = epoch
        self.value = None
        self.is_dma = False
        self.dsem = None
        self.dval = None
        self.ring_wait = None


class Prog:
    ENGS = ("pe", "act", "dve", "pool", "sp")
    PAGE = 4096

    def __init__(self, nc):
        self.nc = nc
        self.ops = {e: [] for e in self.ENGS}
        self.recs = {}
        self.epoch = 0
        self.dma_n = {e: 0 for e in self.ENGS}
        self.dma_last = {}

    @staticmethod
    def rng(ap):
        sp = str(ap.space)
        pat = ap.ap
        esz = _esz(ap.dtype)
        off = int(ap.offset)
        if "DRAM" in sp:
            lo = off
            hi = off
            for (s, c) in pat:
                ext = (c - 1) * s
                if ext < 0:
                    lo += ext
                else:
                    hi += ext
            return ("D:" + ap.tensor.name, lo * esz, (hi + 1) * esz)
        pstride = pat[0][0]
        fo = off % pstride if pstride > 0 else off
        lo = fo
        hi = fo
        for (s, c) in pat[1:]:
            ext = (c - 1) * s
            if ext < 0:
                lo += ext
            else:
                hi += ext
        name = "SB" if "SB" in sp else ("P:" + ap.tensor.name)
        return (name, lo * esz, (hi + 1) * esz)

    def _pages(self, lo, hi):
        return range(lo // self.PAGE, (hi - 1) // self.PAGE + 1)

    def _access(self, op, ap, kind):
        space, lo, hi = self.rng(ap)
        pages = self.recs.setdefault(space, {})
        for pg in self._pages(lo, hi):
            lst = pages.get(pg)
            if not lst:
                continue
            for rec in lst:
                if rec[0] < hi and lo < rec[1]:
                    if kind == "W" or rec[2] == "W":
                        if rec[3] is not op:
                            op.deps.add(rec[3])
        return space, lo, hi

    def _record(self, op, space, lo, hi, kind):
        pages = self.recs[space]
        for pg in self._pages(lo, hi):
            lst = pages.setdefault(pg, [])
            plo = max(lo, pg * self.PAGE)
            phi = min(hi, (pg + 1) * self.PAGE)
            if kind == "W":
                lst[:] = [r for r in lst if not (plo <= max(r[0], pg * self.PAGE) and min(r[1], (pg + 1) * self.PAGE) <= phi)]
                lst.append([lo, hi, "W", op])
            else:
                done = False
                if not op.is_dma:
                    for r in lst:
                        if r[2] == "R" and r[0] == lo and r[1] == hi and r[3].eng == op.eng and not r[3].is_dma:
                            r[3] = op
                            done = True
                            break
                if not done:
                    lst.append([lo, hi, "R", op])

    def op(self, eng, emit, reads=(), writes=()):
        o = Op(eng, emit, self.epoch)
        acc = []
        for ap in reads:
            if ap is None or isinstance(ap, (int, float)):
                continue
            acc.append(self._access(o, ap, "R") + ("R",))
        for ap in writes:
            acc.append(self._access(o, ap, "W") + ("W",))
        for (space, lo, hi, kind) in acc:
            self._record(o, space, lo, hi, kind)
        self.ops[eng].append(o)
        return o

    def dma(self, eng, out, in_, **kw):
        def emit(e):
            return e.dma_start(out=out, in_=in_, **kw)
        o = Op(eng, emit, self.epoch)
        o.is_dma = True
        n = self.dma_n[eng]
        self.dma_n[eng] = n + 1
        ring = n % NDMA_RING
        o.dsem = (eng, ring)
        o.dval = 16 * (n // NDMA_RING + 1)
        prev = self.dma_last.get((eng, ring))
        o.ring_wait = prev
        self.dma_last[(eng, ring)] = o
        a1 = self._access(o, in_, "R") + ("R",)
        a2 = self._access(o, out, "W") + ("W",)
        self._record(o, *a1)
        self._record(o, *a2)
        self.ops[eng].append(o)
        return o

    def new_epoch(self):
        self.epoch += 1

    def finalize_and_emit(self, final_wait_ops):
        nc = self.nc
        for e in self.ENGS:
            for o in self.ops[e]:
                drop = []
                for d in o.deps:
                    if d.is_dma:
                        continue
                    if d.eng == o.eng and not o.is_dma:
                        if o.eng in ("pe", "sp"):
                            drop.append(d)
                            continue
                    d.signal = True
                for d in drop:
                    o.deps.discard(d)
        for o in final_wait_ops:
            if not o.is_dma:
                o.signal = True
        nep = self.epoch + 1
        for e in self.ENGS:
            cnt = [0] * nep
            for o in self.ops[e]:
                if o.is_dma:
                    continue
                if o.signal:
                    cnt[o.epoch] += 1
                    o.value = cnt[o.epoch]
        import contextlib
        with contextlib.ExitStack() as st:
            esem = {}
            for e in ("pe", "act", "dve", "pool"):
                for ep in range(nep):
                    esem[(e, ep)] = st.enter_context(nc.semaphore(f"s_{e}_{ep}"))
            dsem = {}
            for e in ("sp", "pool", "act"):
                if self.dma_n[e] == 0:
                    continue
                for r in range(NDMA_RING):
                    dsem[(e, r)] = st.enter_context(nc.semaphore(f"d_{e}_{r}"))
            block = st.enter_context(nc.Block())

            def run(eng_name, eng):
                waited = {}

                def wait(sem, val):
                    if waited.get(id(sem), 0) >= val:
                        return
                    waited[id(sem)] = val
                    eng.wait_ge(sem, val)

                for o in self.ops[eng_name]:
                    if o.is_dma and o.ring_wait is not None:
                        wait(dsem[o.ring_wait.dsem], o.ring_wait.dval)
                    for d in o.deps:
                        if d.is_dma:
                            wait(dsem[d.dsem], d.dval)
                        else:
                            wait(esem[(d.eng, d.epoch)], d.value)
                    ins = o.emit(eng)
                    if o.is_dma:
                        ins.then_inc(dsem[o.dsem], 16)
                    elif o.signal:
                        ins.then_inc(esem[(o.eng, o.epoch)], 1)
                if eng_name == "sp":
                    for o in final_wait_ops:
                        if o.is_dma:
                            wait(dsem[o.dsem], o.dval)
                        else:
                            wait(esem[(o.eng, o.epoch)], o.value)

            @block.tensor
            def _(eng):
                run("pe", eng)

            @block.scalar
            def _(eng):
                run("act", eng)

            @block.vector
            def _(eng):
                run("dve", eng)

            @block.gpsimd
            def _(eng):
                run("pool", eng)

            @block.sync
            def _(eng):
                run("sp", eng)

    def mm(self, out, lhsT, rhs, start, stop, **kw):
        return self.op("pe", lambda e: e.matmul(out, lhsT=lhsT, rhs=rhs, start=start, stop=stop, **kw),
                       reads=(lhsT, rhs), writes=(out,))

    def tr(self, out, in_, ident):
        return self.op("pe", lambda e: e.transpose(out, in_, ident), reads=(in_, ident), writes=(out,))

    def act(self, out, in_, func, bias=None, scale=None):
        kw = {}
        if bias is not None:
            kw["bias"] = bias
        if scale is not None:
            kw["scale"] = scale
        return self.op("act", lambda e: e.activation(out=out, in_=in_, func=func, **kw),
                       reads=(in_, bias, scale), writes=(out,))

    def tt(self, eng, out, in0, in1, op):
        return self.op(eng, lambda e: e.tensor_tensor(out=out, in0=in0, in1=in1, op=op),
                       reads=(in0, in1), writes=(out,))

    def ts(self, eng, out, in0, s1, s2, op0, op1=None):
        if op1 is None:
            return self.op(eng, lambda e: e.tensor_single_scalar(out=out, in_=in0, scalar=s1, op=op0),
                           reads=(in0, s1), writes=(out,))
        return self.op(eng, lambda e: e.tensor_scalar(out=out, in0=in0, scalar1=s1, scalar2=s2, op0=op0, op1=op1),
                       reads=(in0, s1, s2), writes=(out,))

    def stt(self, out, in0, scalar, in1, op0, op1):
        return self.op("dve", lambda e: e.scalar_tensor_tensor(out=out, in0=in0, scalar=scalar, in1=in1, op0=op0, op1=op1),
                       reads=(in0, scalar, in1), writes=(out,))

    def copy(self, eng, out, in_):
        if eng == "act":
            return self.act(out, in_, AF.Copy)
        return self.op(eng, lambda e: e.tensor_copy(out=out, in_=in_), reads=(in_,), writes=(out,))

    def memset(self, eng, out, val):
        return self.op(eng, lambda e: e.memset(out, val), writes=(out,))

    def recip(self, out, in_):
        return self.op("dve", lambda e: e.reciprocal(out=out, in_=in_), reads=(in_,), writes=(out,))


class Arena:
    def __init__(self, nc, nbytes):
        self.t = nc.alloc_sbuf_tensor("arena", [128, nbytes], U8)
        self.nbytes = nbytes

    def view(self, off, shape, dt):
        n = 1
        for s in shape[1:]:
            n *= s
        nb = n * _esz(dt)
        assert off % 4 == 0 and off + nb <= self.nbytes, (off, nb, self.nbytes)
        v = self.t[:, off:off + nb].bitcast(dt)
        if len(shape) == 2:
            return v
        if len(shape) == 3:
            return v.rearrange("p (a b) -> p a b", a=shape[1])
        if len(shape) == 4:
            return v.rearrange("p (a b c) -> p a b c", a=shape[1], b=shape[2])
        if len(shape) == 5:
            return v.rearrange("p (a b c d) -> p a b c d", a=shape[1], b=shape[2], c=shape[3])
        raise ValueError(shape)


def build(S=SEQ, NL=DEPTH):
    NB = S // T
    nc = bass.Bass("TRN2", target_bir_lowering=False)
    P = Prog(nc)

    def din(name, shape):
        return nc.dram_tensor(name, list(shape), F32, kind="ExternalInput").ap()

    x_d = din("x", [S, D])
    p_d = din("p", [NL, S, PLE])
    gmix_d = din("norm_mix_g", [NL, D])
    w_in_d = din("w_in", [NL, D, 2048])
    pool_w_d = din("pool_w", [NL, 4, 128, 128])
    pool_b_d = din("pool_b", [NL, 512])
    pool_s_d = din("pool_scale", [NL, 512])
    relb_d = din("rel_bias", [NL * 8, 257])
    w_out_d = din("w_out", [NL, D, D])
    gffn_d = din("norm_ffn_g", [NL, D])
    w_up_d = din("w_up", [NL, D, 2 * DFF])
    conv_w_d = din("conv_w", [NL, 3, DFF])
    conv_b_d = din("conv_b", [NL, DFF])
    w_down_d = din("w_down", [NL, DFF, D])
    gple_d = din("norm_ple_g", [NL, D])
    w_gate_d = din("w_ple_gate", [NL, D, D])
    b_gate_d = din("b_ple_gate", [NL, D])
    w_ple_d = din("w_ple", [NL, PLE, D])
    gfin_d = din("final_g", [1, D])
    consts_d = din("consts", [128, 200])
    out_d = nc.dram_tensor("out", [S, D], F32, kind="ExternalOutput").ap()
    hT_d = nc.dram_tensor("hT_scr", [NCH, 128, S], F32, kind="Internal").ap()
    E_t = nc.dram_tensor("E_scr", [NL * 8, 128, 768], F32, kind="Internal")
    E_d = E_t.ap()

    off = 0

    def take(n):
        nonlocal off
        o = off
        off += (n + 31) // 32 * 32
        return o

    o_hblk = take(NCH * T * 4)
    o_hblk1 = take(NCH * T * 4)
    o_hb = take(NCH * T * 2)
    o_rstd = take(T * 4)
    o_consts = take(200 * 4)
    o_ones = take(128 * 2)
    o_gains = take((3 * NL + 1) * 8 * 4)
    o_poolb = take(NL * 4 * 4)
    o_pools = take(NL * 4 * 4)
    o_convw = take(NL * 3 * NF * 4)
    o_convb = take(NL * NF * 4)
    o_bgate = take(NL * 8 * 4)
    o_ghalo = take(NF * 2 * 4)
    o_W = take(135168)
    o_X = take(29184)
    o_zero = o_X
    total = off
    print("SBUF arena bytes/partition:", total)
    assert total <= 208 * 1024, total
    A = Arena(nc, total)

    hblks = [A.view(o_hblk, [128, NCH, T], F32), A.view(o_hblk1, [128, NCH, T], F32)]
    H = [hblks[0]]
    hb = A.view(o_hb, [128, NCH, T], BF16)
    rstd = A.view(o_rstd, [128, T], F32)
    consts = A.view(o_consts, [128, 200], F32)
    eps_col = consts[:, 192:193]
    ident = consts[:, 0:128]
    invcnt = consts[:, 128:192].rearrange("p (g t) -> p g t", g=4)
    ones_bf = A.view(o_ones, [128, 128], BF16)
    gains = A.view(o_gains, [128, 3 * NL + 1, 8], F32)
    poolb = A.view(o_poolb, [128, NL * 4], F32)
    pools = A.view(o_pools, [128, NL * 4], F32)
    convw = A.view(o_convw, [128, NL * 3, NF], F32)
    convb = A.view(o_convb, [128, NL, NF], F32)
    bgate = A.view(o_bgate, [128, NL, 8], F32)
    ghalo = A.view(o_ghalo, [128, NF, 2], F32)
    zero = A.view(o_zero, [128, T], F32)

    w_up = A.view(o_W, [128, NF, 2, NCH, 128], BF16)
    w_down = A.view(o_W + 90112, [128, NF, D], BF16)
    w_in = A.view(o_W, [128, NCH, 2048], BF16)
    w_out = A.view(o_W + 32768, [128, NCH, D], BF16)
    w_gate = A.view(o_W + 49152, [128, NCH, D], BF16)
    w_ple = A.view(o_W + 65536, [128, 2, D], BF16)
    pool_w = A.view(o_W + 69632, [128, 4, 128], BF16)
    oM = o_W + 70656
    Mb = A.view(oM, [128, 8, 640], F32)
    vaug = A.view(oM + 20480, [128, 8, 4, 2, 128], BF16)
    kT = A.view(oM + 36864, [128, 4, 2 * T], BF16)
    qT = A.view(oM + 45056, [128, 4, T], BF16)
    cat = A.view(oM + 49152, [128, NCH, T], BF16)
    pstage = A.view(oM + 57344, [128, 4, PLE], F32)
    pT = A.view(oM + 61440, [128, 2, T], BF16)
    iostage = A.view(oM, [128, 4, D], F32)
    Esb = A.view(oM + 20480, [32, 768], F32)
    NFH = NF // 2
    actb = A.view(o_X, [128, NFH, T], BF16)
    gbuf = [A.view(o_X + 11264 + i * 2080, [128, 520], F32) for i in range(2)]
    accb = [A.view(o_X + 15424 + i * 2048, [128, T], F32) for i in range(3)]
    ubuf = A.view(o_X, [128, 4, 528], F32)
    ptmp = [A.view(o_X + 8448 + i * 2112, [128, 528], F32) for i in range(2)]
    ybf = [A.view(o_X + 12672 + i * 1024, [128, T], BF16) for i in range(4)]
    NPT = 4
    pTs = [A.view(o_X + 16768 + i * 2048, [128, 2, T], BF16) for i in range(NPT)]
    gateb = [A.view(o_X + 24960 + i * 2048, [128, T], F32) for i in range(2)]
    recb = gateb
    tmp16 = A.view(o_X + 29056, [128, 16], F32)

    psp = [nc.alloc_psum_tensor(f"psp{i}", [128, 2, T], F32)[:, :, :] for i in range(4)]
    psb = [psp[i // 2][:, i % 2, :] for i in range(8)]
    pools_ps = {"mm": [0, 1, 2, 3], "o": [6, 7], "aux": [0, 1, 2, 3], "g": [2, 3, 4], "v": [5, 6, 7]}
    scp_ctr = [0]

    def bank_pair():
        i = scp_ctr[0]
        scp_ctr[0] = i + 1
        return psp[i % 3]
    ps_ctr = {k: 0 for k in pools_ps}

    def bank(pool):
        lst = pools_ps[pool]
        if pool == "aux":
            pool = "mm"
        i = ps_ctr[pool]
        ps_ctr[pool] = i + 1
        return psb[lst[i % len(lst)]]

    ctr = {"i": 0}

    def rr(lst):
        ctr["i"] += 1
        return lst[ctr["i"] % len(lst)]

    P.dma("sp", consts, consts_d)
    P.memset("pool", ones_bf, 1.0 / 1024.0)
    P.memset("pool", zero, 0.0)
    for L in range(NL):
        P.dma("sp", gains[:, 3 * L + 0, :], gmix_d[L].rearrange("(c p) -> p c", p=128))
        P.dma("sp", gains[:, 3 * L + 1, :], gffn_d[L].rearrange("(c p) -> p c", p=128))
        P.dma("sp", gains[:, 3 * L + 2, :], gple_d[L].rearrange("(c p) -> p c", p=128))
        P.dma("sp", poolb[:, 4 * L:4 * L + 4], pool_b_d[L].rearrange("(c p) -> p c", p=128))
        P.dma("sp", pools[:, 4 * L:4 * L + 4], pool_s_d[L].rearrange("(c p) -> p c", p=128))
        for k in range(3):
            P.dma("sp", convw[:, 3 * L + k, :], conv_w_d[L, k].rearrange("(c p) -> p c", p=128))
        P.dma("sp", convb[:, L, :], conv_b_d[L].rearrange("(c p) -> p c", p=128))
        P.dma("sp", bgate[:, L, :], b_gate_d[L].rearrange("(c p) -> p c", p=128))
    P.dma("sp", gains[:, 3 * NL, :], gfin_d[0].rearrange("(c p) -> p c", p=128))
    P.dma("sp", Esb[0:NL * 8, 0:256], relb_d[:, 1:257])
    P.ts("dve", Esb[0:NL * 8, 256:768], zero[0:NL * 8, 0:512], Esb[0:NL * 8, 255:256], None, ALU.add)
    for L in range(NL):
        P.dma("sp", E_d[L * 8:(L + 1) * 8], Esb[L * 8:(L + 1) * 8, :].unsqueeze(1).broadcast_to([8, 128, 768]))

    def load_h(b, slot):
        for c in range(NCH):
            P.dma("sp", hblks[slot][:, c, :], hT_d[c, :, b * T:(b + 1) * T])

    def store_h(b):
        for c in range(NCH):
            P.dma("sp", hT_d[c, :, b * T:(b + 1) * T], H[0][:, c, :])

    def wview(wd, L):
        return wd[L].rearrange("(c p) n -> p c n", p=128)

    def load_w_AC(Lc, La):
        if Lc is not None:
            for n0 in (0, 512):
                P.dma("pool", w_gate[:, :, n0:n0 + 512], wview(w_gate_d, Lc)[:, :, n0:n0 + 512])
            P.dma("pool", w_ple[:, :, :], wview(w_ple_d, Lc))
        if La is not None:
            for n0 in range(0, 2048, 512):
                P.dma("pool", w_in[:, :, n0:n0 + 512], wview(w_in_d, La)[:, :, n0:n0 + 512])
            P.dma("pool", pool_w[:, :, :], pool_w_d[La].rearrange("g c d -> c g d"))
            for n0 in (0, 512):
                P.dma("pool", w_out[:, :, n0:n0 + 512], wview(w_out_d, La)[:, :, n0:n0 + 512])
            src = bass.AP(E_t, (La * 8) * 128 * 768 + 127, [[767, 128], [128 * 768, 8], [1, 640]])
            P.dma("sp", Mb[:, :, :], src)
            P.memset("pool", Mb[64:128, :, 0:64], NEG)
            P.memset("pool", Mb[0:64, :, 576:640], NEG)
            P.memset("pool", vaug[:, :, :, :, :], 1.0)

    def load_w_B(L):
        wv = wview(w_up_d, L)
        for f in range(NF):
            for gv in range(2):
                c0 = gv * DFF + f * 128
                P.dma("pool", w_up[:, f, gv, :, :], wv[:, :, c0:c0 + 128])
        wd = w_down_d[L].rearrange("(f p) n -> p f n", p=128)
        for n0 in range(0, D, 256):
            P.dma("pool", w_down[:, :, n0:n0 + 256], wd[:, :, n0:n0 + 256])

    def norm(gidx):
        for c in range(NCH):
            P.act(hb[:, c, :], H[0][:, c, :], AF.Square)
        bk = bank("aux")
        for c in range(NCH):
            P.mm(bk, ones_bf, hb[:, c, :], c == 0, c == NCH - 1)
        P.act(rstd, bk, AF.Ln, bias=eps_col)
        P.act(rstd, rstd, AF.Exp, scale=-0.5)
        return gidx

    def norm_apply_bf(gidx):
        for c in range(NCH):
            P.stt(hb[:, c, :], H[0][:, c, :], gains[:, gidx, c:c + 1], rstd, ALU.mult, ALU.mult)

    def proj(wt, col0, rhs_chunks, nk):
        bk = bank("mm")
        for c in range(nk):
            P.mm(bk, wt[:, c, col0:col0 + 128], rhs_chunks[:, c, :], c == 0, c == nk - 1)
        return bk

    def stage_init(b):
        P.dma("sp", iostage, x_d[b * T:(b + 1) * T, :].rearrange("(tt p) f -> p tt f", p=128))
        for c in range(NCH):
            bk = bank("mm")
            for tt in range(4):
                P.tr(bk[:, tt * 128:(tt + 1) * 128], iostage[:, tt, c * 128:(c + 1) * 128], ident)
            P.copy("act" if c % 2 else "dve", H[0][:, c, :], bk)

    def stage_A(L, b, do_pre=True, hook=None):
        if do_pre:
            norm(3 * L)
            norm_apply_bf(3 * L)
        cur = b % 2
        prv = 1 - cur
        for g in range(4):
            bk = proj(w_in, g * 128, hb, NCH)
            if b == 0:
                P.memset("pool", ubuf[:, g, 0:16], 0.0)
            else:
                P.copy("pool", ubuf[:, g, 0:16], ubuf[:, g, 512:528])
            P.copy("act", ubuf[:, g, 16:528], bk)
        for j in range(4):
            bk = proj(w_in, 512 + j * 128, hb, NCH)
            P.act(qT[:, j, :], bk, AF.Copy, scale=0.125)
        for j in range(4):
            bk = proj(w_in, 1024 + j * 128, hb, NCH)
            P.copy("dve", kT[:, j, cur * T:(cur + 1) * T], bk)
        for tt in range(4):
            bk = bank("mm")
            for c in range(NCH):
                P.mm(bk, hb[:, c, tt * 128:(tt + 1) * 128], w_in[:, c, 1536:2048], c == 0, c == NCH - 1)
            bv = bk.rearrange("p (j e d) -> p j e d", j=4, e=2)
            tile = cur * 4 + tt
            P.copy("act", vaug[:, tile, :, 0, 0:64], bv[:, :, 0, :])
            P.copy("dve", vaug[:, tile, :, 1, 64:128], bv[:, :, 1, :])
        for g in range(4):
            src = ubuf[:, g, :]
            prev = src
            for k in range(1, g + 2):
                sh = 1 << (k - 1)
                lo = (1 << k) - 1
                dst = ptmp[k % 2]
                P.tt("pool", dst[:, lo:528], prev[:, lo:528], prev[:, lo - sh:528 - sh], ALU.add)
                prev = dst
            w = 1 << (g + 1)
            yb = ybf[g]
            if b == 0:
                P.tt("pool", prev[:, 16:32], prev[:, 16:32], invcnt[:, g, :], ALU.mult)
                P.ts("pool", prev[:, 32:528], prev[:, 32:528], 1.0 / w, None, ALU.mult)
            else:
                P.ts("pool", prev[:, 16:528], prev[:, 16:528], 1.0 / w, None, ALU.mult)
            P.tt("pool", yb, prev[:, 16:528], src[:, 16:528], ALU.subtract)
        steps = []
        for j in range(4):
            rs = [r for r in range(8) if 4 * b - 4 + r >= 0]
            for r in rs:
                steps.append((j, r, r == rs[0], r == rs[-1]))
        LA = 2
        st_pt = {}

        def geom(r):
            qlo = max(0, r - 4)
            qhi = min(3, r)
            half = prv if r < 4 else cur
            return qlo, qhi, qhi - qlo + 1, half

        def front(i):
            j, r, first, last = steps[i]
            qlo, qhi, nq, half = geom(r)
            n = nq * 128
            kc0 = half * T + (r % 4) * 128
            sp2 = bank_pair()
            for e in range(2):
                po = 64 * e
                P.mm(sp2[:, e, 0:n], kT[po:po + 64, j, kc0:kc0 + 128], qT[po:po + 64, j, qlo * 128:(qhi + 1) * 128], True, True)
            d0 = qlo - r + 4
            P.tt("dve", sp2[:, :, 0:n], sp2[:, :, 0:n], Mb[:, 2 * j:2 * j + 2, d0 * 128:(d0 + nq) * 128], ALU.add)
            pt = pTs[i % NPT]
            P.act(pt[:, :, 0:n], sp2[:, :, 0:n], AF.Exp)
            st_pt[i] = pt

        def back(i):
            j, r, first, last = steps[i]
            qlo, qhi, nq, half = geom(r)
            n = nq * 128
            vt = half * 4 + (r % 4)
            pt = st_pt[i]
            for e in range(2):
                ob = psb[6 + e]
                P.mm(ob[:, qlo * 128:(qhi + 1) * 128], vaug[:, vt, j, e, :], pt[:, e, 0:n], first, last,
                     skip_group_check=True)
            if last:
                for e in range(2):
                    ob = psb[6 + e]
                    po = 64 * e
                    so = 64 - po
                    rec = recb[e]
                    P.act(rec[so:so + 64, :], ob[so:so + 64, :], AF.Ln)
                    P.act(rec[so:so + 64, :], rec[so:so + 64, :], AF.Exp, scale=-1.0)
                    P.tt("dve", cat[po:po + 64, 4 + j, :], ob[po:po + 64, :], rec[so:so + 64, :], ALU.mult)

        for i in range(len(steps) + LA):
            if i >= LA:
                back(i - LA)
            if i < len(steps):
                front(i)
        for g in range(4):
            bk = bank("mm")
            P.mm(bk, pool_w[:, g, :], ybf[g], True, True)
            P.ts("dve", cat[:, g, :], bk, poolb[:, 4 * L + g:4 * L + g + 1], pools[:, 4 * L + g:4 * L + g + 1], ALU.add, ALU.mult)
        if hook is not None:
            hook()
        for m in range(NCH):
            bk = proj(w_out, m * 128, cat, NCH)
            P.tt("dve", H[0][:, m, :], bk, H[0][:, m, :], ALU.add)

    def stage_B(L, b, do_pre=True, hook=None):
        if do_pre:
            norm(3 * L + 1)
            norm_apply_bf(3 * L + 1)
        for hf in range(2):
            for fi in range(NFH):
                f = hf * NFH + fi
                bg = bank("g")
                for c in range(NCH):
                    P.mm(bg, w_up[:, f, 0, c, :], hb[:, c, :], c == 0, c == NCH - 1)
                bv = bank("v")
                for c in range(NCH):
                    P.mm(bv, w_up[:, f, 1, c, :], hb[:, c, :], c == 0, c == NCH - 1)
                gb = gbuf[f % 2]
                if b == 0:
                    P.memset("pool", gb[:, 0:2], 0.0)
                else:
                    P.copy("pool", gb[:, 0:2], ghalo[:, f, :])
                P.copy("act", gb[:, 2:514], bg)
                P.copy("pool", ghalo[:, f, :], gb[:, 512:514])
                acc = accb[f % 3]
                P.act(acc, gb[:, 0:512], AF.Identity, bias=convb[:, L, f:f + 1], scale=convw[:, 3 * L + 0, f:f + 1])
                P.stt(acc, gb[:, 1:513], convw[:, 3 * L + 1, f:f + 1], acc, ALU.mult, ALU.add)
                P.stt(acc, gb[:, 2:514], convw[:, 3 * L + 2, f:f + 1], acc, ALU.mult, ALU.add)
                P.act(acc, acc, AF.Gelu)
                P.tt("dve", actb[:, fi, :], acc, bv, ALU.mult)
            if hf == 1 and hook is not None:
                hook()
            for m in range(NCH):
                bk = bank("mm")
                for fi in range(NFH):
                    f = hf * NFH + fi
                    P.mm(bk, w_down[:, f, m * 128:(m + 1) * 128], actb[:, fi, :], fi == 0, fi == NFH - 1)
                P.tt("dve", H[0][:, m, :], bk, H[0][:, m, :], ALU.add)

    def stage_C(L, b, do_pre=True):
        P.dma("sp", pstage, p_d[L, b * T:(b + 1) * T, :].rearrange("(tt p) k -> p tt k", p=128))
        if do_pre:
            norm(3 * L + 2)
            norm_apply_bf(3 * L + 2)
        for kc in range(2):
            bk = bank("aux")
            for tt in range(4):
                P.tr(bk[:, tt * 128:(tt + 1) * 128], pstage[:, tt, kc * 128:(kc + 1) * 128], ident)
            P.copy("act", pT[:, kc, :], bk)
        for m in range(NCH):
            bg = proj(w_gate, m * 128, hb, NCH)
            gt = rr(gateb)
            P.act(gt, bg, AF.Sigmoid, bias=bgate[:, L, m:m + 1])
            bp = bank("mm")
            for kc in range(2):
                P.mm(bp, w_ple[:, kc, m * 128:(m + 1) * 128], pT[:, kc, :], kc == 0, kc == 1)
            P.tt("dve", gt, gt, bp, ALU.mult)
            P.tt("pool", H[0][:, m, :], gt, H[0][:, m, :], ALU.add)

    def stage_final(b, hook=None):
        norm(3 * NL)
        for c in range(NCH):
            P.stt(H[0][:, c, :], H[0][:, c, :], gains[:, 3 * NL, c:c + 1], rstd, ALU.mult, ALU.mult)
        if hook is not None:
            hook()
        for tt in range(4):
            for hf in range(2):
                bk = bank("mm")
                for i in range(4):
                    c = hf * 4 + i
                    P.tr(bk[:, i * 128:(i + 1) * 128], H[0][:, c, tt * 128:(tt + 1) * 128], ident)
                P.copy("act" if hf else "dve", iostage[:, tt, hf * 512:(hf + 1) * 512], bk)
        return P.dma("sp", out_d[b * T:(b + 1) * T, :].rearrange("(tt p) f -> p tt f", p=128), iostage)

    finals = []
    for b in range(NB):
        H[0] = hblks[b % 2]
        stage_init(b)
        store_h(b)
    sweeps = []
    for L in range(NL):
        sweeps.append(("A", L))
        sweeps.append(("B", L))
    sweeps.append(("F", NL))
    items = [(si, b) for si in range(len(sweeps)) for b in range(NB)]
    load_h(0, 0)
    for idx, (si, b) in enumerate(items):
        kind, L = sweeps[si]
        slot = idx % 2
        if b == 0:
            P.new_epoch()
            if kind == "A":
                load_w_AC(L - 1 if L > 0 else None, L)
            elif kind == "B":
                load_w_B(L)
            else:
                load_w_AC(NL - 1, None)
        if idx + 1 < len(items):
            load_h(items[idx + 1][1], 1 - slot)
        H[0] = hblks[slot]
        hook = None
        if idx + 1 < len(items):
            nkind, nL = sweeps[items[idx + 1][0]]
            if nkind == "A":
                ng = 3 * (nL - 1) + 2 if nL > 0 else 0
            elif nkind == "B":
                ng = 3 * nL + 1
            else:
                ng = 3 * (NL - 1) + 2

            def hook(ng=ng, nslot=1 - slot, cslot=slot):
                H[0] = hblks[nslot]
                norm(ng)
                norm_apply_bf(ng)
                H[0] = hblks[cslot]
        pre_done = idx > 0
        if kind == "A":
            if L > 0:
                stage_C(L - 1, b, do_pre=not pre_done)
                stage_A(L, b, True, hook)
            else:
                stage_A(L, b, not pre_done, hook)
            store_h(b)
        elif kind == "B":
            stage_B(L, b, not pre_done, hook)
            store_h(b)
        else:
            stage_C(NL - 1, b, do_pre=not pre_done)
            finals.append(stage_final(b, hook))
    with nc.allow_non_contiguous_dma(reason="small one-time parameter vectors"):
        P.finalize_and_emit(finals)
    return nc


def make_consts():
    c = np.zeros((128, 200), np.float32)
    c[:, 192] = EPS
    c[:, 0:128] = np.eye(128, dtype=np.float32)
    for g in range(4):
        w = 1 << (g + 1)
        for t in range(16):
            c[:, 128 + g * 16 + t] = 1.0 / min(t + 1, w)
    return c


_NC_CACHE = {}


def run_cores(inputs, S, NL, n_cores):
    key = (S, NL)
    if key not in _NC_CACHE:
        _NC_CACHE[key] = build(S, NL)
    nc = _NC_CACHE[key]
    consts = make_consts()
    in_maps = []
    f = lambda a: np.ascontiguousarray(np.asarray(a, dtype=np.float32))
    for i in range(n_cores):
        m = {
            "x": f(inputs["x"][i]),
            "p": f(inputs["p"][:, i]),
            "norm_mix_g": f(inputs["norm_mix_g"]),
            "w_in": f(inputs["w_in"]),
            "pool_w": f(inputs["pool_w"]),
            "pool_b": f(inputs["pool_b"]),
            "pool_scale": f(inputs["pool_scale"]),
            "rel_bias": f(inputs["rel_bias"]).reshape(NL * 8, 257),
            "w_out": f(inputs["w_out"]),
            "norm_ffn_g": f(inputs["norm_ffn_g"]),
            "w_up": f(inputs["w_up"]),
            "conv_w": f(inputs["conv_w"]),
            "conv_b": f(inputs["conv_b"]),
            "w_down": f(inputs["w_down"]),
            "norm_ple_g": f(inputs["norm_ple_g"]),
            "w_ple_gate": f(inputs["w_ple_gate"]),
            "b_ple_gate": f(inputs["b_ple_gate"]),
            "w_ple": f(inputs["w_ple"]),
            "final_g": f(inputs["final_g"]).reshape(1, D),
            "consts": consts,
        }
        in_maps.append(m)
    res = run_bass_kernel_spmd(nc, in_maps, core_ids=list(range(n_cores)))
    return np.stack([np.asarray(r["out"]) for r in res.results], axis=0)


def kernel(**inputs):
    out = run_cores(inputs, SEQ, DEPTH, 8)
    return out.astype(np.float32)
```

```python
import numpy as np
import concourse.bass as bass
import concourse.mybir as mybir
from concourse.bass_utils import run_bass_kernel_spmd

F32 = mybir.dt.float32
BF16 = mybir.dt.bfloat16
U8 = mybir.dt.uint8
ALU = mybir.AluOpType
AF = mybir.ActivationFunctionType

D = 1024
NCH = 8
T = 512
DFF = 2816
NF = 22
PLE = 256
NEG = -100.0
EPS = 1e-6
SEQ = 4096
DEPTH = 4
NDMA_RING = 8


def _esz(dt):
    if dt == F32:
        return 4
    if dt == BF16:
        return 2
    if dt == U8:
        return 1
    raise ValueError(dt)


class Op:
    __slots__ = ("eng", "emit", "deps", "signal", "epoch", "value", "is_dma", "dsem", "dval", "ring_wait")

    def __init__(self, eng, emit, epoch):
        self.eng = eng
        self.emit = emit
        self.deps = set()
        self.signal = False
        self.epoch = epoch
        self.value = None
        self.is_dma = False
        self.dsem = None
        self.dval = None
        self.ring_wait = None


class Prog:
    ENGS = ("pe", "act", "dve", "pool", "sp")
    PAGE = 4096

    def __init__(self, nc):
        self.nc = nc
        self.ops = {e: [] for e in self.ENGS}
        self.recs = {}
        self.epoch = 0
        self.dma_n = {e: 0 for e in self.ENGS}
        self.dma_last = {}

    @staticmethod
    def rng(ap):
        sp = str(ap.space)
        pat = ap.ap
        esz = _esz(ap.dtype)
        off = int(ap.offset)
        if "DRAM" in sp:
            lo = off
            hi = off
            for (s, c) in pat:
                ext = (c - 1) * s
                if ext < 0:
                    lo += ext
                else:
                    hi += ext
            return ("D:" + ap.tensor.name, lo * esz, (hi + 1) * esz)
        pstride = pat[0][0]
        fo = off % pstride if pstride > 0 else off
        lo = fo
        hi = fo
        for (s, c) in pat[1:]:
            ext = (c - 1) * s
            if ext < 0:
                lo += ext
            else:
                hi += ext
        name = "SB" if "SB" in sp else ("P:" + ap.tensor.name)
        return (name, lo * esz, (hi + 1) * esz)

    def _pages(self, lo, hi):
        return range(lo // self.PAGE, (hi - 1) // self.PAGE + 1)

    def _access(self, op, ap, kind):
        space, lo, hi = self.rng(ap)
        pages = self.recs.setdefault(space, {})
        for pg in self._pages(lo, hi):
            lst = pages.get(pg)
            if not lst:
                continue
            for rec in lst:
                if rec[0] < hi and lo < rec[1]:
                    if kind == "W" or rec[2] == "W":
                        if rec[3] is not op:
                            op.deps.add(rec[3])
        return space, lo, hi

    def _record(self, op, space, lo, hi, kind):
        pages = self.recs[space]
        for pg in self._pages(lo, hi):
            lst = pages.setdefault(pg, [])
            plo = max(lo, pg * self.PAGE)
            phi = min(hi, (pg + 1) * self.PAGE)
            if kind == "W":
                lst[:] = [r for r in lst if not (plo <= max(r[0], pg * self.PAGE) and min(r[1], (pg + 1) * self.PAGE) <= phi)]
                lst.append([lo, hi, "W", op])
            else:
                done = False
                if not op.is_dma:
                    for r in lst:
                        if r[2] == "R" and r[0] == lo and r[1] == hi and r[3].eng == op.eng and not r[3].is_dma:
                            r[3] = op
                            done = True
                            break
                if not done:
                    lst.append([lo, hi, "R", op])

    def op(self, eng, emit, reads=(), writes=()):
        o = Op(eng, emit, self.epoch)
        acc = []
        for ap in reads:
            if ap is None or isinstance(ap, (int, float)):
                continue
            acc.append(self._access(o, ap, "R") + ("R",))
        for ap in writes:
            acc.append(self._access(o, ap, "W") + ("W",))
        for (space, lo, hi, kind) in acc:
            self._record(o, space, lo, hi, kind)
        self.ops[eng].append(o)
        return o

    def dma(self, eng, out, in_, **kw):
        def emit(e):
            return e.dma_start(out=out, in_=in_, **kw)
        o = Op(eng, emit, self.epoch)
        o.is_dma = True
        n = self.dma_n[eng]
        self.dma_n[eng] = n + 1
        ring = n % NDMA_RING
        o.dsem = (eng, ring)
        o.dval = 16 * (n // NDMA_RING + 1)
        prev = self.dma_last.get((eng, ring))
        o.ring_wait = prev
        self.dma_last[(eng, ring)] = o
        a1 = self._access(o, in_, "R") + ("R",)
        a2 = self._access(o, out, "W") + ("W",)
        self._record(o, *a1)
        self._record(o, *a2)
        self.ops[eng].append(o)
        return o

    def new_epoch(self):
        self.epoch += 1

    def finalize_and_emit(self, final_wait_ops):
        nc = self.nc
        for e in self.ENGS:
            for o in self.ops[e]:
                drop = []
                for d in o.deps:
                    if d.is_dma:
                        continue
                    if d.eng == o.eng and not o.is_dma:
                        if o.eng in ("pe", "sp"):
                            drop.append(d)
                            continue
                    d.signal = True
                for d in drop:
                    o.deps.discard(d)
        for o in final_wait_ops:
            if not o.is_dma:
                o.signal = True
        nep = self.epoch + 1
        for e in self.ENGS:
            cnt = [0] * nep
            for o in self.ops[e]:
                if o.is_dma:
                    continue
                if o.signal:
                    cnt[o.epoch] += 1
                    o.value = cnt[o.epoch]
        import contextlib
        with contextlib.ExitStack() as st:
            esem = {}
            for e in ("pe", "act", "dve", "pool"):
                for ep in range(nep):
                    esem[(e, ep)] = st.enter_context(nc.semaphore(f"s_{e}_{ep}"))
            dsem = {}
            for e in ("sp", "pool", "act"):
                if self.dma_n[e] == 0:
                    continue
                for r in range(NDMA_RING):
                    dsem[(e, r)] = st.enter_context(nc.semaphore(f"d_{e}_{r}"))
            block = st.enter_context(nc.Block())

            def run(eng_name, eng):
                waited = {}

                def wait(sem, val):
                    if waited.get(id(sem), 0) >= val:
                        return
                    waited[id(sem)] = val
                    eng.wait_ge(sem, val)

                for o in self.ops[eng_name]:
                    if o.is_dma and o.ring_wait is not None:
                        wait(dsem[o.ring_wait.dsem], o.ring_wait.dval)
                    for d in o.deps:
                        if d.is_dma:
                            wait(dsem[d.dsem], d.dval)
                        else:
                            wait(esem[(d.eng, d.epoch)], d.value)
                    ins = o.emit(eng)
                    if o.is_dma:
                        ins.then_inc(dsem[o.dsem], 16)
                    elif o.signal:
                        ins.then_inc(esem[(o.eng, o.epoch)], 1)
                if eng_name == "sp":
                    for o in final_wait_ops:
                        if o.is_dma:
                            wait(dsem[o.dsem], o.dval)
                        else:
                            wait(esem[(o.eng, o.epoch)], o.value)

            @block.tensor
            def _(eng):
                run("pe", eng)

            @block.scalar
            def _(eng):
                run("act", eng)

            @block.vector
            def _(eng):
                run("dve", eng)

            @block.gpsimd
            def _(eng):
                run("pool", eng)

            @block.sync
            def _(eng):
                run("sp", eng)

    def mm(self, out, lhsT, rhs, start, stop, **kw):
        return self.op("pe", lambda e: e.matmul(out, lhsT=lhsT, rhs=rhs, start=start, stop=stop, **kw),
                       reads=(lhsT, rhs), writes=(out,))

    def tr(self, out, in_, ident):
        return self.op("pe", lambda e: e.transpose(out, in_, ident), reads=(in_, ident), writes=(out,))

    def act(self, out, in_, func, bias=None, scale=None):
        kw = {}
        if bias is not None:
            kw["bias"] = bias
        if scale is not None:
            kw["scale"] = scale
        return self.op("act", lambda e: e.activation(out=out, in_=in_, func=func, **kw),
                       reads=(in_, bias, scale), writes=(out,))

    def tt(self, eng, out, in0, in1, op):
        return self.op(eng, lambda e: e.tensor_tensor(out=out, in0=in0, in1=in1, op=op),
                       reads=(in0, in1), writes=(out,))

    def ts(self, eng, out, in0, s1, s2, op0, op1=None):
        if op1 is None:
            return self.op(eng, lambda e: e.tensor_single_scalar(out=out, in_=in0, scalar=s1, op=op0),
                           reads=(in0, s1), writes=(out,))
        return self.op(eng, lambda e: e.tensor_scalar(out=out, in0=in0, scalar1=s1, scalar2=s2, op0=op0, op1=op1),
                       reads=(in0, s1, s2), writes=(out,))

    def stt(self, out, in0, scalar, in1, op0, op1):
        return self.op("dve", lambda e: e.scalar_tensor_tensor(out=out, in0=in0, scalar=scalar, in1=in1, op0=op0, op1=op1),
                       reads=(in0, scalar, in1), writes=(out,))

    def copy(self, eng, out, in_):
        if eng == "act":
            return self.act(out, in_, AF.Copy)
        return self.op(eng, lambda e: e.tensor_copy(out=out, in_=in_), reads=(in_,), writes=(out,))

    def memset(self, eng, out, val):
        return self.op(eng, lambda e: e.memset(out, val), writes=(out,))

    def recip(self, out, in_):
        return self.op("dve", lambda e: e.reciprocal(out=out, in_=in_), reads=(in_,), writes=(out,))


class Arena:
    def __init__(self, nc, nbytes):
        self.t = nc.alloc_sbuf_tensor("arena", [128, nbytes], U8)
        self.nbytes = nbytes

    def view(self, off, shape, dt):
        n = 1
        for s in shape[1:]:
            n *= s
        nb = n * _esz(dt)
        assert off % 4 == 0 and off + nb <= self.nbytes, (off, nb, self.nbytes)
        v = self.t[:, off:off + nb].bitcast(dt)
        if len(shape) == 2:
            return v
        if len(shape) == 3:
            return v.rearrange("p (a b) -> p a b", a=shape[1])
        if len(shape) == 4:
            return v.rearrange("p (a b c) -> p a b c", a=shape[1], b=shape[2])
        if len(shape) == 5:
            return v.rearrange("p (a b c d) -> p a b c d", a=shape[1], b=shape[2], c=shape[3])
        raise ValueError(shape)


def build(S=SEQ, NL=DEPTH):
    NB = S // T
    nc = bass.Bass("TRN2", target_bir_lowering=False)
    P = Prog(nc)

    def din(name, shape):
        return nc.dram_tensor(name, list(shape), F32, kind="ExternalInput").ap()

    x_d = din("x", [S, D])
    p_d = din("p", [NL, S, PLE])
    gmix_d = din("norm_mix_g", [NL, D])
    w_in_d = din("w_in", [NL, D, 2048])
    pool_w_d = din("pool_w", [NL, 4, 128, 128])
    pool_b_d = din("pool_b", [NL, 512])
    pool_s_d = din("pool_scale", [NL, 512])
    relb_d = din("rel_bias", [NL * 8, 257])
    w_out_d = din("w_out", [NL, D, D])
    gffn_d = din("norm_ffn_g", [NL, D])
    w_up_d = din("w_up", [NL, D, 2 * DFF])
    conv_w_d = din("conv_w", [NL, 3, DFF])
    conv_b_d = din("conv_b", [NL, DFF])
    w_down_d = din("w_down", [NL, DFF, D])
    gple_d = din("norm_ple_g", [NL, D])
    w_gate_d = din("w_ple_gate", [NL, D, D])
    b_gate_d = din("b_ple_gate", [NL, D])
    w_ple_d = din("w_ple", [NL, PLE, D])
    gfin_d = din("final_g", [1, D])
    consts_d = din("consts", [128, 200])
    out_d = nc.dram_tensor("out", [S, D], F32, kind="ExternalOutput").ap()
    hT_d = nc.dram_tensor("hT_scr", [NCH, 128, S], F32, kind="Internal").ap()
    E_t = nc.dram_tensor("E_scr", [NL * 8, 128, 768], F32, kind="Internal")
    E_d = E_t.ap()

    off = 0

    def take(n):
        nonlocal off
        o = off
        off += (n + 31) // 32 * 32
        return o

    o_hblk = take(NCH * T * 4)
    o_hblk1 = take(NCH * T * 4)
    o_hb = take(NCH * T * 2)
    o_rstd = take(T * 4)
    o_consts = take(200 * 4)
    o_ones = take(128 * 2)
    o_gains = take((3 * NL + 1) * 8 * 4)
    o_poolb = take(NL * 4 * 4)
    o_pools = take(NL * 4 * 4)
    o_convw = take(NL * 3 * NF * 4)
    o_convb = take(NL * NF * 4)
    o_bgate = take(NL * 8 * 4)
    o_ghalo = take(NF * 2 * 4)
    o_W = take(135168)
    o_X = take(29184)
    o_zero = o_X
    total = off
    print("SBUF arena bytes/partition:", total)
    assert total <= 208 * 1024, total
    A = Arena(nc, total)

    hblks = [A.view(o_hblk, [128, NCH, T], F32), A.view(o_hblk1, [128, NCH, T], F32)]
    H = [hblks[0]]
    hb = A.view(o_hb, [128, NCH, T], BF16)
    rstd = A.view(o_rstd, [128, T], F32)
    consts = A.view(o_consts, [128, 200], F32)
    eps_col = consts[:, 192:193]
    ident = consts[:, 0:128]
    invcnt = consts[:, 128:192].rearrange("p (g t) -> p g t", g=4)
    ones_bf = A.view(o_ones, [128, 128], BF16)
    gains = A.view(o_gains, [128, 3 * NL + 1, 8], F32)
    poolb = A.view(o_poolb, [128, NL * 4], F32)
    pools = A.view(o_pools, [128, NL * 4], F32)
    convw = A.view(o_convw, [128, NL * 3, NF], F32)
    convb = A.view(o_convb, [128, NL, NF], F32)
    bgate = A.view(o_bgate, [128, NL, 8], F32)
    ghalo = A.view(o_ghalo, [128, NF, 2], F32)
    zero = A.view(o_zero, [128, T], F32)

    w_up = A.view(o_W, [128, NF, 2, NCH, 128], BF16)
    w_down = A.view(o_W + 90112, [128, NF, D], BF16)
    w_in = A.view(o_W, [128, NCH, 2048], BF16)
    w_out = A.view(o_W + 32768, [128, NCH, D], BF16)
    w_gate = A.view(o_W + 49152, [128, NCH, D], BF16)
    w_ple = A.view(o_W + 65536, [128, 2, D], BF16)
    pool_w = A.view(o_W + 69632, [128, 4, 128], BF16)
    oM = o_W + 70656
    Eb = A.view(oM, [128, 8, 640], BF16)
    Mtmp = [A.view(oM + 10240 + i * 2560, [128, 640], F32) for i in range(2)]
    vaug = A.view(oM + 20480, [128, 8, 4, 2, 128], BF16)
    kT = A.view(oM + 36864, [128, 4, 2 * T], BF16)
    qT = A.view(oM + 45056, [128, 4, T], BF16)
    cat = A.view(oM + 49152, [128, NCH, T], BF16)
    pstage = A.view(oM + 57344, [128, 4, PLE], F32)
    pT = A.view(oM + 61440, [128, 2, T], BF16)
    iostage = A.view(oM, [128, 4, D], F32)
    Esb = A.view(oM + 20480, [32, 768], F32)
    NFH = NF // 2
    NSH = 2
    actb = A.view(o_X, [128, NFH + NSH, T], BF16)
    gbuf = [A.view(o_X + 13312 + i * 2080, [128, 520], F32) for i in range(2)]
    accb = [A.view(o_X + 17472 + i * 2048, [128, T], F32) for i in range(3)]
    ubuf = A.view(o_X, [128, 4, 528], F32)
    ptmp = [A.view(o_X + 8448 + i * 2112, [128, 528], F32) for i in range(2)]
    ybf = [A.view(o_X + 12672 + i * 1024, [128, T], BF16) for i in range(4)]
    NPT = 6
    pTs = [A.view(o_X + 16768 + i * 2048, [128, 2, T], BF16) for i in range(4)]
    pTs += [A.view(oM + 15360 + i * 2048, [128, 2, T], BF16) for i in range(2)]
    gateb = [A.view(o_X + 24960 + i * 2048, [128, T], F32) for i in range(2)]
    recb = gateb
    tmp16 = A.view(o_X + 29056, [128, 16], F32)

    psp = [nc.alloc_psum_tensor(f"psp{i}", [128, 2, T], F32)[:, :, :] for i in range(4)]
    psb = [psp[i // 2][:, i % 2, :] for i in range(8)]
    pools_ps = {"mm": [0, 1, 2, 3], "o": [6, 7], "aux": [0, 1, 2, 3], "g": [2, 3, 4], "v": [5, 6, 7]}
    scp_ctr = [0]

    def bank_pair():
        i = scp_ctr[0]
        scp_ctr[0] = i + 1
        return psp[i % 3]
    ps_ctr = {k: 0 for k in pools_ps}

    def bank(pool):
        lst = pools_ps[pool]
        if pool == "aux":
            pool = "mm"
        i = ps_ctr[pool]
        ps_ctr[pool] = i + 1
        return psb[lst[i % len(lst)]]

    ctr = {"i": 0}

    def rr(lst):
        ctr["i"] += 1
        return lst[ctr["i"] % len(lst)]

    P.dma("sp", consts, consts_d)
    P.memset("pool", ones_bf, 1.0 / 1024.0)
    P.memset("pool", zero, 0.0)
    for L in range(NL):
        P.dma("sp", gains[:, 3 * L + 0, :], gmix_d[L].rearrange("(c p) -> p c", p=128))
        P.dma("sp", gains[:, 3 * L + 1, :], gffn_d[L].rearrange("(c p) -> p c", p=128))
        P.dma("sp", gains[:, 3 * L + 2, :], gple_d[L].rearrange("(c p) -> p c", p=128))
        P.dma("sp", poolb[:, 4 * L:4 * L + 4], pool_b_d[L].rearrange("(c p) -> p c", p=128))
        P.dma("sp", pools[:, 4 * L:4 * L + 4], pool_s_d[L].rearrange("(c p) -> p c", p=128))
        for k in range(3):
            P.dma("sp", convw[:, 3 * L + k, :], conv_w_d[L, k].rearrange("(c p) -> p c", p=128))
        P.dma("sp", convb[:, L, :], conv_b_d[L].rearrange("(c p) -> p c", p=128))
        P.dma("sp", bgate[:, L, :], b_gate_d[L].rearrange("(c p) -> p c", p=128))
    P.dma("sp", gains[:, 3 * NL, :], gfin_d[0].rearrange("(c p) -> p c", p=128))
    P.dma("sp", Esb[0:NL * 8, 0:256], relb_d[:, 1:257])
    P.ts("dve", Esb[0:NL * 8, 256:768], zero[0:NL * 8, 0:512], Esb[0:NL * 8, 255:256], None, ALU.add)
    for L in range(NL):
        P.dma("sp", E_d[L * 8:(L + 1) * 8], Esb[L * 8:(L + 1) * 8, :].unsqueeze(1).broadcast_to([8, 128, 768]))

    def load_h(b, slot):
        for c in range(NCH):
            P.dma("sp", hblks[slot][:, c, :], hT_d[c, :, b * T:(b + 1) * T])

    def store_h(b):
        for c in range(NCH):
            P.dma("sp", hT_d[c, :, b * T:(b + 1) * T], H[0][:, c, :])

    def wview(wd, L):
        return wd[L].rearrange("(c p) n -> p c n", p=128)

    def load_w_AC(Lc, La):
        if Lc is not None:
            for n0 in (0, 512):
                P.dma("pool", w_gate[:, :, n0:n0 + 512], wview(w_gate_d, Lc)[:, :, n0:n0 + 512])
            P.dma("pool", w_ple[:, :, :], wview(w_ple_d, Lc))
        if La is not None:
            for n0 in range(0, 2048, 512):
                P.dma("pool", w_in[:, :, n0:n0 + 512], wview(w_in_d, La)[:, :, n0:n0 + 512])
            P.dma("pool", pool_w[:, :, :], pool_w_d[La].rearrange("g c d -> c g d"))
            for n0 in (0, 512):
                P.dma("pool", w_out[:, :, n0:n0 + 512], wview(w_out_d, La)[:, :, n0:n0 + 512])
            for h in range(8):
                mt = Mtmp[h % 2]
                src = bass.AP(E_t, (La * 8 + h) * 128 * 768 + 127, [[767, 128], [1, 640]])
                P.dma("sp", mt, src)
                P.memset("pool", mt[64:128, 0:64], NEG)
                P.memset("pool", mt[0:64, 576:640], NEG)
                P.act(Eb[:, h, :], mt, AF.Exp)
            P.memset("pool", vaug[:, :, :, :, :], 1.0)

    def load_w_B(L):
        wv = wview(w_up_d, L)
        for f in range(NF):
            for gv in range(2):
                c0 = gv * DFF + f * 128
                P.dma("pool", w_up[:, f, gv, :, :], wv[:, :, c0:c0 + 128])
        wd = w_down_d[L].rearrange("(f p) n -> p f n", p=128)
        for n0 in range(0, D, 256):
            P.dma("pool", w_down[:, :, n0:n0 + 256], wd[:, :, n0:n0 + 256])

    def norm(gidx):
        for c in range(NCH):
            P.act(hb[:, c, :], H[0][:, c, :], AF.Square)
        bk = bank("aux")
        for c in range(NCH):
            P.mm(bk, ones_bf, hb[:, c, :], c == 0, c == NCH - 1)
        P.act(rstd, bk, AF.Ln, bias=eps_col)
        P.act(rstd, rstd, AF.Exp, scale=-0.5)
        return gidx

    def norm_apply_bf(gidx):
        for c in range(NCH):
            P.stt(hb[:, c, :], H[0][:, c, :], gains[:, gidx, c:c + 1], rstd, ALU.mult, ALU.mult)

    def proj(wt, col0, rhs_chunks, nk):
        bk = bank("mm")
        for c in range(nk):
            P.mm(bk, wt[:, c, col0:col0 + 128], rhs_chunks[:, c, :], c == 0, c == nk - 1)
        return bk

    def stage_init(b):
        P.dma("sp", iostage, x_d[b * T:(b + 1) * T, :].rearrange("(tt p) f -> p tt f", p=128))
        for c in range(NCH):
            bk = bank("mm")
            for tt in range(4):
                P.tr(bk[:, tt * 128:(tt + 1) * 128], iostage[:, tt, c * 128:(c + 1) * 128], ident)
            P.copy("act" if c % 2 else "dve", H[0][:, c, :], bk)

    def stage_A(L, b, do_pre=True, hook=None):
        if do_pre:
            norm(3 * L)
            norm_apply_bf(3 * L)
        cur = b % 2
        prv = 1 - cur
        for g in range(4):
            bk = proj(w_in, g * 128, hb, NCH)
            if b == 0:
                P.memset("pool", ubuf[:, g, 0:16], 0.0)
            else:
                P.copy("pool", ubuf[:, g, 0:16], ubuf[:, g, 512:528])
            P.copy("act", ubuf[:, g, 16:528], bk)
        for j in range(4):
            bk = proj(w_in, 512 + j * 128, hb, NCH)
            P.act(qT[:, j, :], bk, AF.Copy, scale=0.125)
        for j in range(4):
            bk = proj(w_in, 1024 + j * 128, hb, NCH)
            P.copy("dve", kT[:, j, cur * T:(cur + 1) * T], bk)
        for tt in range(4):
            bk = bank("mm")
            for c in range(NCH):
                P.mm(bk, hb[:, c, tt * 128:(tt + 1) * 128], w_in[:, c, 1536:2048], c == 0, c == NCH - 1)
            bv = bk.rearrange("p (j e d) -> p j e d", j=4, e=2)
            tile = cur * 4 + tt
            P.copy("act", vaug[:, tile, :, 0, 0:64], bv[:, :, 0, :])
            P.copy("dve", vaug[:, tile, :, 1, 64:128], bv[:, :, 1, :])
        for g in range(4):
            src = ubuf[:, g, :]
            prev = src
            for k in range(1, g + 2):
                sh = 1 << (k - 1)
                lo = (1 << k) - 1
                dst = ptmp[k % 2]
                P.tt("pool", dst[:, lo:528], prev[:, lo:528], prev[:, lo - sh:528 - sh], ALU.add)
                prev = dst
            w = 1 << (g + 1)
            yb = ybf[g]
            if b == 0:
                P.tt("pool", prev[:, 16:32], prev[:, 16:32], invcnt[:, g, :], ALU.mult)
                P.act(prev[:, 32:528], prev[:, 32:528], AF.Copy, scale=1.0 / w)
            else:
                P.act(prev[:, 16:528], prev[:, 16:528], AF.Copy, scale=1.0 / w)
            P.tt("pool", yb, prev[:, 16:528], src[:, 16:528], ALU.subtract)
        steps = []
        for j in range(4):
            rs = [r for r in range(8) if 4 * b - 4 + r >= 0]
            for r in rs:
                steps.append((j, r, r == rs[0], r == rs[-1]))
        LA = 4
        st_pt = {}

        def geom(r):
            qlo = max(0, r - 4)
            qhi = min(3, r)
            half = prv if r < 4 else cur
            return qlo, qhi, qhi - qlo + 1, half

        def front(i):
            j, r, first, last = steps[i]
            qlo, qhi, nq, half = geom(r)
            n = nq * 128
            kc0 = half * T + (r % 4) * 128
            sp2 = bank_pair()
            for e in range(2):
                po = 64 * e
                P.mm(sp2[:, e, 0:n], kT[po:po + 64, j, kc0:kc0 + 128], qT[po:po + 64, j, qlo * 128:(qhi + 1) * 128], True, True)
            d0 = qlo - r + 4
            pt = pTs[i % NPT]
            P.act(pt[:, :, 0:n], sp2[:, :, 0:n], AF.Exp)
            P.tt("dve", pt[:, :, 0:n], pt[:, :, 0:n], Eb[:, 2 * j:2 * j + 2, d0 * 128:(d0 + nq) * 128], ALU.mult)
            st_pt[i] = pt

        def back(i):
            j, r, first, last = steps[i]
            qlo, qhi, nq, half = geom(r)
            n = nq * 128
            vt = half * 4 + (r % 4)
            pt = st_pt[i]
            for e in range(2):
                ob = psb[6 + e]
                P.mm(ob[:, qlo * 128:(qhi + 1) * 128], vaug[:, vt, j, e, :], pt[:, e, 0:n], first, last,
                     skip_group_check=True)
            if last:
                for e in range(2):
                    ob = psb[6 + e]
                    po = 64 * e
                    so = 64 - po
                    rec = recb[e]
                    P.act(rec[so:so + 64, :], ob[so:so + 64, :], AF.Ln)
                    P.act(rec[so:so + 64, :], rec[so:so + 64, :], AF.Exp, scale=-1.0)
                    P.tt("dve", cat[po:po + 64, 4 + j, :], ob[po:po + 64, :], rec[so:so + 64, :], ALU.mult)

        for i in range(len(steps) + LA):
            if i >= LA:
                back(i - LA)
            if i < len(steps):
                front(i)
        for g in range(4):
            bk = bank("mm")
            P.mm(bk, pool_w[:, g, :], ybf[g], True, True)
            P.ts("dve", cat[:, g, :], bk, poolb[:, 4 * L + g:4 * L + g + 1], pools[:, 4 * L + g:4 * L + g + 1], ALU.add, ALU.mult)
        if hook is not None:
            hook()
        for m in range(NCH):
            bk = proj(w_out, m * 128, cat, NCH)
            P.tt("dve", H[0][:, m, :], bk, H[0][:, m, :], ALU.add)

    def stage_B(L, b, do_pre=True, hook=None):
        if do_pre:
            norm(3 * L + 1)
            norm_apply_bf(3 * L + 1)

        def up(f, slot):
            bg = bank("g")
            for c in range(NCH):
                P.mm(bg, w_up[:, f, 0, c, :], hb[:, c, :], c == 0, c == NCH - 1)
            bv = bank("v")
            for c in range(NCH):
                P.mm(bv, w_up[:, f, 1, c, :], hb[:, c, :], c == 0, c == NCH - 1)
            gb = gbuf[f % 2]
            if b == 0:
                P.memset("pool", gb[:, 0:2], 0.0)
            else:
                P.copy("pool", gb[:, 0:2], ghalo[:, f, :])
            P.copy("act", gb[:, 2:514], bg)
            P.copy("pool", ghalo[:, f, :], gb[:, 512:514])
            acc = accb[f % 3]
            P.act(acc, gb[:, 0:512], AF.Identity, bias=convb[:, L, f:f + 1], scale=convw[:, 3 * L + 0, f:f + 1])
            P.stt(acc, gb[:, 1:513], convw[:, 3 * L + 1, f:f + 1], acc, ALU.mult, ALU.add)
            P.stt(acc, gb[:, 2:514], convw[:, 3 * L + 2, f:f + 1], acc, ALU.mult, ALU.add)
            P.act(acc, acc, AF.Gelu)
            P.tt("dve", actb[:, slot, :], acc, bv, ALU.mult)

        def down(fs, slots):
            for m in range(NCH):
                bk = bank("mm")
                for k, (f, sl) in enumerate(zip(fs, slots)):
                    P.mm(bk, w_down[:, f, m * 128:(m + 1) * 128], actb[:, sl, :], k == 0, k == len(fs) - 1)
                P.tt("dve", H[0][:, m, :], bk, H[0][:, m, :], ALU.add)

        slots1 = [NFH + i for i in range(NSH)] + list(range(NSH, NFH))
        for fi in range(NFH):
            up(fi, fi)
        for fi in range(NSH):
            up(NFH + fi, slots1[fi])
        down(list(range(NFH)), list(range(NFH)))
        for fi in range(NSH, NFH):
            up(NFH + fi, slots1[fi])
        if hook is not None:
            hook()
        down([NFH + fi for fi in range(NFH)], slots1)

    def load_p(L, b):
        P.dma("sp", pstage, p_d[L, b * T:(b + 1) * T, :].rearrange("(tt p) k -> p tt k", p=128))

    def stage_C(L, b, do_pre=True):
        if do_pre:
            norm(3 * L + 2)
            norm_apply_bf(3 * L + 2)
        for kc in range(2):
            bk = bank("aux")
            for tt in range(4):
                P.tr(bk[:, tt * 128:(tt + 1) * 128], pstage[:, tt, kc * 128:(kc + 1) * 128], ident)
            P.copy("act", pT[:, kc, :], bk)
        for m in range(NCH):
            bg = proj(w_gate, m * 128, hb, NCH)
            gt = rr(gateb)
            P.act(gt, bg, AF.Sigmoid, bias=bgate[:, L, m:m + 1])
            bp = bank("mm")
            for kc in range(2):
                P.mm(bp, w_ple[:, kc, m * 128:(m + 1) * 128], pT[:, kc, :], kc == 0, kc == 1)
            P.tt("dve", gt, gt, bp, ALU.mult)
            P.tt("pool", H[0][:, m, :], gt, H[0][:, m, :], ALU.add)

    def stage_final(b, hook=None):
        norm(3 * NL)
        for c in range(NCH):
            P.stt(H[0][:, c, :], H[0][:, c, :], gains[:, 3 * NL, c:c + 1], rstd, ALU.mult, ALU.mult)
        if hook is not None:
            hook()
        for tt in range(4):
            for hf in range(2):
                bk = bank("mm")
                for i in range(4):
                    c = hf * 4 + i
                    P.tr(bk[:, i * 128:(i + 1) * 128], H[0][:, c, tt * 128:(tt + 1) * 128], ident)
                P.copy("act" if hf else "dve", iostage[:, tt, hf * 512:(hf + 1) * 512], bk)
        return P.dma("sp", out_d[b * T:(b + 1) * T, :].rearrange("(tt p) f -> p tt f", p=128), iostage)

    finals = []
    for b in range(NB):
        H[0] = hblks[b % 2]
        stage_init(b)
        store_h(b)
    sweeps = []
    for L in range(NL):
        sweeps.append(("A", L))
        sweeps.append(("B", L))
    sweeps.append(("F", NL))
    items = [(si, b) for si in range(len(sweeps)) for b in range(NB)]
    load_h(0, 0)
    for idx, (si, b) in enumerate(items):
        kind, L = sweeps[si]
        slot = idx % 2
        if b == 0:
            P.new_epoch()
            if kind == "A":
                load_w_AC(L - 1 if L > 0 else None, L)
            elif kind == "B":
                load_w_B(L)
            else:
                load_w_AC(NL - 1, None)
        if b == 0 and (kind == "F" or (kind == "A" and L > 0)):
            load_p(L - 1, 0)
        if idx + 1 < len(items):
            load_h(items[idx + 1][1], 1 - slot)
        H[0] = hblks[slot]
        hook = None
        if idx + 1 < len(items):
            nkind, nL = sweeps[items[idx + 1][0]]
            if nkind == "A":
                ng = 3 * (nL - 1) + 2 if nL > 0 else 0
            elif nkind == "B":
                ng = 3 * nL + 1
            else:
                ng = 3 * (NL - 1) + 2

            def hook(ng=ng, nslot=1 - slot, cslot=slot):
                H[0] = hblks[nslot]
                norm(ng)
                norm_apply_bf(ng)
                H[0] = hblks[cslot]
        pre_done = idx > 0
        if kind == "A":
            if L > 0:
                stage_C(L - 1, b, do_pre=not pre_done)
                if b + 1 < NB:
                    load_p(L - 1, b + 1)
                stage_A(L, b, True, hook)
            else:
                stage_A(L, b, not pre_done, hook)
            store_h(b)
        elif kind == "B":
            stage_B(L, b, not pre_done, hook)
            store_h(b)
        else:
            stage_C(NL - 1, b, do_pre=not pre_done)
            if b + 1 < NB:
                load_p(NL - 1, b + 1)
            finals.append(stage_final(b, hook))
    with nc.allow_non_contiguous_dma(reason="small one-time parameter vectors"):
        P.finalize_and_emit(finals)
    return nc


def make_consts():
    c = np.zeros((128, 200), np.float32)
    c[:, 192] = EPS
    c[:, 0:128] = np.eye(128, dtype=np.float32)
    for g in range(4):
        w = 1 << (g + 1)
        for t in range(16):
            c[:, 128 + g * 16 + t] = 1.0 / min(t + 1, w)
    return c


_NC_CACHE = {}


def run_cores(inputs, S, NL, n_cores):
    key = (S, NL)
    if key not in _NC_CACHE:
        _NC_CACHE[key] = build(S, NL)
    nc = _NC_CACHE[key]
    consts = make_consts()
    in_maps = []
    f = lambda a: np.ascontiguousarray(np.asarray(a, dtype=np.float32))
    for i in range(n_cores):
        m = {
            "x": f(inputs["x"][i]),
            "p": f(inputs["p"][:, i]),
            "norm_mix_g": f(inputs["norm_mix_g"]),
            "w_in": f(inputs["w_in"]),
            "pool_w": f(inputs["pool_w"]),
            "pool_b": f(inputs["pool_b"]),
            "pool_scale": f(inputs["pool_scale"]),
            "rel_bias": f(inputs["rel_bias"]).reshape(NL * 8, 257),
            "w_out": f(inputs["w_out"]),
            "norm_ffn_g": f(inputs["norm_ffn_g"]),
            "w_up": f(inputs["w_up"]),
            "conv_w": f(inputs["conv_w"]),
            "conv_b": f(inputs["conv_b"]),
            "w_down": f(inputs["w_down"]),
            "norm_ple_g": f(inputs["norm_ple_g"]),
            "w_ple_gate": f(inputs["w_ple_gate"]),
            "b_ple_gate": f(inputs["b_ple_gate"]),
            "w_ple": f(inputs["w_ple"]),
            "final_g": f(inputs["final_g"]).reshape(1, D),
            "consts": consts,
        }
        in_maps.append(m)
    res = run_bass_kernel_spmd(nc, in_maps, core_ids=list(range(n_cores)))
    return np.stack([np.asarray(r["out"]) for r in res.results], axis=0)


def kernel(**inputs):
    out = run_cores(inputs, SEQ, DEPTH, 8)
    return out.astype(np.float32)
```

```python
import numpy as np
import concourse.bass as bass
import concourse.mybir as mybir
from concourse.bass_utils import run_bass_kernel_spmd

F32 = mybir.dt.float32
BF16 = mybir.dt.bfloat16
U8 = mybir.dt.uint8
ALU = mybir.AluOpType
AF = mybir.ActivationFunctionType

D = 1024
NCH = 8
T = 512
DFF = 2816
NF = 22
PLE = 256
NEG = -100.0
EPS = 1e-6
SEQ = 4096
DEPTH = 4
NDMA_RING = 8


def _esz(dt):
    if dt == F32:
        return 4
    if dt == BF16:
        return 2
    if dt == U8:
        return 1
    raise ValueError(dt)


class Op:
    __slots__ = ("eng", "emit", "deps", "signal", "epoch", "value", "is_dma", "dsem", "dval", "ring_wait")

    def __init__(self, eng, emit, epoch):
        self.eng = eng
        self.emit = emit
        self.deps = set()
        self.signal = False
        self.epoch = epoch
        self.value = None
        self.is_dma = False
        self.dsem = None
        self.dval = None
        self.ring_wait = None


class Prog:
    ENGS = ("pe", "act", "dve", "pool", "sp")
    PAGE = 4096

    def __init__(self, nc):
        self.nc = nc
        self.ops = {e: [] for e in self.ENGS}
        self.recs = {}
        self.epoch = 0
        self.dma_n = {e: 0 for e in self.ENGS}
        self.dma_last = {}

    @staticmethod
    def rng(ap):
        sp = str(ap.space)
        pat = ap.ap
        esz = _esz(ap.dtype)
        off = int(ap.offset)
        if "DRAM" in sp:
            lo = off
            hi = off
            for (s, c) in pat:
                ext = (c - 1) * s
                if ext < 0:
                    lo += ext
                else:
                    hi += ext
            return ("D:" + ap.tensor.name, lo * esz, (hi + 1) * esz)
        pstride = pat[0][0]
        fo = off % pstride if pstride > 0 else off
        lo = fo
        hi = fo
        for (s, c) in pat[1:]:
            ext = (c - 1) * s
            if ext < 0:
                lo += ext
            else:
                hi += ext
        name = "SB" if "SB" in sp else ("P:" + ap.tensor.name)
        return (name, lo * esz, (hi + 1) * esz)

    def _pages(self, lo, hi):
        return range(lo // self.PAGE, (hi - 1) // self.PAGE + 1)

    def _access(self, op, ap, kind):
        space, lo, hi = self.rng(ap)
        pages = self.recs.setdefault(space, {})
        for pg in self._pages(lo, hi):
            lst = pages.get(pg)
            if not lst:
                continue
            for rec in lst:
                if rec[0] < hi and lo < rec[1]:
                    if kind == "W" or rec[2] == "W":
                        if rec[3] is not op:
                            op.deps.add(rec[3])
        return space, lo, hi

    def _record(self, op, space, lo, hi, kind):
        pages = self.recs[space]
        for pg in self._pages(lo, hi):
            lst = pages.setdefault(pg, [])
            plo = max(lo, pg * self.PAGE)
            phi = min(hi, (pg + 1) * self.PAGE)
            if kind == "W":
                lst[:] = [r for r in lst if not (plo <= max(r[0], pg * self.PAGE) and min(r[1], (pg + 1) * self.PAGE) <= phi)]
                lst.append([lo, hi, "W", op])
            else:
                done = False
                if not op.is_dma:
                    for r in lst:
                        if r[2] == "R" and r[0] == lo and r[1] == hi and r[3].eng == op.eng and not r[3].is_dma:
                            r[3] = op
                            done = True
                            break
                if not done:
                    lst.append([lo, hi, "R", op])

    def op(self, eng, emit, reads=(), writes=()):
        o = Op(eng, emit, self.epoch)
        acc = []
        for ap in reads:
            if ap is None or isinstance(ap, (int, float)):
                continue
            acc.append(self._access(o, ap, "R") + ("R",))
        for ap in writes:
            acc.append(self._access(o, ap, "W") + ("W",))
        for (space, lo, hi, kind) in acc:
            self._record(o, space, lo, hi, kind)
        self.ops[eng].append(o)
        return o

    def dma(self, eng, out, in_, **kw):
        def emit(e):
            return e.dma_start(out=out, in_=in_, **kw)
        o = Op(eng, emit, self.epoch)
        o.is_dma = True
        n = self.dma_n[eng]
        self.dma_n[eng] = n + 1
        ring = n % NDMA_RING
        o.dsem = (eng, ring)
        o.dval = 16 * (n // NDMA_RING + 1)
        prev = self.dma_last.get((eng, ring))
        o.ring_wait = prev
        self.dma_last[(eng, ring)] = o
        a1 = self._access(o, in_, "R") + ("R",)
        a2 = self._access(o, out, "W") + ("W",)
        self._record(o, *a1)
        self._record(o, *a2)
        self.ops[eng].append(o)
        return o

    def new_epoch(self):
        self.epoch += 1

    def finalize_and_emit(self, final_wait_ops):
        nc = self.nc
        for e in self.ENGS:
            for o in self.ops[e]:
                drop = []
                for d in o.deps:
                    if d.is_dma:
                        continue
                    if d.eng == o.eng and not o.is_dma:
                        if o.eng in ("pe", "sp"):
                            drop.append(d)
                            continue
                    d.signal = True
                for d in drop:
                    o.deps.discard(d)
        for o in final_wait_ops:
            if not o.is_dma:
                o.signal = True
        nep = self.epoch + 1
        for e in self.ENGS:
            cnt = [0] * nep
            for o in self.ops[e]:
                if o.is_dma:
                    continue
                if o.signal:
                    cnt[o.epoch] += 1
                    o.value = cnt[o.epoch]
        import contextlib
        with contextlib.ExitStack() as st:
            esem = {}
            for e in ("pe", "act", "dve", "pool"):
                for ep in range(nep):
                    esem[(e, ep)] = st.enter_context(nc.semaphore(f"s_{e}_{ep}"))
            dsem = {}
            for e in ("sp", "pool", "act"):
                if self.dma_n[e] == 0:
                    continue
                for r in range(NDMA_RING):
                    dsem[(e, r)] = st.enter_context(nc.semaphore(f"d_{e}_{r}"))
            block = st.enter_context(nc.Block())

            def run(eng_name, eng):
                waited = {}

                def wait(sem, val):
                    if waited.get(id(sem), 0) >= val:
                        return
                    waited[id(sem)] = val
                    eng.wait_ge(sem, val)

                for o in self.ops[eng_name]:
                    if o.is_dma and o.ring_wait is not None:
                        wait(dsem[o.ring_wait.dsem], o.ring_wait.dval)
                    for d in o.deps:
                        if d.is_dma:
                            wait(dsem[d.dsem], d.dval)
                        else:
                            wait(esem[(d.eng, d.epoch)], d.value)
                    ins = o.emit(eng)
                    if o.is_dma:
                        ins.then_inc(dsem[o.dsem], 16)
                    elif o.signal:
                        ins.then_inc(esem[(o.eng, o.epoch)], 1)
                if eng_name == "sp":
                    for o in final_wait_ops:
                        if o.is_dma:
                            wait(dsem[o.dsem], o.dval)
                        else:
                            wait(esem[(o.eng, o.epoch)], o.value)

            @block.tensor
            def _(eng):
                run("pe", eng)

            @block.scalar
            def _(eng):
                run("act", eng)

            @block.vector
            def _(eng):
                run("dve", eng)

            @block.gpsimd
            def _(eng):
                run("pool", eng)

            @block.sync
            def _(eng):
                run("sp", eng)

    def mm(self, out, lhsT, rhs, start, stop, **kw):
        return self.op("pe", lambda e: e.matmul(out, lhsT=lhsT, rhs=rhs, start=start, stop=stop, **kw),
                       reads=(lhsT, rhs), writes=(out,))

    def tr(self, out, in_, ident):
        return self.op("pe", lambda e: e.transpose(out, in_, ident), reads=(in_, ident), writes=(out,))

    def act(self, out, in_, func, bias=None, scale=None):
        kw = {}
        if bias is not None:
            kw["bias"] = bias
        if scale is not None:
            kw["scale"] = scale
        return self.op("act", lambda e: e.activation(out=out, in_=in_, func=func, **kw),
                       reads=(in_, bias, scale), writes=(out,))

    def tt(self, eng, out, in0, in1, op):
        return self.op(eng, lambda e: e.tensor_tensor(out=out, in0=in0, in1=in1, op=op),
                       reads=(in0, in1), writes=(out,))

    def ts(self, eng, out, in0, s1, s2, op0, op1=None):
        if op1 is None:
            return self.op(eng, lambda e: e.tensor_single_scalar(out=out, in_=in0, scalar=s1, op=op0),
                           reads=(in0, s1), writes=(out,))
        return self.op(eng, lambda e: e.tensor_scalar(out=out, in0=in0, scalar1=s1, scalar2=s2, op0=op0, op1=op1),
                       reads=(in0, s1, s2), writes=(out,))

    def stt(self, out, in0, scalar, in1, op0, op1):
        return self.op("dve", lambda e: e.scalar_tensor_tensor(out=out, in0=in0, scalar=scalar, in1=in1, op0=op0, op1=op1),
                       reads=(in0, scalar, in1), writes=(out,))

    def copy(self, eng, out, in_):
        if eng == "act":
            return self.act(out, in_, AF.Copy)
        return self.op(eng, lambda e: e.tensor_copy(out=out, in_=in_), reads=(in_,), writes=(out,))

    def memset(self, eng, out, val):
        return self.op(eng, lambda e: e.memset(out, val), writes=(out,))

    def recip(self, out, in_):
        return self.op("dve", lambda e: e.reciprocal(out=out, in_=in_), reads=(in_,), writes=(out,))


class Arena:
    def __init__(self, nc, nbytes):
        self.t = nc.alloc_sbuf_tensor("arena", [128, nbytes], U8)
        self.nbytes = nbytes

    def view(self, off, shape, dt):
        n = 1
        for s in shape[1:]:
            n *= s
        nb = n * _esz(dt)
        assert off % 4 == 0 and off + nb <= self.nbytes, (off, nb, self.nbytes)
        v = self.t[:, off:off + nb].bitcast(dt)
        if len(shape) == 2:
            return v
        if len(shape) == 3:
            return v.rearrange("p (a b) -> p a b", a=shape[1])
        if len(shape) == 4:
            return v.rearrange("p (a b c) -> p a b c", a=shape[1], b=shape[2])
        if len(shape) == 5:
            return v.rearrange("p (a b c d) -> p a b c d", a=shape[1], b=shape[2], c=shape[3])
        raise ValueError(shape)


def build(S=SEQ, NL=DEPTH):
    NB = S // T
    nc = bass.Bass("TRN2", target_bir_lowering=False)
    P = Prog(nc)

    def din(name, shape):
        return nc.dram_tensor(name, list(shape), F32, kind="ExternalInput").ap()

    x_d = din("x", [S, D])
    p_d = din("p", [NL, S, PLE])
    gmix_d = din("norm_mix_g", [NL, D])
    w_in_d = din("w_in", [NL, D, 2048])
    pool_w_d = din("pool_w", [NL, 4, 128, 128])
    pool_b_d = din("pool_b", [NL, 512])
    pool_s_d = din("pool_scale", [NL, 512])
    relb_d = din("rel_bias", [NL * 8, 257])
    w_out_d = din("w_out", [NL, D, D])
    gffn_d = din("norm_ffn_g", [NL, D])
    w_up_d = din("w_up", [NL, D, 2 * DFF])
    conv_w_d = din("conv_w", [NL, 3, DFF])
    conv_b_d = din("conv_b", [NL, DFF])
    w_down_d = din("w_down", [NL, DFF, D])
    gple_d = din("norm_ple_g", [NL, D])
    w_gate_d = din("w_ple_gate", [NL, D, D])
    b_gate_d = din("b_ple_gate", [NL, D])
    w_ple_d = din("w_ple", [NL, PLE, D])
    gfin_d = din("final_g", [1, D])
    consts_d = din("consts", [128, 200])
    out_d = nc.dram_tensor("out", [S, D], F32, kind="ExternalOutput").ap()
    hT_d = nc.dram_tensor("hT_scr", [NCH, 128, S], F32, kind="Internal").ap()
    E_t = nc.dram_tensor("E_scr", [NL * 8, 128, 768], F32, kind="Internal")
    E_d = E_t.ap()

    off = 0

    def take(n):
        nonlocal off
        o = off
        off += (n + 31) // 32 * 32
        return o

    o_hblk = take(NCH * T * 4)
    o_hblk1 = take(NCH * T * 4)
    o_hb = take(NCH * T * 2)
    o_rstd = take(T * 4)
    o_consts = take(200 * 4)
    o_ones = take(128 * 2)
    o_gains = take((3 * NL + 1) * 8 * 4)
    o_poolb = take(NL * 4 * 4)
    o_pools = take(NL * 4 * 4)
    o_convw = take(NL * 3 * NF * 4)
    o_convb = take(NL * NF * 4)
    o_bgate = take(NL * 8 * 4)
    o_ghalo = take(NF * 2 * 4)
    o_W = take(135168)
    o_X = take(29184)
    o_zero = o_X
    total = off
    print("SBUF arena bytes/partition:", total)
    assert total <= 208 * 1024, total
    A = Arena(nc, total)

    hblks = [A.view(o_hblk, [128, NCH, T], F32), A.view(o_hblk1, [128, NCH, T], F32)]
    H = [hblks[0]]
    hb = A.view(o_hb, [128, NCH, T], BF16)
    rstd = A.view(o_rstd, [128, T], F32)
    consts = A.view(o_consts, [128, 200], F32)
    eps_col = consts[:, 192:193]
    ident = consts[:, 0:128]
    invcnt = consts[:, 128:192].rearrange("p (g t) -> p g t", g=4)
    ones_bf = A.view(o_ones, [128, 128], BF16)
    gains = A.view(o_gains, [128, 3 * NL + 1, 8], F32)
    poolb = A.view(o_poolb, [128, NL * 4], F32)
    pools = A.view(o_pools, [128, NL * 4], F32)
    convw = A.view(o_convw, [128, NL * 3, NF], F32)
    convb = A.view(o_convb, [128, NL, NF], F32)
    bgate = A.view(o_bgate, [128, NL, 8], F32)
    ghalo = A.view(o_ghalo, [128, NF, 2], F32)
    zero = A.view(o_zero, [128, T], F32)

    w_up = A.view(o_W, [128, NF, 2, NCH, 128], BF16)
    w_down = A.view(o_W + 90112, [128, NF, D], BF16)
    w_in = A.view(o_W, [128, NCH, 2048], BF16)
    w_out = A.view(o_W + 32768, [128, NCH, D], BF16)
    w_gate = A.view(o_W + 49152, [128, NCH, D], BF16)
    w_ple = A.view(o_W + 65536, [128, 2, D], BF16)
    pool_w = A.view(o_W + 69632, [128, 4, 128], BF16)
    oM = o_W + 70656
    Eb = A.view(oM, [128, 8, 640], BF16)
    Mtmp = [A.view(oM + 10240 + i * 2560, [128, 640], F32) for i in range(2)]
    vaug = A.view(oM + 20480, [128, 8, 4, 2, 128], BF16)
    kT = A.view(oM + 36864, [128, 4, 2 * T], BF16)
    qT = A.view(oM + 45056, [128, 4, T], BF16)
    cat = A.view(oM + 49152, [128, NCH, T], BF16)
    pstage = A.view(oM + 57344, [128, 4, PLE], F32)
    pT = A.view(oM + 61440, [128, 2, T], BF16)
    iostage = A.view(oM, [128, 4, D], F32)
    Esb = A.view(oM + 20480, [32, 768], F32)
    NFH = NF // 2
    NSH = 2
    actb = A.view(o_X, [128, NFH + NSH, T], BF16)
    gbuf = [A.view(o_X + 13312 + i * 2080, [128, 520], F32) for i in range(2)]
    accb = [A.view(o_X + 17472 + i * 2048, [128, T], F32) for i in range(3)]
    ubuf = A.view(o_X, [128, 4, 528], F32)
    ptmp = [A.view(o_X + 8448 + i * 2112, [128, 528], F32) for i in range(2)]
    ybf = [A.view(o_X + 12672 + i * 1024, [128, T], BF16) for i in range(4)]
    NPT = 6
    pTs = [A.view(o_X + 16768 + i * 2048, [128, 2, T], BF16) for i in range(4)]
    pTs += [A.view(oM + 15360 + i * 2048, [128, 2, T], BF16) for i in range(2)]
    gateb = [A.view(o_X + 24960 + i * 2048, [128, T], F32) for i in range(2)]
    recb = gateb
    tmp16 = A.view(o_X + 29056, [128, 16], F32)

    psp = [nc.alloc_psum_tensor(f"psp{i}", [128, 2, T], F32)[:, :, :] for i in range(4)]
    psb = [psp[i // 2][:, i % 2, :] for i in range(8)]
    pools_ps = {"mm": [0, 1, 2, 3], "o": [6, 7], "aux": [0, 1, 2, 3], "g": [2, 3, 4], "v": [5, 6, 7]}
    scp_ctr = [0]

    def bank_pair():
        i = scp_ctr[0]
        scp_ctr[0] = i + 1
        return psp[i % 3]
    ps_ctr = {k: 0 for k in pools_ps}

    def bank(pool):
        lst = pools_ps[pool]
        if pool == "aux":
            pool = "mm"
        i = ps_ctr[pool]
        ps_ctr[pool] = i + 1
        return psb[lst[i % len(lst)]]

    ctr = {"i": 0}

    def rr(lst):
        ctr["i"] += 1
        return lst[ctr["i"] % len(lst)]

    P.dma("sp", consts, consts_d)
    P.memset("pool", ones_bf, 1.0 / 1024.0)
    P.memset("pool", zero, 0.0)
    for L in range(NL):
        P.dma("sp", gains[:, 3 * L + 0, :], gmix_d[L].rearrange("(c p) -> p c", p=128))
        P.dma("sp", gains[:, 3 * L + 1, :], gffn_d[L].rearrange("(c p) -> p c", p=128))
        P.dma("sp", gains[:, 3 * L + 2, :], gple_d[L].rearrange("(c p) -> p c", p=128))
        P.dma("sp", poolb[:, 4 * L:4 * L + 4], pool_b_d[L].rearrange("(c p) -> p c", p=128))
        P.dma("sp", pools[:, 4 * L:4 * L + 4], pool_s_d[L].rearrange("(c p) -> p c", p=128))
        for k in range(3):
            P.dma("sp", convw[:, 3 * L + k, :], conv_w_d[L, k].rearrange("(c p) -> p c", p=128))
        P.dma("sp", convb[:, L, :], conv_b_d[L].rearrange("(c p) -> p c", p=128))
        P.dma("sp", bgate[:, L, :], b_gate_d[L].rearrange("(c p) -> p c", p=128))
    P.dma("sp", gains[:, 3 * NL, :], gfin_d[0].rearrange("(c p) -> p c", p=128))
    P.dma("sp", Esb[0:NL * 8, 0:256], relb_d[:, 1:257])
    P.ts("dve", Esb[0:NL * 8, 256:768], zero[0:NL * 8, 0:512], Esb[0:NL * 8, 255:256], None, ALU.add)
    for L in range(NL):
        P.dma("sp", E_d[L * 8:(L + 1) * 8], Esb[L * 8:(L + 1) * 8, :].unsqueeze(1).broadcast_to([8, 128, 768]))

    def load_h(b, slot):
        for c in range(NCH):
            P.dma("sp", hblks[slot][:, c, :], hT_d[c, :, b * T:(b + 1) * T])

    def store_h(b):
        for c in range(NCH):
            P.dma("sp", hT_d[c, :, b * T:(b + 1) * T], H[0][:, c, :])

    def wview(wd, L):
        return wd[L].rearrange("(c p) n -> p c n", p=128)

    def load_w_AC(Lc, La):
        if Lc is not None:
            for n0 in (0, 512):
                P.dma("pool", w_gate[:, :, n0:n0 + 512], wview(w_gate_d, Lc)[:, :, n0:n0 + 512])
            P.dma("pool", w_ple[:, :, :], wview(w_ple_d, Lc))
        if La is not None:
            for n0 in range(0, 2048, 512):
                P.dma("pool", w_in[:, :, n0:n0 + 512], wview(w_in_d, La)[:, :, n0:n0 + 512])
            P.dma("pool", pool_w[:, :, :], pool_w_d[La].rearrange("g c d -> c g d"))
            for n0 in (0, 512):
                P.dma("pool", w_out[:, :, n0:n0 + 512], wview(w_out_d, La)[:, :, n0:n0 + 512])
            for h in range(8):
                mt = Mtmp[h % 2]
                src = bass.AP(E_t, (La * 8 + h) * 128 * 768 + 127, [[767, 128], [1, 640]])
                P.dma("sp", mt, src)
                P.memset("pool", mt[64:128, 0:64], NEG)
                P.memset("pool", mt[0:64, 576:640], NEG)
                P.act(Eb[:, h, :], mt, AF.Exp)
            P.memset("pool", vaug[:, :, :, :, :], 1.0)

    def load_w_B(L):
        wv = wview(w_up_d, L)
        for f in range(NF):
            for gv in range(2):
                c0 = gv * DFF + f * 128
                P.dma("pool", w_up[:, f, gv, :, :], wv[:, :, c0:c0 + 128])
        wd = w_down_d[L].rearrange("(f p) n -> p f n", p=128)
        for n0 in range(0, D, 256):
            P.dma("pool", w_down[:, :, n0:n0 + 256], wd[:, :, n0:n0 + 256])

    def norm(gidx):
        for c in range(NCH):
            P.act(hb[:, c, :], H[0][:, c, :], AF.Square)
        bk = bank("aux")
        for c in range(NCH):
            P.mm(bk, ones_bf, hb[:, c, :], c == 0, c == NCH - 1)
        P.act(rstd, bk, AF.Ln, bias=eps_col)
        P.act(rstd, rstd, AF.Exp, scale=-0.5)
        return gidx

    def norm_apply_bf(gidx):
        for c in range(NCH):
            P.stt(hb[:, c, :], H[0][:, c, :], gains[:, gidx, c:c + 1], rstd, ALU.mult, ALU.mult)

    def proj(wt, col0, rhs_chunks, nk):
        bk = bank("mm")
        for c in range(nk):
            P.mm(bk, wt[:, c, col0:col0 + 128], rhs_chunks[:, c, :], c == 0, c == nk - 1)
        return bk

    def stage_init(b):
        P.dma("sp", iostage, x_d[b * T:(b + 1) * T, :].rearrange("(tt p) f -> p tt f", p=128))
        for c in range(NCH):
            bk = bank("mm")
            for tt in range(4):
                P.tr(bk[:, tt * 128:(tt + 1) * 128], iostage[:, tt, c * 128:(c + 1) * 128], ident)
            P.copy("act" if c % 2 else "dve", H[0][:, c, :], bk)

    def stage_A(L, b, do_pre=True, hook=None):
        if do_pre:
            norm(3 * L)
            norm_apply_bf(3 * L)
        cur = b % 2
        prv = 1 - cur
        for g in range(4):
            bk = proj(w_in, g * 128, hb, NCH)
            if b == 0:
                P.memset("pool", ubuf[:, g, 0:16], 0.0)
            else:
                P.copy("pool", ubuf[:, g, 0:16], ubuf[:, g, 512:528])
            P.copy("act", ubuf[:, g, 16:528], bk)
        for j in range(4):
            bk = proj(w_in, 512 + j * 128, hb, NCH)
            P.act(qT[:, j, :], bk, AF.Copy, scale=0.125)
        for j in range(4):
            bk = proj(w_in, 1024 + j * 128, hb, NCH)
            P.copy("dve", kT[:, j, cur * T:(cur + 1) * T], bk)
        for tt in range(4):
            bk = bank("mm")
            for c in range(NCH):
                P.mm(bk, hb[:, c, tt * 128:(tt + 1) * 128], w_in[:, c, 1536:2048], c == 0, c == NCH - 1)
            bv = bk.rearrange("p (j e d) -> p j e d", j=4, e=2)
            tile = cur * 4 + tt
            P.copy("act", vaug[:, tile, :, 0, 0:64], bv[:, :, 0, :])
            P.copy("dve", vaug[:, tile, :, 1, 64:128], bv[:, :, 1, :])
        pm_prev = {}

        def pool_adds(g):
            src = ubuf[:, g, :]
            prev = src
            for k in range(1, g + 2):
                sh = 1 << (k - 1)
                lo = (1 << k) - 1
                dst = ptmp[k % 2]
                P.tt("pool", dst[:, lo:528], prev[:, lo:528], prev[:, lo - sh:528 - sh], ALU.add)
                prev = dst
            pm_prev[g] = prev

        def pool_fin(g):
            src = ubuf[:, g, :]
            prev = pm_prev[g]
            w = 1 << (g + 1)
            yb = ybf[g]
            if b == 0:
                P.tt("pool", prev[:, 16:32], prev[:, 16:32], invcnt[:, g, :], ALU.mult)
                P.act(prev[:, 32:528], prev[:, 32:528], AF.Copy, scale=1.0 / w)
            else:
                P.act(prev[:, 16:528], prev[:, 16:528], AF.Copy, scale=1.0 / w)
            P.tt("pool", yb, prev[:, 16:528], src[:, 16:528], ALU.subtract)

        pool_adds(0)
        steps = []
        for j in range(4):
            rs = [r for r in range(8) if 4 * b - 4 + r >= 0]
            for r in rs:
                steps.append((j, r, r == rs[0], r == rs[-1]))
        LA = 4
        st_pt = {}

        def geom(r):
            qlo = max(0, r - 4)
            qhi = min(3, r)
            half = prv if r < 4 else cur
            return qlo, qhi, qhi - qlo + 1, half

        def front(i):
            j, r, first, last = steps[i]
            qlo, qhi, nq, half = geom(r)
            n = nq * 128
            kc0 = half * T + (r % 4) * 128
            sp2 = bank_pair()
            for e in range(2):
                po = 64 * e
                P.mm(sp2[:, e, 0:n], kT[po:po + 64, j, kc0:kc0 + 128], qT[po:po + 64, j, qlo * 128:(qhi + 1) * 128], True, True)
            d0 = qlo - r + 4
            pt = pTs[i % NPT]
            P.act(pt[:, :, 0:n], sp2[:, :, 0:n], AF.Exp)
            P.tt("dve", pt[:, :, 0:n], pt[:, :, 0:n], Eb[:, 2 * j:2 * j + 2, d0 * 128:(d0 + nq) * 128], ALU.mult)
            st_pt[i] = pt

        def back(i):
            j, r, first, last = steps[i]
            qlo, qhi, nq, half = geom(r)
            n = nq * 128
            vt = half * 4 + (r % 4)
            pt = st_pt[i]
            for e in range(2):
                ob = psb[6 + e]
                P.mm(ob[:, qlo * 128:(qhi + 1) * 128], vaug[:, vt, j, e, :], pt[:, e, 0:n], first, last,
                     skip_group_check=True)
            if last:
                for e in range(2):
                    ob = psb[6 + e]
                    po = 64 * e
                    so = 64 - po
                    rec = recb[e]
                    P.act(rec[so:so + 64, :], ob[so:so + 64, :], AF.Ln)
                    P.act(rec[so:so + 64, :], rec[so:so + 64, :], AF.Exp, scale=-1.0)
                    P.tt("dve", cat[po:po + 64, 4 + j, :], ob[po:po + 64, :], rec[so:so + 64, :], ALU.mult)

        nst = len(steps)
        sched = {}
        for g, (fa, ff) in enumerate(((None, 0.08), (0.10, 0.26), (0.28, 0.48), (0.50, 0.78))):
            sched.setdefault(int(ff * nst), []).append((0, g))
            if fa is not None:
                sched.setdefault(int(fa * nst), []).append((1, g))
        for i in range(nst + LA):
            for (kind_, g) in sorted(sched.get(i, [])):
                if kind_ == 0:
                    pool_fin(g)
                else:
                    pool_adds(g)
            if i >= LA:
                back(i - LA)
            if i < nst:
                front(i)
        for g in range(4):
            bk = bank("mm")
            P.mm(bk, pool_w[:, g, :], ybf[g], True, True)
            P.ts("dve", cat[:, g, :], bk, poolb[:, 4 * L + g:4 * L + g + 1], pools[:, 4 * L + g:4 * L + g + 1], ALU.add, ALU.mult)
        if hook is not None:
            hook()
        for m in range(NCH):
            bk = proj(w_out, m * 128, cat, NCH)
            P.tt("dve", H[0][:, m, :], bk, H[0][:, m, :], ALU.add)

    def stage_B(L, b, do_pre=True, hook=None):
        if do_pre:
            norm(3 * L + 1)
            norm_apply_bf(3 * L + 1)

        def up(f, slot):
            bg = bank("g")
            for c in range(NCH):
                P.mm(bg, w_up[:, f, 0, c, :], hb[:, c, :], c == 0, c == NCH - 1)
            bv = bank("v")
            for c in range(NCH):
                P.mm(bv, w_up[:, f, 1, c, :], hb[:, c, :], c == 0, c == NCH - 1)
            gb = gbuf[f % 2]
            if b == 0:
                P.memset("pool", gb[:, 0:2], 0.0)
            else:
                P.copy("pool", gb[:, 0:2], ghalo[:, f, :])
            P.copy("act", gb[:, 2:514], bg)
            P.copy("pool", ghalo[:, f, :], gb[:, 512:514])
            acc = accb[f % 3]
            P.act(acc, gb[:, 0:512], AF.Identity, bias=convb[:, L, f:f + 1], scale=convw[:, 3 * L + 0, f:f + 1])
            P.stt(acc, gb[:, 1:513], convw[:, 3 * L + 1, f:f + 1], acc, ALU.mult, ALU.add)
            P.stt(acc, gb[:, 2:514], convw[:, 3 * L + 2, f:f + 1], acc, ALU.mult, ALU.add)
            P.act(acc, acc, AF.Gelu)
            P.tt("dve", actb[:, slot, :], acc, bv, ALU.mult)

        def down(fs, slots):
            for m in range(NCH):
                bk = bank("mm")
                for k, (f, sl) in enumerate(zip(fs, slots)):
                    P.mm(bk, w_down[:, f, m * 128:(m + 1) * 128], actb[:, sl, :], k == 0, k == len(fs) - 1)
                P.tt("dve", H[0][:, m, :], bk, H[0][:, m, :], ALU.add)

        slots1 = [NFH + i for i in range(NSH)] + list(range(NSH, NFH))
        for fi in range(NFH):
            up(fi, fi)
        for fi in range(NSH):
            up(NFH + fi, slots1[fi])
        down(list(range(NFH)), list(range(NFH)))
        for fi in range(NSH, NFH):
            up(NFH + fi, slots1[fi])
        if hook is not None:
            hook()
        down([NFH + fi for fi in range(NFH)], slots1)

    def load_p(L, b):
        P.dma("sp", pstage, p_d[L, b * T:(b + 1) * T, :].rearrange("(tt p) k -> p tt k", p=128))

    def stage_C(L, b, do_pre=True):
        if do_pre:
            norm(3 * L + 2)
            norm_apply_bf(3 * L + 2)
        for kc in range(2):
            bk = bank("aux")
            for tt in range(4):
                P.tr(bk[:, tt * 128:(tt + 1) * 128], pstage[:, tt, kc * 128:(kc + 1) * 128], ident)
            P.copy("act", pT[:, kc, :], bk)
        for m in range(NCH):
            bg = proj(w_gate, m * 128, hb, NCH)
            gt = rr(gateb)
            P.act(gt, bg, AF.Sigmoid, bias=bgate[:, L, m:m + 1])
            bp = bank("mm")
            for kc in range(2):
                P.mm(bp, w_ple[:, kc, m * 128:(m + 1) * 128], pT[:, kc, :], kc == 0, kc == 1)
            P.tt("dve", gt, gt, bp, ALU.mult)
            P.tt("pool", H[0][:, m, :], gt, H[0][:, m, :], ALU.add)

    def stage_final(b, hook=None):
        norm(3 * NL)
        for c in range(NCH):
            P.stt(H[0][:, c, :], H[0][:, c, :], gains[:, 3 * NL, c:c + 1], rstd, ALU.mult, ALU.mult)
        if hook is not None:
            hook()
        for tt in range(4):
            for hf in range(2):
                bk = bank("mm")
                for i in range(4):
                    c = hf * 4 + i
                    P.tr(bk[:, i * 128:(i + 1) * 128], H[0][:, c, tt * 128:(tt + 1) * 128], ident)
                P.copy("act" if hf else "dve", iostage[:, tt, hf * 512:(hf + 1) * 512], bk)
        return P.dma("sp", out_d[b * T:(b + 1) * T, :].rearrange("(tt p) f -> p tt f", p=128), iostage)

    finals = []
    for b in range(NB):
        H[0] = hblks[b % 2]
        stage_init(b)
        store_h(b)
    sweeps = []
    for L in range(NL):
        sweeps.append(("A", L))
        sweeps.append(("B", L))
    sweeps.append(("F", NL))
    items = [(si, b) for si in range(len(sweeps)) for b in range(NB)]
    load_h(0, 0)
    for idx, (si, b) in enumerate(items):
        kind, L = sweeps[si]
        slot = idx % 2
        if b == 0:
            P.new_epoch()
            if kind == "A":
                load_w_AC(L - 1 if L > 0 else None, L)
            elif kind == "B":
                load_w_B(L)
            else:
                load_w_AC(NL - 1, None)
        if b == 0 and (kind == "F" or (kind == "A" and L > 0)):
            load_p(L - 1, 0)
        if idx + 1 < len(items):
            load_h(items[idx + 1][1], 1 - slot)
        H[0] = hblks[slot]
        hook = None
        if idx + 1 < len(items):
            nkind, nL = sweeps[items[idx + 1][0]]
            if nkind == "A":
                ng = 3 * (nL - 1) + 2 if nL > 0 else 0
            elif nkind == "B":
                ng = 3 * nL + 1
            else:
                ng = 3 * (NL - 1) + 2

            def hook(ng=ng, nslot=1 - slot, cslot=slot):
                H[0] = hblks[nslot]
                norm(ng)
                norm_apply_bf(ng)
                H[0] = hblks[cslot]
        pre_done = idx > 0
        if kind == "A":
            if L > 0:
                stage_C(L - 1, b, do_pre=not pre_done)
                if b + 1 < NB:
                    load_p(L - 1, b + 1)
                stage_A(L, b, True, hook)
            else:
                stage_A(L, b, not pre_done, hook)
            store_h(b)
        elif kind == "B":
            stage_B(L, b, not pre_done, hook)
            store_h(b)
        else:
            stage_C(NL - 1, b, do_pre=not pre_done)
            if b + 1 < NB:
                load_p(NL - 1, b + 1)
            finals.append(stage_final(b, hook))
    with nc.allow_non_contiguous_dma(reason="small one-time parameter vectors"):
        P.finalize_and_emit(finals)
    return nc


def make_consts():
    c = np.zeros((128, 200), np.float32)
    c[:, 192] = EPS
    c[:, 0:128] = np.eye(128, dtype=np.float32)
    for g in range(4):
        w = 1 << (g + 1)
        for t in range(16):
            c[:, 128 + g * 16 + t] = 1.0 / min(t + 1, w)
    return c


_NC_CACHE = {}


def run_cores(inputs, S, NL, n_cores):
    key = (S, NL)
    if key not in _NC_CACHE:
        _NC_CACHE[key] = build(S, NL)
    nc = _NC_CACHE[key]
    consts = make_consts()
    in_maps = []
    f = lambda a: np.ascontiguousarray(np.asarray(a, dtype=np.float32))
    for i in range(n_cores):
        m = {
            "x": f(inputs["x"][i]),
            "p": f(inputs["p"][:, i]),
            "norm_mix_g": f(inputs["norm_mix_g"]),
            "w_in": f(inputs["w_in"]),
            "pool_w": f(inputs["pool_w"]),
            "pool_b": f(inputs["pool_b"]),
            "pool_scale": f(inputs["pool_scale"]),
            "rel_bias": f(inputs["rel_bias"]).reshape(NL * 8, 257),
            "w_out": f(inputs["w_out"]),
            "norm_ffn_g": f(inputs["norm_ffn_g"]),
            "w_up": f(inputs["w_up"]),
            "conv_w": f(inputs["conv_w"]),
            "conv_b": f(inputs["conv_b"]),
            "w_down": f(inputs["w_down"]),
            "norm_ple_g": f(inputs["norm_ple_g"]),
            "w_ple_gate": f(inputs["w_ple_gate"]),
            "b_ple_gate": f(inputs["b_ple_gate"]),
            "w_ple": f(inputs["w_ple"]),
            "final_g": f(inputs["final_g"]).reshape(1, D),
            "consts": consts,
        }
        in_maps.append(m)
    res = run_bass_kernel_spmd(nc, in_maps, core_ids=list(range(n_cores)))
    return np.stack([np.asarray(r["out"]) for r in res.results], axis=0)


def kernel(**inputs):
    out = run_cores(inputs, SEQ, DEPTH, 8)
    return out.astype(np.float32)
```
